# Optimizing a Trainium2 kernel written in Bass

```python
import jax
import jax.numpy as jnp
from jax import lax
import numpy as np

D_MODEL = 1024
BATCH = 8
SEQ = 8192
DEPTH = 1
DEC_BATCH = 32
DEC_SEQ = 32
PAST_LEN = 4096

CHUNK = 64
N_META = 16
Q_BLOCK = 128
EPS = 1e-6
NEG_INF = -1e30
MLA_HEADS = 8
MLA_Q_LORA = 384
MLA_KV_LORA = 256
MLA_NOPE = 64
MLA_ROPE = 32
MLA_QK_DIM = MLA_NOPE + MLA_ROPE
MLA_V = 64
MLA_WIDTH = MLA_HEADS * MLA_V
MLA_SCALE = MLA_QK_DIM ** -0.5
ROPE_THETA = 10000.0
FOX_HEADS = 8
FOX_HEAD_DIM = 64
FOX_WIDTH = FOX_HEADS * FOX_HEAD_DIM
FOX_SCALE = FOX_HEAD_DIM ** -0.5
FORGET_BIAS_INIT = 3.0
D_MIX = MLA_WIDTH + FOX_WIDTH
IN_SPLIT_SIZES = (MLA_Q_LORA, MLA_KV_LORA, MLA_ROPE, FOX_WIDTH, FOX_WIDTH, FOX_WIDTH, FOX_HEADS)
D_IN = sum(IN_SPLIT_SIZES)
IN_SPLIT_POINTS = tuple(int(v) for v in np.cumsum(IN_SPLIT_SIZES)[:-1])
D_FF = ((8 * D_MODEL + 3 * 256 - 1) // (3 * 256)) * 256

kernel_name = 'hymba_mla_fox_streaming_step'


def rms_norm(x, g):
    xf = x.astype(jnp.float32)
    y = xf * lax.rsqrt(jnp.mean(xf * xf, axis=-1, keepdims=True) + EPS)
    return (y * g.astype(jnp.float32)).astype(x.dtype)


def rope(x, pos):
    half = MLA_ROPE // 2
    inv_freq = ROPE_THETA ** (-jnp.arange(half, dtype=jnp.float32) / half)
    ang = pos.astype(jnp.float32)[:, None] * inv_freq[None, :]
    cos = jnp.cos(ang)[None, :, None, :]
    sin = jnp.sin(ang)[None, :, None, :]
    xf = x.astype(jnp.float32)
    x1, x2 = xf[..., :half], xf[..., half:]
    return jnp.concatenate([x1 * cos - x2 * sin, x2 * cos + x1 * sin], axis=-1).astype(x.dtype)


def attend(q, k, v, scale, valid, bias=None):
    s = jnp.einsum('bqhd,bkhd->bhqk', q, k).astype(jnp.float32) * scale
    if bias is not None:
        s = s + bias
    if valid is not None:
        s = jnp.where(valid, s, NEG_INF)
    p = jax.nn.softmax(s, axis=-1).astype(v.dtype)
    return jnp.einsum('bhqk,bkhd->bqhd', p, v)


def fox_bias(cum_q, cum_k):
    return jnp.transpose(cum_q, (0, 2, 1))[..., :, None] - jnp.transpose(cum_k, (0, 2, 1))[..., None, :]


def project(h, pos, w_in, b_forget, q_norm, w_uq, kv_norm):
    B, L, _ = h.shape
    proj = h @ w_in
    c_q, c_kv, k_r, f_q, f_k, f_v, f_logit = jnp.split(proj, IN_SPLIT_POINTS, axis=-1)
    q = (rms_norm(c_q, q_norm) @ w_uq).reshape(B, L, MLA_HEADS, MLA_QK_DIM)
    q_mla = jnp.concatenate([q[..., :MLA_NOPE], rope(q[..., MLA_NOPE:], pos)], axis=-1)
    latent = rms_norm(c_kv, kv_norm)
    k_rope = rope(k_r[:, :, None, :], pos)[:, :, 0, :]
    heads = lambda a: a.reshape(B, L, FOX_HEADS, FOX_HEAD_DIM)
    log_f = jax.nn.log_sigmoid(f_logit.astype(jnp.float32) + b_forget.astype(jnp.float32))
    return q_mla, latent, k_rope, heads(f_q), heads(f_k), heads(f_v), log_f


def expand_mla_kv(latent, k_rope, w_ukv):
    B, K, _ = latent.shape
    kv = (latent @ w_ukv).reshape(B, K, MLA_HEADS, MLA_NOPE + MLA_V)
    k = jnp.concatenate([kv[..., :MLA_NOPE],
                         jnp.broadcast_to(k_rope[:, :, None, :], (B, K, MLA_HEADS, MLA_ROPE))], axis=-1)
    return k, kv[..., MLA_NOPE:]


def mix_prompt(h, w_in, b_forget, q_norm, w_uq, kv_norm, w_ukv):
    B, L, _ = h.shape
    idx = jnp.arange(L)
    cid = jnp.where(idx < N_META, -1, (idx - N_META) // CHUNK)
    q_mla, latent, k_rope, f_q, f_k, f_v, log_f = project(h, idx, w_in, b_forget, q_norm, w_uq, kv_norm)
    k_mla, v_mla = expand_mla_kv(latent, k_rope, w_ukv)
    cum = jnp.cumsum(log_f, axis=1)

    def block(start):
        take = lambda a: lax.dynamic_slice_in_dim(a, start, Q_BLOCK, axis=1)
        q_idx = start + jnp.arange(Q_BLOCK)
        q_cid = lax.dynamic_slice_in_dim(cid, start, Q_BLOCK, axis=0)
        o_a = attend(take(q_mla), k_mla, v_mla, MLA_SCALE, cid[None, :] <= q_cid[:, None])
        o_b = attend(take(f_q), f_k, f_v, FOX_SCALE, idx[None, :] <= q_idx[:, None],
                     fox_bias(take(cum), cum))
        return jnp.concatenate([o_a.reshape(B, Q_BLOCK, MLA_WIDTH),
                                o_b.reshape(B, Q_BLOCK, FOX_WIDTH)], axis=-1)

    out = lax.map(block, jnp.arange(0, L, Q_BLOCK))
    mixed = jnp.moveaxis(out, 0, 1).reshape(B, L, D_MIX)
    return mixed, (latent, k_rope, f_k, f_v, log_f)


def mix_sample(h, c_lat, c_kr, c_fk, c_fv, c_lf, w_in, b_forget, q_norm, w_uq, kv_norm, w_ukv):
    B, S, _ = h.shape
    past = c_lat.shape[1]
    pos = past + jnp.arange(S)
    q_mla, latent, k_rope, f_q, f_k, f_v, log_f = project(h, pos, w_in, b_forget, q_norm, w_uq, kv_norm)
    k_mla, v_mla = expand_mla_kv(jnp.concatenate([c_lat, latent], axis=1),
                                 jnp.concatenate([c_kr, k_rope], axis=1), w_ukv)
    o_a = attend(q_mla, k_mla, v_mla, MLA_SCALE, None)
    cum = jnp.cumsum(jnp.concatenate([c_lf.astype(jnp.float32), log_f], axis=1), axis=1)
    valid = jnp.arange(past + S)[None, :] <= pos[:, None]
    o_b = attend(f_q, jnp.concatenate([c_fk, f_k], axis=1), jnp.concatenate([c_fv, f_v], axis=1),
                 FOX_SCALE, valid, fox_bias(cum[:, past:], cum))
    mixed = jnp.concatenate([o_a.reshape(B, S, MLA_WIDTH), o_b.reshape(B, S, FOX_WIDTH)], axis=-1)
    return mixed, (latent, k_rope, f_k, f_v, log_f)


def finish_layer(x, mixed, w_out, norm_ffn, w_gate, w_up, w_down):
    x = x + mixed @ w_out
    h = rms_norm(x, norm_ffn)
    return x + (jax.nn.silu(h @ w_gate) * (h @ w_up)) @ w_down


def setup_inputs(seed: int = 0) -> dict:
    key = jax.random.key(seed)
    ks = jax.random.split(key, 22)
    nrm = lambda k, shape, scale=1.0: scale * jax.random.normal(k, shape, jnp.float32)
    gain = lambda k, shape: 1.0 + 0.02 * jax.random.normal(k, shape, jnp.float32)
    return {
        'x_prompt': nrm(ks[0], (BATCH, SEQ, D_MODEL)),
        'x_sample': nrm(ks[1], (DEC_BATCH, DEC_SEQ, D_MODEL)),
        'cache_mla_latent': nrm(ks[2], (DEPTH, DEC_BATCH, PAST_LEN, MLA_KV_LORA)),
        'cache_mla_krope': nrm(ks[3], (DEPTH, DEC_BATCH, PAST_LEN, MLA_ROPE)),
        'cache_fox_k': nrm(ks[4], (DEPTH, DEC_BATCH, PAST_LEN, FOX_HEADS, FOX_HEAD_DIM)),
        'cache_fox_v': nrm(ks[5], (DEPTH, DEC_BATCH, PAST_LEN, FOX_HEADS, FOX_HEAD_DIM)),
        'cache_fox_logf': jax.nn.log_sigmoid(FORGET_BIAS_INIT + nrm(ks[6], (DEPTH, DEC_BATCH, PAST_LEN, FOX_HEADS))),
        'meta_tokens': nrm(ks[7], (N_META, D_MODEL)),
        'norm_mix': gain(ks[8], (DEPTH, D_MODEL)),
        'w_in': nrm(ks[9], (DEPTH, D_MODEL, D_IN), D_MODEL ** -0.5),
        'b_forget': FORGET_BIAS_INIT + 0.5 * nrm(ks[10], (DEPTH, FOX_HEADS)),
        'mla_q_norm': gain(ks[11], (DEPTH, MLA_Q_LORA)),
        'w_mla_uq': nrm(ks[12], (DEPTH, MLA_Q_LORA, MLA_HEADS * MLA_QK_DIM), MLA_Q_LORA ** -0.5),
        'mla_kv_norm': gain(ks[13], (DEPTH, MLA_KV_LORA)),
        'w_mla_ukv': nrm(ks[14], (DEPTH, MLA_KV_LORA, MLA_HEADS * (MLA_NOPE + MLA_V)), MLA_KV_LORA ** -0.5),
        'w_out': nrm(ks[15], (DEPTH, D_MIX, D_MODEL), D_MIX ** -0.5),
        'norm_ffn': gain(ks[16], (DEPTH, D_MODEL)),
        'w_ffn_gate': nrm(ks[17], (DEPTH, D_MODEL, D_FF), D_MODEL ** -0.5),
        'w_ffn_up': nrm(ks[18], (DEPTH, D_MODEL, D_FF), D_MODEL ** -0.5),
        'w_ffn_down': nrm(ks[19], (DEPTH, D_FF, D_MODEL), D_FF ** -0.5),
        'norm_final': gain(ks[20], (D_MODEL,)),
    }


def reference(x_prompt, x_sample, cache_mla_latent, cache_mla_krope, cache_fox_k, cache_fox_v,
              cache_fox_logf, meta_tokens, norm_mix, w_in, b_forget, mla_q_norm, w_mla_uq,
              mla_kv_norm, w_mla_ukv, w_out, norm_ffn, w_ffn_gate, w_ffn_up, w_ffn_down, norm_final):
    B = x_prompt.shape[0]
    L = N_META + x_prompt.shape[1]
    L_pad = -(-L // Q_BLOCK) * Q_BLOCK
    xp = jnp.concatenate([jnp.broadcast_to(meta_tokens[None].astype(x_prompt.dtype), (B, N_META, D_MODEL)),
                          x_prompt], axis=1)
    xp = jnp.pad(xp, ((0, 0), (0, L_pad - L), (0, 0)))
    xs = x_sample
    prompt_rows = []
    sample_rows = []
    for l in range(DEPTH):
        mix_w = (w_in[l], b_forget[l], mla_q_norm[l], w_mla_uq[l], mla_kv_norm[l], w_mla_ukv[l])
        ffn_w = (w_out[l], norm_ffn[l], w_ffn_gate[l], w_ffn_up[l], w_ffn_down[l])
        mixed, rows = mix_prompt(rms_norm(xp, norm_mix[l]), *mix_w)
        xp = finish_layer(xp, mixed, *ffn_w)
        prompt_rows.append(rows)
        mixed, rows = mix_sample(rms_norm(xs, norm_mix[l]), cache_mla_latent[l], cache_mla_krope[l],
                                 cache_fox_k[l], cache_fox_v[l], cache_fox_logf[l], *mix_w)
        xs = finish_layer(xs, mixed, *ffn_w)
        sample_rows.append(rows)
    y_prompt = rms_norm(xp[:, N_META:L], norm_final)
    y_sample = rms_norm(xs, norm_final)
    lat_p, kr_p, fk_p, fv_p, lf_p = [jnp.stack([r[i][:, :L] for r in prompt_rows]) for i in range(5)]
    lat_s, kr_s, fk_s, fv_s, lf_s = [jnp.stack([r[i] for r in sample_rows]) for i in range(5)]
    return (y_prompt, y_sample, lat_p, kr_p, fk_p, fv_p, lf_p, lat_s, kr_s, fk_s, fv_s, lf_s)
```

```python
from contextlib import ExitStack
import numpy as np
import concourse.bass as bass
import concourse.mybir as mybir
from concourse.bass_utils import run_bass_kernel_spmd

F32 = mybir.dt.float32
BF16 = mybir.dt.bfloat16
AF = mybir.ActivationFunctionType
ALU = mybir.AluOpType

D = 1024
DIN = 2216
DFF = 2816
NMETA = 16
EPS = 1e-6
MLA_SCALE = 96 ** -0.5
FOX_SCALE = 0.125
NEG = -30000.0
COMPUTE = ("pe", "act", "dve", "pool")


class Prog:
    uid = 0

    def __init__(self):
        self.ins = []

    def op(self, eng, fn, r=(), w=()):
        self.ins.append((eng, fn, tuple(r), tuple(w), None))

    def dma(self, eng, fn, r=(), w=(), tag=None):
        assert tag is not None
        self.ins.append((eng, fn, tuple(r), tuple(w), tag))

    def analyze(self):
        ins = self.ins
        n = len(ins)
        last_w, readers = {}, {}
        need = [False] * n
        waits = [None] * n
        for i, (eng, fn, R, W, tag) in enumerate(ins):
            d = {}
            for k in R:
                j = last_w.get(k)
                if j is not None:
                    d[j] = True
            for k in W:
                j = last_w.get(k)
                if j is not None and j not in d:
                    d[j] = d.get(j, False)
                for j in readers.get(k, ()):
                    if j != i and j not in d:
                        d[j] = False
            for k in R:
                readers.setdefault(k, []).append(i)
            for k in W:
                last_w[k] = i
                readers[k] = []
            wl = []
            for j, raw in d.items():
                ej, tj = ins[j][0], ins[j][4]
                if tj is None:
                    if ej == "pe" and eng == "pe":
                        continue
                    need[j] = True
                wl.append(j)
            waits[i] = wl
        cnt, val, semkey = {}, [0] * n, [None] * n
        for i in range(n):
            eng, tag = ins[i][0], ins[i][4]
            if tag is not None:
                key = ("d", tag)
                cnt[key] = cnt.get(key, 0) + 16
            elif need[i]:
                key = ("e", eng)
                cnt[key] = cnt.get(key, 0) + 1
            else:
                continue
            val[i] = cnt[key]
            semkey[i] = key
        self.need, self.waits, self.val, self.semkey = need, waits, val, semkey
        return cnt

    def emit(self, nc, final_tags=None):
        cnt = self.analyze()
        ins = self.ins
        with nc.cleanup_on_exit():
          with ExitStack() as st:
            sems = {}
            for idx, k in enumerate(cnt.keys()):
                sems[k] = nc.alloc_semaphore(name="s%d_%d_%s" % (Prog.uid, idx, str(k[1])[:12]))
            Prog.uid += 1
            block = st.enter_context(nc.Block())
            per_eng = {}
            for i, rec in enumerate(ins):
                per_eng.setdefault(rec[0], []).append(i)
            if final_tags is None:
                final_tags = [k[1] for k in cnt if k[0] == "d"]

            def run(name, e):
                waited = {}
                for i in per_eng.get(name, []):
                    fn, tag = ins[i][1], ins[i][4]
                    req = {}
                    for j in self.waits[i]:
                        k, v = self.semkey[j], self.val[j]
                        if v > req.get(k, 0):
                            req[k] = v
                    for k, v in req.items():
                        if waited.get(k, 0) >= v:
                            continue
                        e.wait_ge(sems[k], v)
                        waited[k] = v
                    bi = fn(e)
                    if tag is not None:
                        bi.then_inc(sems[("d", tag)], 16)
                    elif self.need[i]:
                        bi.then_inc(sems[("e", name)], 1)
                if name == "sp":
                    for t in final_tags:
                        e.wait_ge(sems[("d", t)], cnt[("d", t)])

            block.tensor(lambda e: run("pe", e))
            block.scalar(lambda e: run("act", e))
            block.vector(lambda e: run("dve", e))
            block.gpsimd(lambda e: run("pool", e))
            block.sync(lambda e: run("sp", e))
          nc.all_engine_barrier()


class Ring:
    def __init__(self, alloc, name, shape, dt, n):
        self.t = [alloc("%s%d" % (name, i), shape, dt) for i in range(n)]
        self.k = ["%s%d" % (name, i) for i in range(n)]
        self.n = n
        self.i = -1

    def next(self):
        self.i += 1
        s = self.i % self.n
        return self.t[s], self.k[s]


class Em:
    def __init__(self, P):
        self.P = P

    def mm(self, out, lhsT, rhs, start, stop, r, w):
        self.P.op("pe", lambda e: e.matmul(out, lhsT=lhsT, rhs=rhs, start=start, stop=stop), r, w)

    def tr(self, out, in_, ident, r, w):
        self.P.op("pe", lambda e: e.transpose(out=out, in_=in_, identity=ident), r, w)

    def act(self, out, in_, func, r, w, bias=None, scale=None, accum=None):
        kw = {}
        if bias is not None:
            kw["bias"] = bias
        if scale is not None:
            kw["scale"] = scale
        if accum is not None:
            kw["accum_out"] = accum
        self.P.op("act", lambda e: e.activation(out=out, in_=in_, func=func, **kw), r, w)

    def cp(self, eng, out, in_, r, w):
        if eng == "act":
            self.P.op("act", lambda e: e.copy(out=out, in_=in_), r, w)
        else:
            self.P.op(eng, lambda e: e.tensor_copy(out=out, in_=in_), r, w)

    def tt(self, eng, out, in0, in1, op, r, w):
        self.P.op(eng, lambda e: e.tensor_tensor(out=out, in0=in0, in1=in1, op=op), r, w)

    def ts(self, eng, out, in0, s1, s2, op0, op1, r, w):
        if s2 is None:
            self.P.op(eng, lambda e: e.tensor_scalar(out=out, in0=in0, scalar1=s1, scalar2=None, op0=op0), r, w)
        else:
            self.P.op(eng, lambda e: e.tensor_scalar(out=out, in0=in0, scalar1=s1, scalar2=s2, op0=op0, op1=op1), r, w)

    def stt(self, out, in0, scalar, in1, op0, op1, r, w):
        self.P.op("dve", lambda e: e.scalar_tensor_tensor(out=out, in0=in0, scalar=scalar, in1=in1, op0=op0, op1=op1), r, w)

    def memset(self, eng, ap, v, w):
        self.P.op(eng, lambda e: e.memset(ap, v), (), w)

    def dma(self, eng, out, in_, r, w, tag):
        self.P.dma(eng, lambda e: e.dma_start(out=out, in_=in_), r, w, tag)


class Job:
    pass


def build(S, PAST, NS):
    import os
    nc = bass.Bass("TRN2", target_bir_lowering=False)
    NBF = S // 128
    QW_P = 512 if S % 512 == 0 else 128
    NBC = PAST // 128
    LP = NMETA + S

    def din(name, shape):
        return nc.dram_tensor(name, list(shape), F32, kind="ExternalInput").ap()

    def dout(name, shape):
        return nc.dram_tensor(name, list(shape), F32, kind="ExternalOutput").ap()

    def dscr(name, shape, dt=BF16):
        return nc.dram_tensor(name, list(shape), dt, kind="Internal").ap()

    xp = din("xp", [S, D]); meta = din("meta", [NMETA, D]); xs = din("xs", [NS, 32, D])
    c_lat = din("c_lat", [NS, PAST, 256]); c_kr = din("c_kr", [NS, PAST, 32])
    c_fk = din("c_fk", [NS, PAST, 512]); c_fv = din("c_fv", [NS, PAST, 512]); c_lf = din("c_lf", [NS, PAST, 8])
    w_in = din("w_in", [D, DIN]); w_uq = din("w_uq", [384, 768]); w_ukv = din("w_ukv", [256, 1024])
    w_out = din("w_out", [D, D]); w_gate = din("w_gate", [D, DFF]); w_up = din("w_up", [D, DFF]); w_down = din("w_down", [DFF, D])
    g_mix = din("g_mix", [1, D]); b_forget = din("b_forget", [1, 8]); g_q = din("g_q", [1, 384]); g_kv = din("g_kv", [1, 256])
    g_ffn = din("g_ffn", [1, D]); g_fin = din("g_fin", [1, D])
    c_ident = din("c_ident", [128, 128])
    c_tri = din("c_tri", [3, 128, 128])
    c_ones = din("c_ones", [3, 128, 128])
    c_mask = din("c_mask", [2, 128, 128])
    NPB = NBF + 1
    c_ropep = din("c_ropep", [128, NPB, 64])
    c_ropes = din("c_ropes", [128, 1, 64])

    y_p = dout("y_p", [S, D]); lat_p = dout("lat_p", [LP, 256]); kr_p = dout("kr_p", [LP, 32])
    fk_p = dout("fk_p", [LP, 512]); fv_p = dout("fv_p", [LP, 512]); lf_p = dout("lf_p", [LP, 8])
    y_s = dout("y_s", [NS, 32, D]); lat_s = dout("lat_s", [NS, 32, 256]); kr_s = dout("kr_s", [NS, 32, 32])
    fk_s = dout("fk_s", [NS, 32, 512]); fv_s = dout("fv_s", [NS, 32, 512]); lf_s = dout("lf_s", [NS, 32, 8])

    jobs = []
    jp = Job()
    jp.name = "p"; jp.nbc = 0; jp.cache = None
    jp.blocks = [dict(src=meta, r0=0, nv=NMETA, q=False, tri=1, rope=(c_ropep, 0), orow=0)]
    for b in range(NBF):
        jp.blocks.append(dict(src=xp, r0=b * 128, nv=128, q=True, tri=0, rope=(c_ropep, b + 1), orow=NMETA + b * 128))
    jp.outs = (lat_p, kr_p, fk_p, fv_p, lf_p); jp.y = y_p; jp.xq = xp
    jp.qw = QW_P; jp.lq = S; jp.nbk = NBF + 1; jp.nvq = 128
    jp.mla_mask = 1
    jobs.append(jp)
    for s in range(NS):
        j = Job()
        j.name = "s%d" % s; j.nbc = NBC
        j.cache = (c_lat[s], c_kr[s], c_fk[s], c_fv[s], c_lf[s])
        j.blocks = [dict(src=xs[s], r0=0, nv=32, q=True, tri=2, rope=(c_ropes, 0), orow=0)]
        j.outs = (lat_s[s], kr_s[s], fk_s[s], fv_s[s], lf_s[s]); j.y = y_s[s]; j.xq = xs[s]
        j.qw = 128; j.lq = 128; j.nbk = NBC + 1; j.nvq = 32
        j.mla_mask = None
        jobs.append(j)
    for j in jobs:
        lk = j.nbk * 128
        n = j.name
        j.KTn = dscr("KTn_" + n, [8, 64, lk]); j.KTr = dscr("KTr_" + n, [32, lk]); j.KTf = (nc.dram_tensor("KTf_" + n, [8, 64, lk], BF16, kind="ExternalOutput").ap() if (os.environ.get("K_DBG") and n == "p") else dscr("KTf_" + n, [8, 64, lk]))
        j.Vm = dscr("Vm_" + n, [8, 128, j.nbk, 64]); j.Vf = dscr("Vf_" + n, [8, 128, j.nbk, 64])
        j.QTm = dscr("QTm_" + n, [8, 96, j.lq]); j.QTf = (nc.dram_tensor("QTf_" + n, [8, 65, j.lq], BF16, kind="ExternalOutput").ap() if (os.environ.get("K_DBG") and n == "p") else dscr("QTf_" + n, [8, 65, j.lq]))
        j.MIX = (nc.dram_tensor("MIX_" + n, [D, j.lq], BF16, kind="ExternalOutput").ap() if (os.environ.get("K_DBG") and n == "p") else dscr("MIX_" + n, [D, j.lq])); j.NCUM = dscr("NCUM_" + n, [128, j.nbk, 8], F32)
    LKMAX = max(j.nbk for j in jobs) * 128
    LQMAX = max(j.lq for j in jobs)
    NBKMAX = LKMAX // 128

    import os
    PH = os.environ.get('K_PHASES', 'ABC')
    with ExitStack() as st:
        sb = lambda name, shape, dt: st.enter_context(nc.sbuf_tensor("A_" + name, shape, dt))
        psb = lambda name, shape, dt: st.enter_context(nc.psum_tensor("A_" + name, shape, dt))
        P = Prog(); E = Em(P)
        STQ = os.environ.get('K_STQ', 'sp')
        win = sb("win", [128, 8, DIN], BF16)
        wuq = sb("wuq", [128, 3, 768], BF16)
        wuk = sb("wuk", [128, 2, 8, 64], BF16)
        wuv = sb("wuv", [128, 2, 8, 64], BF16)
        identf = sb("identf", [128, 128], F32); identb = sb("identb", [128, 128], BF16)
        tri = sb("tri", [128, 3, 128], F32); ones3 = sb("ones3", [128, 3, 128], F32)
        gmix = sb("gmix", [128, D], F32); gq = sb("gq", [128, 384], F32); gkv = sb("gkv", [128, 256], F32)
        bfg = sb("bfg", [128, 8], F32)
        ropep = sb("ropep", [128, NPB, 64], F32); ropes = sb("ropes", [128, 1, 64], F32)
        carry = sb("carry", [128, 8], F32)
        ncum = sb("ncum", [128, NBKMAX, 8], F32)
        CQ, CKV, CKR, CLG, CFQ, CFK, CFV = 0, 384, 640, 672, 680, 1192, 1704
        src_off = dict(cq=0, ckv=384, kr=640, fq=672, fk=1184, fv=1696, lg=2208)
        stg_r = Ring(sb, "xt", [128, D], F32, 2)
        _ce = [0]

        def wload(dst_ap, src_ap, wkey, ncols):
            stg, sk = stg_r.next()
            E.dma("sp", stg[:, 0:ncols], src_ap, [], [sk], sk)
            eng = "act" if _ce[0] % 2 == 0 else "dve"
            _ce[0] += 1
            E.cp(eng, dst_ap, stg[:, 0:ncols], [sk], [wkey])
        for c in range(8):
            for (dst, so, wd_) in ((CQ, 0, 384), (CKV, 384, 256), (CKR, 640, 32), (CLG, 2208, 8), (CFQ, 672, 512), (CFK, 1184, 512), (CFV, 1696, 512)):
                wload(win[:, c, dst:dst + wd_], w_in[c * 128:(c + 1) * 128, so:so + wd_], "win", wd_)
        for c in range(3):
            wload(wuq[:, c, :], w_uq[c * 128:(c + 1) * 128, :], "wuq", 768)
        for c in range(2):
            stg, sk = stg_r.next()
            E.dma("sp", stg[:, :], w_ukv[c * 128:(c + 1) * 128, :], [], [sk], sk)
            sv = stg[:, :].rearrange("p (h t d) -> p h t d", h=8, t=2)
            E.cp("act", wuk[:, c], sv[:, :, 0, :], [sk], ["wuk"])
            E.cp("dve", wuv[:, c], sv[:, :, 1, :], [sk], ["wuv"])
        E.dma("sp", identf[:], c_ident[:, :], [], ["identf"], "w_id")
        E.cp("dve", identb[:], identf[:], ["identf"], ["identb"])
        E.dma("sp", tri[:], c_tri.rearrange("a p n -> p a n"), [], ["tri"], "w_tri")
        E.dma("sp", ones3[:], c_ones.rearrange("a p n -> p a n"), [], ["ones3"], "w_ones")
        E.dma("sp", gmix[:], g_mix.partition_broadcast(128), [], ["gmix"], "w_gmix")
        E.dma("sp", gq[:], g_q.partition_broadcast(128), [], ["gq"], "w_gq")
        E.dma("sp", gkv[:], g_kv.partition_broadcast(128), [], ["gkv"], "w_gkv")
        E.dma("sp", bfg[:], b_forget.partition_broadcast(128), [], ["bfg"], "w_bfg")
        E.dma("sp", ropep[:], c_ropep[:, :, :], [], ["ropep"], "w_ropep")
        E.dma("sp", ropes[:], c_ropes[:, :, :], [], ["ropes"], "w_ropes")
        rope_sb = {id(c_ropep): (ropep, "ropep"), id(c_ropes): (ropes, "ropes")}

        xt_r = stg_r
        xpart = sb("xpart", [128, D], F32)
        E.memset("dve", xpart[:], 0.0, ["xpart"])
        junk_r = Ring(sb, "junk", [128, D], BF16, 2)
        st_r = Ring(sb, "stat", [128, 8], F32, 4)
        for _i in range(4):
            E.memset("dve", st_r.t[_i][:], 0.0, [st_r.k[_i] + "z"])
        hb_r = Ring(sb, "hb", [128, D], BF16, 2)
        hT_r = Ring(sb, "hT", [128, 8, 128], BF16, 2)
        lat32_r = Ring(sb, "lat32", [128, 256], F32, 2)
        kr32_r = Ring(sb, "kr32", [128, 32], F32, 2)
        krt_r = Ring(sb, "krt", [128, 64], F32, 2)
        lf32_r = Ring(sb, "lf32", [128, 8], F32, 3)
        lgt_r = Ring(sb, "lgt", [128, 8], F32, 2)
        fk32_r = Ring(sb, "fk32", [128, 512], F32, 2)
        fv32_r = Ring(sb, "fv32", [128, 512], F32, 2)
        lkb_r = Ring(sb, "lkb", [128, 352], BF16, 2)
        for _i in range(2):
            E.memset("dve", lkb_r.t[_i][:, 256:320], 0.0, [lkb_r.k[_i] + "p"])
        fkb_r = Ring(sb, "fkb", [128, 512], BF16, 2)
        vfb_r = Ring(sb, "vfb", [128, 512], BF16, 2)
        vmb_r = Ring(sb, "vmb", [128, 512], BF16, 2)
        cqn_r = Ring(sb, "cqn", [128, 384], BF16, 2)
        fqb_r = Ring(sb, "fqb", [128, 512], BF16, 2)
        cqT_r = Ring(sb, "cqT", [128, 3, 128], BF16, 2)
        qb_r = Ring(sb, "qb", [128, 768], BF16, 2)
        qrt_r = Ring(sb, "qrt", [128, 8, 64], F32, 2)
        latT_r = Ring(sb, "latT", [128, 3, 128], BF16, 2)
        ktn_r = Ring(sb, "ktn", [64, 8, 128], BF16, 2)
        ktf_r = Ring(sb, "ktf", [64, 8, 128], BF16, 2)
        qtm_r = Ring(sb, "qtm", [96, 8, 128], BF16, 2)
        qtf_r = Ring(sb, "qtf", [64, 8, 128], BF16, 2)
        cum32_r = Ring(sb, "cum32", [128, 8], F32, 2)
        cum8_r = Ring(sb, "cum8", [128, 8], BF16, 2)
        cumT_r = Ring(sb, "cumT", [8, 128], BF16, 2)
        pT_r = Ring(psb, "pT", [128, 1024], BF16, 2)
        pG_r = Ring(psb, "pG", [128, 512], F32, 5)

        def rstd_from(ss_ap, n, stt, stk):
            E.cp("act", stt[:, 6:7], stt[:, 7:8], [stk + "a", stk + "z"], [stk + "a2"])
            E.act(stt[:, 1:2], ss_ap, AF.Ln, [stk + "a", stk + "a2"], [stk + "b"], bias=EPS, scale=1.0 / n)
            E.act(stt[:, 2:3], stt[:, 1:2], AF.Exp, [stk + "b"], [stk + "c"], scale=-0.5)
            return stt[:, 2:3], stk + "c"

        def rope32(out32, ok, src, sk, tab, tk, tmp, tmk):
            E.tt("dve", tmp[:, 0:32], src, tab[:, 0:32], ALU.mult, [sk, tk], [tmk + "a"])
            E.tt("dve", tmp[:, 32:48], src[:, 16:32], tab[:, 32:48], ALU.mult, [sk, tk], [tmk + "b"])
            E.tt("dve", tmp[:, 48:64], src[:, 0:16], tab[:, 48:64], ALU.mult, [sk, tk], [tmk + "c"])
            E.tt("dve", out32, tmp[:, 0:32], tmp[:, 32:64], ALU.add, [tmk + "a", tmk + "b", tmk + "c"], [ok])

        for job in jobs:
            lat_o, kr_o, fk_o, fv_o, lf_o = job.outs
            jn = job.name
            E.memset("dve", carry[:], 0.0, ["carry"])
            kb = 0
            qblk = 0
            allblocks = [("c", i) for i in range(min(job.nbc, int(os.environ.get("K_NBC_LIMIT", "9999"))))] + [("n", b) for b in job.blocks]
            for kind, bi in allblocks:
                lkb, lkbk = lkb_r.next(); fkb, fkbk = fkb_r.next(); vfb, vfbk = vfb_r.next()
                lf32, lf32k = lf32_r.next()
                hasq = False
                if kind == "c":
                    cl, ckr_, cfk_, cfv_, clf_ = job.cache
                    r0 = bi * 128
                    lat32, lat32k = lat32_r.next(); kr32, kr32k = kr32_r.next()
                    fk32, fk32k = fk32_r.next(); fv32, fv32k = fv32_r.next()
                    E.dma("sp", lat32[:], cl[r0:r0 + 128, :], [], [lat32k], lat32k)
                    E.dma("sp", kr32[:], ckr_[r0:r0 + 128, :], [], [kr32k], kr32k)
                    E.dma("sp", fk32[:], cfk_[r0:r0 + 128, :], [], [fk32k], fk32k)
                    E.dma("sp", fv32[:], cfv_[r0:r0 + 128, :], [], [fv32k], fv32k)
                    E.cp("act", lkb[:, 0:256], lat32[:], [lat32k], [lkbk + "l"])
                    E.cp("dve", lkb[:, 320:352], kr32[:], [kr32k], [lkbk + "r"])
                    E.cp("act", fkb[:], fk32[:], [fk32k], [fkbk])
                    E.cp("dve", vfb[:], fv32[:], [fv32k], [vfbk])
                    E.dma("sp", lf32[:], clf_[r0:r0 + 128, :], [], [lf32k], lf32k)
                    tri_i = 0
                else:
                    blk = bi
                    nv = blk["nv"]; hasq = blk["q"]; tri_i = blk["tri"]
                    rtab_t, rtab_k = rope_sb[id(blk["rope"][0])]
                    rtab = rtab_t[:, blk["rope"][1], :]
                    orow = blk["orow"]
                    if nv == 128:
                        xt, xtk = xt_r.next()
                        E.dma("sp", xt[:], blk["src"][blk["r0"]:blk["r0"] + 128, :], [], [xtk], xtk)
                    else:
                        xt, xtk = xpart, "xpart"
                        E.dma("sp", xt[0:nv, :], blk["src"][blk["r0"]:blk["r0"] + nv, :], [], [xtk], "xpartd")
                    junk, jk = junk_r.next(); stt, stk = st_r.next()
                    E.act(junk[:], xt[:], AF.Square, [xtk, stk + "z"], [jk, stk + "a"], accum=stt[:, 0:1])
                    rs, rsk = rstd_from(stt[:, 0:1], D, stt, stk)
                    hb, hbk = hb_r.next()
                    E.stt(hb[:], xt[:], rs, gmix[:], ALU.mult, ALU.mult, [xtk, rsk, "gmix"], [hbk])
                    pT, pTk = pT_r.next(); hT, hTk = hT_r.next()
                    for c in range(8):
                        E.tr(pT[:, c * 128:(c + 1) * 128], hb[:, c * 128:(c + 1) * 128], identb[:], [hbk, "identb"], [pTk])
                    E.cp("act", hT[:].rearrange("p c t -> p (c t)"), pT[:, :], [pTk], [hTk])
                    pg, pgk = pG_r.next()
                    for c in range(8):
                        E.mm(pg[:, 0:296], hT[:, c, :], win[:, c, CKV:CKV + 296], c == 0, c == 7, [hTk, "win"], [pgk])
                    stt2, stk2 = st_r.next(); junk2, jk2 = junk_r.next()
                    E.act(junk2[:, 0:256], pg[:, 0:256], AF.Square, [pgk, stk2 + "z"], [jk2, stk2 + "a"], accum=stt2[:, 0:1])
                    rs2, rs2k = rstd_from(stt2[:, 0:1], 256, stt2, stk2)
                    lat32, lat32k = lat32_r.next()
                    E.stt(lat32[:], pg[:, 0:256], rs2, gkv[:], ALU.mult, ALU.mult, [pgk, rs2k, "gkv"], [lat32k])
                    E.dma(STQ, lat_o[orow:orow + nv, :], lat32[0:nv, :], [lat32k], [], lat32k + "o")
                    E.cp("act", lkb[:, 0:256], lat32[:], [lat32k], [lkbk + "l"])
                    kr32, kr32k = kr32_r.next(); krt, krtk = krt_r.next()
                    rope32(kr32[:], kr32k, pg[:, 256:288], pgk, rtab, rtab_k, krt, krtk)
                    E.dma(STQ, kr_o[orow:orow + nv, :], kr32[0:nv, :], [kr32k], [], kr32k + "o")
                    E.cp("act", lkb[:, 320:352], kr32[:], [kr32k], [lkbk + "r"])
                    lgt, lgtk = lgt_r.next()
                    E.tt("dve", lgt[:], pg[:, 288:296], bfg[:], ALU.add, [pgk, "bfg"], [lgtk])
                    E.act(lgt[:], lgt[:], AF.Exp, [lgtk], [lgtk], scale=-1.0)
                    E.act(lgt[:], lgt[:], AF.Ln, [lgtk], [lgtk], bias=1.0)
                    E.ts("dve", lf32[:], lgt[:], -1.0, None, ALU.mult, None, [lgtk], [lf32k])
                    E.dma(STQ, lf_o[orow:orow + nv, :], lf32[0:nv, :], [lf32k], [], lf32k + "o")
                    pg, pgk = pG_r.next()
                    for c in range(8):
                        E.mm(pg[:, :], hT[:, c, :], win[:, c, CFK:CFK + 512], c == 0, c == 7, [hTk, "win"], [pgk])
                    fk32, fk32k = fk32_r.next()
                    E.cp("act", fk32[:], pg[:, :], [pgk], [fk32k])
                    E.dma(STQ, fk_o[orow:orow + nv, :], fk32[0:nv, :], [fk32k], [], fk32k + "o")
                    E.cp("act", fkb[:], fk32[:], [fk32k], [fkbk])
                    pg, pgk = pG_r.next()
                    for c in range(8):
                        E.mm(pg[:, :], hT[:, c, :], win[:, c, CFV:CFV + 512], c == 0, c == 7, [hTk, "win"], [pgk])
                    fv32, fv32k = fv32_r.next()
                    E.cp("act", fv32[:], pg[:, :], [pgk], [fv32k])
                    E.dma(STQ, fv_o[orow:orow + nv, :], fv32[0:nv, :], [fv32k], [], fv32k + "o")
                    E.cp("act", vfb[:], fv32[:], [fv32k], [vfbk])
                    if hasq:
                        pg, pgk = pG_r.next()
                        for c in range(8):
                            E.mm(pg[:, 0:384], hT[:, c, :], win[:, c, CQ:CQ + 384], c == 0, c == 7, [hTk, "win"], [pgk])
                        stt3, stk3 = st_r.next(); junk3, jk3 = junk_r.next()
                        E.act(junk3[:, 0:384], pg[:, 0:384], AF.Square, [pgk, stk3 + "z"], [jk3, stk3 + "a"], accum=stt3[:, 0:1])
                        rs3, rs3k = rstd_from(stt3[:, 0:1], 384, stt3, stk3)
                        cqn, cqnk = cqn_r.next()
                        E.stt(cqn[:], pg[:, 0:384], rs3, gq[:], ALU.mult, ALU.mult, [pgk, rs3k, "gq"], [cqnk])
                        pg, pgk = pG_r.next()
                        for c in range(8):
                            E.mm(pg[:, :], hT[:, c, :], win[:, c, CFQ:CFQ + 512], c == 0, c == 7, [hTk, "win"], [pgk])
                        fqb, fqbk = fqb_r.next()
                        E.cp("act", fqb[:], pg[:, :], [pgk], [fqbk])
                t0 = kb * 128
                pg, pgk = pG_r.next()
                E.mm(pg[:, 0:8], tri[:, tri_i, :], lf32[:], True, True, ["tri", lf32k], [pgk])
                E.mm(pg[:, 8:16], ones3[:, tri_i, :], lf32[:], True, True, ["ones3", lf32k], [pgk])
                cum32, cum32k = cum32_r.next()
                E.tt("dve", cum32[:], pg[:, 0:8], carry[:], ALU.add, [pgk, "carry"], [cum32k])
                E.tt("dve", carry[:], pg[:, 8:16], carry[:], ALU.add, [pgk, "carry"], ["carry"])
                E.ts("dve", ncum[:, kb, :], cum32[:], -1.0, None, ALU.mult, None, [cum32k], ["ncum"])
                pT, pTk = pT_r.next(); latT, latTk = latT_r.next()
                E.tr(pT[:, 0:128], lkb[:, 0:128], identb[:], [lkbk + "l", "identb"], [pTk])
                E.tr(pT[:, 128:256], lkb[:, 128:256], identb[:], [lkbk + "l", "identb"], [pTk])
                E.tr(pT[:, 256:384], lkb[:, 224:352], identb[:], [lkbk + "l", lkbk + "r", lkbk + "p", "identb"], [pTk])
                E.cp("dve", latT[:, 0:2, :].rearrange("p c t -> p (c t)"), pT[:, 0:256], [pTk], [latTk + "l"])
                E.cp("dve", latT[96:128, 2, :], pT[96:128, 256:384], [pTk], [latTk + "r"])
                E.dma(STQ, job.KTr[:, t0:t0 + 128], latT[96:128, 2, :], [latTk + "r"], ["dram_KTr_" + jn], latTk + "ro")
                ktn, ktnk = ktn_r.next()
                for half in range(2):
                    pg, pgk = pG_r.next()
                    pgv = pg[0:64, :].rearrange("p (h t) -> p h t", h=4)
                    for hh in range(4):
                        h = half * 4 + hh
                        for c in range(2):
                            E.mm(pgv[:, hh, :], wuk[:, c, h, :], latT[:, c, :], c == 0, c == 1, ["wuk", latTk + "l"], [pgk])
                    E.cp("act" if half == 0 else "dve", ktn[:, half * 4:(half + 1) * 4, :], pgv, [pgk], [ktnk + str(half)])
                E.dma(STQ, job.KTn[:, :, t0:t0 + 128].rearrange("h p t -> p h t"), ktn[:], [ktnk + "0", ktnk + "1"], ["dram_KTn_" + jn], ktnk + "o")
                pg, pgk = pG_r.next()
                for c in range(2):
                    E.mm(pg[:, :], latT[:, c, :], wuv[:, c].rearrange("p h d -> p (h d)"), c == 0, c == 1, [latTk + "l", "wuv"], [pgk])
                vmb, vmbk = vmb_r.next()
                E.cp("act", vmb[:], pg[:, :], [pgk], [vmbk])
                E.dma(STQ, job.Vm[:, :, kb, :].rearrange("h p d -> p h d"), vmb[:].rearrange("p (h d) -> p h d", h=8), [vmbk], ["dram_Vm_" + jn], vmbk + "o")
                E.dma(STQ, job.Vf[:, :, kb, :].rearrange("h p d -> p h d"), vfb[:].rearrange("p (h d) -> p h d", h=8), [vfbk], ["dram_Vf_" + jn], vfbk + "o")
                pT, pTk = pT_r.next(); ktf, ktfk = ktf_r.next()
                for h in range(8):
                    E.tr(pT[0:64, h * 128:(h + 1) * 128], fkb[:, h * 64:(h + 1) * 64], identb[:], [fkbk, "identb"], [pTk])
                E.cp("dve", ktf[:].rearrange("p h t -> p (h t)"), pT[0:64, :], [pTk], [ktfk])
                E.dma(STQ, job.KTf[:, :, t0:t0 + 128].rearrange("h p t -> p h t"), ktf[:], [ktfk], ["dram_KTf_" + jn], ktfk + "o")
                if hasq:
                    q0 = qblk * 128
                    pT, pTk = pT_r.next(); cqT, cqTk = cqT_r.next()
                    for c in range(3):
                        E.tr(pT[:, c * 128:(c + 1) * 128], cqn[:, c * 128:(c + 1) * 128], identb[:], [cqnk, "identb"], [pTk])
                    E.cp("act", cqT[:].rearrange("p c t -> p (c t)"), pT[:, 0:384], [pTk], [cqTk])
                    qb, qbk = qb_r.next(); qrt, qrtk = qrt_r.next()
                    for half in range(2):
                        pg, pgk = pG_r.next()
                        for c in range(3):
                            E.mm(pg[:, 0:384], cqT[:, c, :], wuq[:, c, half * 384:(half + 1) * 384], c == 0, c == 2, [cqTk, "wuq"], [pgk])
                        pv = pg[:, 0:384].rearrange("p (h d) -> p h d", h=4)
                        qv = qb[:, half * 384:(half + 1) * 384].rearrange("p (h d) -> p h d", h=4)
                        E.cp("dve", qv[:, :, 0:64], pv[:, :, 0:64], [pgk], [qbk + "n%d" % half])
                        tb = rtab.unsqueeze(1)
                        tmp = qrt[:, half * 4:(half + 1) * 4, :]
                        tk = qrtk + str(half)
                        E.tt("dve", tmp[:, :, 0:32], pv[:, :, 64:96], tb[:, :, 0:32].broadcast_to([128, 4, 32]), ALU.mult, [pgk, rtab_k], [tk + "a"])
                        E.tt("dve", tmp[:, :, 32:48], pv[:, :, 80:96], tb[:, :, 32:48].broadcast_to([128, 4, 16]), ALU.mult, [pgk, rtab_k], [tk + "b"])
                        E.tt("dve", tmp[:, :, 48:64], pv[:, :, 64:80], tb[:, :, 48:64].broadcast_to([128, 4, 16]), ALU.mult, [pgk, rtab_k], [tk + "c"])
                        E.tt("dve", qv[:, :, 64:96], tmp[:, :, 0:32], tmp[:, :, 32:64], ALU.add, [tk + "a", tk + "b", tk + "c"], [qbk + "r%d" % half])
                    qkeys = [qbk + "n0", qbk + "n1", qbk + "r0", qbk + "r1"]
                    pT, pTk = pT_r.next(); qtm, qtmk = qtm_r.next()
                    for h in range(8):
                        E.tr(pT[0:96, h * 128:(h + 1) * 128], qb[:, h * 96:(h + 1) * 96], identb[:], qkeys + ["identb"], [pTk])
                    E.cp("dve", qtm[:].rearrange("p h t -> p (h t)"), pT[0:96, :], [pTk], [qtmk])
                    E.dma(STQ, job.QTm[:, :, q0:q0 + 128].rearrange("h p t -> p h t"), qtm[:], [qtmk], ["dram_QTm_" + jn], qtmk + "o")
                    pT, pTk = pT_r.next(); qtf, qtfk = qtf_r.next()
                    for h in range(8):
                        E.tr(pT[0:64, h * 128:(h + 1) * 128], fqb[:, h * 64:(h + 1) * 64], identb[:], [fqbk, "identb"], [pTk])
                    E.cp("act", qtf[:].rearrange("p h t -> p (h t)"), pT[0:64, :], [pTk], [qtfk])
                    E.dma(STQ, job.QTf[:, 0:64, q0:q0 + 128].rearrange("h p t -> p h t"), qtf[:], [qtfk], ["dram_QTf_" + jn], qtfk + "o")
                    cum8, cum8k = cum8_r.next(); cumT, cumTk = cumT_r.next()
                    E.ts("dve", cum8[:], cum32[:], 8.0, None, ALU.mult, None, [cum32k], [cum8k])
                    pT, pTk = pT_r.next()
                    E.tr(pT[0:8, 0:128], cum8[:, 0:8], identb[:], [cum8k, "identb"], [pTk])
                    E.cp("dve", cumT[:], pT[0:8, 0:128], [pTk], [cumTk])
                    E.dma(STQ, job.QTf[:, 64, q0:q0 + 128], cumT[:], [cumTk], ["dram_QTf_" + jn], cumTk + "o")
                    qblk += 1
                kb += 1
            E.dma(STQ, job.NCUM[:, :, :], ncum[:, 0:job.nbk, :], ["ncum"], [], "ncum_o")
        if 'A' in PH:
            _tr = int(os.environ.get('K_ATRUNC', '0'))
            if _tr:
                print('PHASE A n_ins', len(P.ins)); P.ins = P.ins[:_tr]
            P.emit(nc)

    with ExitStack() as st:
        sb = lambda name, shape, dt: st.enter_context(nc.sbuf_tensor("B_" + name, shape, dt))
        psb = lambda name, shape, dt: st.enter_context(nc.psum_tensor("B_" + name, shape, dt))
        P = Prog(); E = Em(P)
        identf = sb("identf", [128, 128], F32); identb = sb("identb", [128, 128], BF16)
        maskf = sb("maskf", [128, 2, 128], F32); maskb = sb("maskb", [128, 2, 128], BF16)
        onesr = sb("onesr", [65, 64], F32)
        E.dma("sp", identf[:], c_ident[:, :], [], ["identf"], "w_id")
        E.cp("dve", identb[:], identf[:], ["identf"], ["identb"])
        E.dma("sp", maskf[:], c_mask.rearrange("a p n -> p a n"), [], ["maskf"], "w_mask")
        E.cp("dve", maskb[:], maskf[:], ["maskf"], ["maskb"])
        E.memset("dve", onesr[:], 1.0, ["onesr"])
        ktm_r = Ring(sb, "ktm", [96, LKMAX], BF16, 2)
        ktf_r = Ring(sb, "ktf", [96, LKMAX], BF16, 2)
        v_r = Ring(sb, "vv", [128, NBKMAX, 128], BF16, 2)
        qm_r = Ring(sb, "qm", [96, LQMAX], BF16, 2)
        qf_r = Ring(sb, "qf", [96, LQMAX], BF16, 2)
        pt_r = Ring(sb, "pt", [128, 512], BF16, 3)
        osb_r = Ring(sb, "osb", [65, 512], F32, 2)
        rec_r = Ring(sb, "rec", [65, 512], F32, 2)
        mixo_r = Ring(sb, "mixo", [64, 512], BF16, 2)
        pS_r = Ring(psb, "pS", [128, 512], F32, 3)
        pO_r = Ring(psb, "pO", [128, 512], F32, 2)
        pB_r = Ring(psb, "pB", [128, 512], F32, 2)
        for i in range(2):
            E.memset("dve", ktf_r.t[i][64:96, :], 0.0, [ktf_r.k[i] + "1"])
            E.memset("dve", ktf_r.t[i][64:65, :], 1.0, [ktf_r.k[i] + "1"])
            E.memset("dve", qf_r.t[i][64:96, :], 0.0, [qf_r.k[i] + "z", qf_r.k[i]])
            E.memset("dve", v_r.t[i][:, :, 65:128], 0.0, [v_r.k[i] + "1"])

        ncum_r = Ring(sb, "ncumr", [128, NBKMAX, 8], F32, 2)

        def load_head(job, hd):
            nbk = job.nbk; lk = nbk * 128; lq = job.lq
            fox = hd >= 8
            h = hd % 8
            vt, vk = v_r.next()
            vkeys = []
            if fox:
                kt, ktk = ktf_r.next(); qt, qtk = qf_r.next(); KD = 96
                E.dma("sp", kt[0:64, 0:lk], job.KTf[h], [], [ktk], ktk)
                E.dma("sp", qt[0:65, 0:lq], job.QTf[h], [], [qtk], qtk)
                vsrc = job.Vf
                kr_keys = [ktk, ktk + "1", qtk + "z"]
                scale = FOX_SCALE
            else:
                kt, ktk = ktm_r.next(); qt, qtk = qm_r.next(); KD = 96
                E.dma("sp", kt[0:64, 0:lk], job.KTn[h], [], [ktk], ktk)
                E.dma("sp", kt[64:96, 0:lk], job.KTr[:, :], [], [ktk + "r"], ktk + "r")
                E.dma("sp", qt[0:96, 0:lq], job.QTm[h], [], [qtk], qtk)
                vsrc = job.Vm
                kr_keys = [ktk, ktk + "r"]
                scale = MLA_SCALE
            for b0_ in range(0, nbk, 16):
                b1_ = min(nbk, b0_ + 16)
                E.dma("sp", vt[:, b0_:b1_, 0:64], vsrc[h, :, b0_:b1_, :], [], [vk + "c%d" % b0_], vk + "_%d" % b0_)
                vkeys.append(vk + "c%d" % b0_)
            return dict(fox=fox, h=h, vt=vt, vk=vk, vkeys=vkeys, kt=kt, ktk=ktk, qt=qt, qtk=qtk, KD=KD, kr_keys=kr_keys, scale=scale)

        RS = dscr("RS_scr", [4, 512], F32)
        rs_cnt = [0]
        items = [(job, hd) for job in jobs for hd in range(16)]
        nxt_loaded = load_head(*items[0])
        for it_i, (job, hd) in enumerate(items):
            nbk = job.nbk; lk = nbk * 128; lq = job.lq; QW = job.qw
            if hd == 0:
                ncum, ncumk = ncum_r.next()
                E.dma("sp", ncum[:, 0:nbk, :], job.NCUM[:, :, :], [], [ncumk], ncumk)
                for i in range(2):
                    vt, vk = v_r.t[i], v_r.k[i]
                    E.memset("dve", vt[:, :, 64:65], 1.0, [vk + "1"])
                    pb_ = job.nbc if job.name != "p" else 0
                    nvb = job.blocks[0]["nv"]
                    E.memset("dve", vt[:, pb_, 64:65], 0.0, [vk + "1"])
                    E.memset("dve", vt[0:nvb, pb_, 64:65], 1.0, [vk + "1"])
            L_ = nxt_loaded
            if it_i + 1 < len(items):
                nxt_loaded = load_head(*items[it_i + 1])
            fox = L_["fox"]; h = L_["h"]; vt = L_["vt"]; vk = L_["vk"]; kt = L_["kt"]; ktk = L_["ktk"]
            qt = L_["qt"]; qtk = L_["qtk"]; KD = L_["KD"]; kr_keys = L_["kr_keys"]; scale = L_["scale"]; vkeys = L_["vkeys"]
            if True:
                nqt = lq // QW
                for t in range(nqt):
                    q0 = t * QW
                    if job.name == "p":
                        nfull = 1 + t * (QW // 128)
                        kbl = [(j, 0, None) for j in range(nfull)]
                        for jj in range(QW // 128):
                            kbl.append((nfull + jj, jj * 128, 0 if fox else 1))
                    else:
                        kbl = [(j, 0, None) for j in range(job.nbc)]
                        kbl.append((job.nbc, 0, 0 if fox else None))
                    pO, pOk = pO_r.next()
                    pend = []

                    def qk(idx):
                        j, c0, mk = kbl[idx]
                        pS, pSk = pS_r.next()
                        E.mm(pS[:, c0:QW], kt[0:KD, j * 128:(j + 1) * 128], qt[0:KD, q0 + c0:q0 + QW], True, mk is None, kr_keys + [qtk], [pSk])
                        if mk is not None:
                            E.mm(pS[:, c0:c0 + 128], identb[:, :], maskb[:, mk, :], False, True, ["identb", "maskb"], [pSk])
                        return pS, pSk

                    nxt = qk(0)
                    for idx in range(len(kbl)):
                        j, c0, mk = kbl[idx]
                        pS, pSk = nxt
                        if idx + 1 < len(kbl):
                            nxt = qk(idx + 1)
                        pt, ptk = pt_r.next()
                        if fox:
                            E.act(pt[:, c0:QW], pS[:, c0:QW], AF.Exp, [pSk, ncumk], [ptk], bias=ncum[:, j, h:h + 1], scale=scale)
                        else:
                            E.act(pt[:, c0:QW], pS[:, c0:QW], AF.Exp, [pSk], [ptk], scale=scale)
                        E.mm(pO[:, c0:QW], vt[:, j, :], pt[:, c0:QW], idx == 0, idx == len(kbl) - 1, vkeys + [vk + "1", ptk], [pOk])
                    osb, osbk = osb_r.next(); rec, reck = rec_r.next(); mixo, mixok = mixo_r.next()
                    E.cp("dve", osb[0:65, 0:QW], pO[0:65, 0:QW], [pOk], [osbk])
                    rsl = rs_cnt[0] % 4
                    rs_cnt[0] += 1
                    E.dma("sp", RS[rsl:rsl + 1, 0:QW], osb[64:65, 0:QW], [osbk], ["RS%d" % rsl], "rs_w%d" % rsl)
                    E.dma("sp", rec[0:64, 0:QW], RS[rsl:rsl + 1, 0:QW].partition_broadcast(64), ["RS%d" % rsl], [reck], "rs_r%d" % rsl)
                    E.P.op("dve", (lambda o, i: (lambda e: e.reciprocal(out=o, in_=i)))(rec[0:64, 0:QW], rec[0:64, 0:QW]), [reck], [reck])
                    E.tt("dve", mixo[0:64, 0:QW], osb[0:64, 0:QW], rec[0:64, 0:QW], ALU.mult, [osbk, reck], [mixok])
                    E.dma("sp", job.MIX[hd * 64:(hd + 1) * 64, q0:q0 + QW], mixo[0:64, 0:QW], [mixok], [], mixok + "o")
        if 'B' in PH:
            P.emit(nc)

    with ExitStack() as st:
        sb = lambda name, shape, dt: st.enter_context(nc.sbuf_tensor("C_" + name, shape, dt))
        psb = lambda name, shape, dt: st.enter_context(nc.psum_tensor("C_" + name, shape, dt))
        P = Prog(); E = Em(P)
        NFC = DFF // 128
        wo = sb("wo", [128, 8, D], BF16)
        wg = sb("wg", [128, 8, DFF], BF16)
        wu = sb("wu", [128, 8, DFF], BF16)
        wd = sb("wd", [128, NFC, D], BF16)
        identf = sb("identf", [128, 128], F32); identb = sb("identb", [128, 128], BF16)
        gffn = sb("gffn", [128, D], F32); gfin = sb("gfin", [128, D], F32)
        E.dma("sp", identf[:], c_ident[:, :], [], ["identf"], "w_id")
        E.cp("dve", identb[:], identf[:], ["identf"], ["identb"])
        E.dma("sp", gffn[:], g_ffn.partition_broadcast(128), [], ["gffn"], "w_gffn")
        E.dma("sp", gfin[:], g_fin.partition_broadcast(128), [], ["gfin"], "w_gfin")
        TWMAX = 256
        mix_r = Ring(sb, "mixt", [128, 8, TWMAX], BF16, 1)
        xl_r = Ring(sb, "xl", [128, D], F32, 2)
        _ce = [0]

        def wloadc(dst_ap, src_ap, wkey, ncols):
            stg, sk = xl_r.next()
            E.dma("sp", stg[:, 0:ncols], src_ap, [], [sk], sk)
            eng = "act" if _ce[0] % 2 == 0 else "dve"
            _ce[0] += 1
            E.cp(eng, dst_ap, stg[:, 0:ncols], [sk], [wkey])
        for c in range(8):
            wloadc(wo[:, c, :], w_out[c * 128:(c + 1) * 128, :], "wo", 1024)
        for c in range(8):
            for (c0, cw) in ((0, 1024), (1024, 1024), (2048, 768)):
                wloadc(wg[:, c, c0:c0 + cw], w_gate[c * 128:(c + 1) * 128, c0:c0 + cw], "wg", cw)
                wloadc(wu[:, c, c0:c0 + cw], w_up[c * 128:(c + 1) * 128, c0:c0 + cw], "wu", cw)
        for c in range(NFC):
            wloadc(wd[:, c, :], w_down[c * 128:(c + 1) * 128, :], "wd", 1024)
        x2_r = Ring(sb, "x2", [128, 2, D], F32, 1)
        h2_r = Ring(sb, "h2", [128, D], BF16, 2)
        h2T_r = Ring(sb, "h2T", [128, 8, TWMAX], BF16, 1)
        actT_r = Ring(sb, "actT", [128, NFC, TWMAX], BF16, 1)
        sg_r = Ring(sb, "sg", [128, TWMAX], F32, 2)
        junk_r = Ring(sb, "junkc", [128, D], BF16, 1)
        st_r = Ring(sb, "statc", [128, 8], F32, 4)
        for _i in range(4):
            E.memset("dve", st_r.t[_i][:], 0.0, [st_r.k[_i] + "z"])
        pA_r = Ring(psb, "pA", [128, 512], F32, 4)
        pF_r = Ring(psb, "pF", [128, 512], F32, 3)
        pT_r = Ring(psb, "pTc", [128, 1024], BF16, 1)

        def rstd_c(ss_ap, n, stt, stk):
            E.cp("act", stt[:, 6:7], stt[:, 7:8], [stk + "a", stk + "z"], [stk + "a2"])
            E.act(stt[:, 1:2], ss_ap, AF.Ln, [stk + "a", stk + "a2"], [stk + "b"], bias=EPS, scale=1.0 / n)
            E.act(stt[:, 2:3], stt[:, 1:2], AF.Exp, [stk + "b"], [stk + "c"], scale=-0.5)
            return stt[:, 2:3], stk + "c"

        for job in jobs:
            lq = job.lq
            TW = 256 if lq % 256 == 0 else 128
            nsub = TW // 128
            for t in range(lq // TW):
                q0 = t * TW
                mixt, mixk = mix_r.next()
                E.dma("sp", mixt[:, :, 0:TW], job.MIX[:, q0:q0 + TW].rearrange("(c p) t -> p c t", p=128), [], [mixk], mixk)
                x2, x2k = x2_r.next(); h2T, h2Tk = h2T_r.next()
                for s in range(nsub):
                    r0 = q0 + s * 128
                    if job.nvq == 128:
                        xl, xlk = xl_r.next()
                        E.dma("sp", xl[:], job.xq[r0:r0 + 128, :], [], [xlk], xlk)
                    else:
                        xl, xlk = xl_r.next()
                        E.dma("sp", xl[0:job.nvq, :], job.xq[0:job.nvq, :], [], [xlk], xlk)
                    pa0, pa0k = pA_r.next(); pa1, pa1k = pA_r.next()
                    for c in range(8):
                        E.mm(pa0[:, :], mixt[:, c, s * 128:(s + 1) * 128], wo[:, c, 0:512], c == 0, c == 7, [mixk, "wo"], [pa0k])
                        E.mm(pa1[:, :], mixt[:, c, s * 128:(s + 1) * 128], wo[:, c, 512:1024], c == 0, c == 7, [mixk, "wo"], [pa1k])
                    E.tt("dve", x2[:, s, 0:512], pa0[:, :], xl[:, 0:512], ALU.add, [pa0k, xlk], [x2k + "a%d" % s])
                    E.tt("dve", x2[:, s, 512:1024], pa1[:, :], xl[:, 512:1024], ALU.add, [pa1k, xlk], [x2k + "b%d" % s])
                    xk2 = [x2k + "a%d" % s, x2k + "b%d" % s]
                    junk, jk = junk_r.next(); stt, stk = st_r.next()
                    E.act(junk[:], x2[:, s, :], AF.Square, xk2 + [stk + "z"], [jk, stk + "a"], accum=stt[:, 0:1])
                    rs, rsk = rstd_c(stt[:, 0:1], D, stt, stk)
                    h2, h2k = h2_r.next()
                    E.stt(h2[:], x2[:, s, :], rs, gffn[:], ALU.mult, ALU.mult, xk2 + [rsk, "gffn"], [h2k])
                    pT, pTk = pT_r.next()
                    for c in range(8):
                        E.tr(pT[:, c * 128:(c + 1) * 128], h2[:, c * 128:(c + 1) * 128], identb[:], [h2k, "identb"], [pTk])
                    E.cp("act", h2T[:, :, s * 128:(s + 1) * 128], pT[:, :].rearrange("p (c t) -> p c t", c=8), [pTk], [h2Tk + str(s)])
                h2keys = [h2Tk + str(s) for s in range(nsub)]
                actT, actTk = actT_r.next()
                for f in range(NFC):
                    pgt, pgtk = pF_r.next(); put, putk = pF_r.next()
                    for c in range(8):
                        E.mm(pgt[:, 0:TW], wg[:, c, f * 128:(f + 1) * 128], h2T[:, c, 0:TW], c == 0, c == 7, ["wg"] + h2keys, [pgtk])
                    for c in range(8):
                        E.mm(put[:, 0:TW], wu[:, c, f * 128:(f + 1) * 128], h2T[:, c, 0:TW], c == 0, c == 7, ["wu"] + h2keys, [putk])
                    sg, sgk = sg_r.next()
                    E.act(sg[:, 0:TW], pgt[:, 0:TW], AF.Silu, [pgtk], [sgk])
                    E.tt("dve", actT[:, f, 0:TW], sg[:, 0:TW], put[:, 0:TW], ALU.mult, [sgk, putk], [actTk + "_%d" % f])
                akeys = [actTk + "_%d" % f for f in range(NFC)]
                for s in range(nsub):
                    r0 = q0 + s * 128
                    pa0, pa0k = pA_r.next(); pa1, pa1k = pA_r.next()
                    for f in range(NFC):
                        E.mm(pa0[:, :], actT[:, f, s * 128:(s + 1) * 128], wd[:, f, 0:512], f == 0, f == NFC - 1, akeys + ["wd"], [pa0k])
                        E.mm(pa1[:, :], actT[:, f, s * 128:(s + 1) * 128], wd[:, f, 512:1024], f == 0, f == NFC - 1, akeys + ["wd"], [pa1k])
                    xk2 = [x2k + "a%d" % s, x2k + "b%d" % s]
                    E.tt("dve", x2[:, s, 0:512], pa0[:, :], x2[:, s, 0:512], ALU.add, [pa0k] + xk2, [x2k + "a%d" % s])
                    E.tt("dve", x2[:, s, 512:1024], pa1[:, :], x2[:, s, 512:1024], ALU.add, [pa1k] + xk2, [x2k + "b%d" % s])
                    junk, jk = junk_r.next(); stt, stk = st_r.next()
                    E.act(junk[:], x2[:, s, :], AF.Square, xk2 + [stk + "z"], [jk, stk + "a"], accum=stt[:, 0:1])
                    rs, rsk = rstd_c(stt[:, 0:1], D, stt, stk)
                    E.stt(x2[:, s, :], x2[:, s, :], rs, gfin[:], ALU.mult, ALU.mult, xk2 + [rsk, "gfin"], xk2)
                    nv = job.nvq
                    E.dma("sp", job.y[r0:r0 + nv, :], x2[0:nv, s, :], xk2, [], x2k + "o%d" % s)
        if 'C' in PH:
            P.emit(nc)
    return nc


def make_consts(S, PAST):
    NBF = S // 128
    ident = np.eye(128, dtype=np.float32)
    jj = np.arange(128)[:, None]; tt = np.arange(128)[None, :]
    U = (jj <= tt).astype(np.float32)
    tri = np.stack([U, U * (jj < 16), U * (jj < 32)]).astype(np.float32)
    on = np.ones((128, 128), np.float32)
    ones = np.stack([on, on * (jj < 16), on * (jj < 32)]).astype(np.float32)
    m_fox = np.where(jj <= tt, 0.0, NEG)
    m_mla = np.where((jj // 64) <= (tt // 64), 0.0, NEG)
    mask = np.stack([m_fox, m_mla]).astype(np.float32)
    half = 16
    inv = (10000.0 ** (-np.arange(half, dtype=np.float32) / half)).astype(np.float32)

    def tab(pos):
        ang = pos.astype(np.float32)[..., None] * inv
        c = np.cos(ang).astype(np.float32); s = np.sin(ang).astype(np.float32)
        return np.concatenate([c, c, -s, s], axis=-1).astype(np.float32)
    p = np.arange(128)[:, None]
    b = np.arange(NBF + 1)[None, :]
    posp = np.where(b == 0, p, NMETA + (b - 1) * 128 + p)
    ropep = tab(posp)
    ropes = tab((PAST + np.arange(128))[:, None])
    return dict(c_ident=ident, c_tri=tri, c_ones=ones, c_mask=mask, c_ropep=ropep, c_ropes=ropes)


def make_in_maps(inp, ncores, S, PAST, NS):
    consts = make_consts(S, PAST)
    f = lambda a: np.ascontiguousarray(np.asarray(a, dtype=np.float32))
    shared = dict(
        meta=f(inp["meta_tokens"]), w_in=f(inp["w_in"][0]), w_uq=f(inp["w_mla_uq"][0]), w_ukv=f(inp["w_mla_ukv"][0]),
        w_out=f(inp["w_out"][0]), w_gate=f(inp["w_ffn_gate"][0]), w_up=f(inp["w_ffn_up"][0]), w_down=f(inp["w_ffn_down"][0]),
        g_mix=f(inp["norm_mix"][0]).reshape(1, -1), b_forget=f(inp["b_forget"][0]).reshape(1, -1),
        g_q=f(inp["mla_q_norm"][0]).reshape(1, -1), g_kv=f(inp["mla_kv_norm"][0]).reshape(1, -1),
        g_ffn=f(inp["norm_ffn"][0]).reshape(1, -1), g_fin=f(inp["norm_final"]).reshape(1, -1), **consts)
    maps = []
    for c in range(ncores):
        m = dict(shared)
        m["xp"] = f(inp["x_prompt"][c])
        sl = slice(c * NS, (c + 1) * NS)
        m["xs"] = f(inp["x_sample"][sl])
        m["c_lat"] = f(inp["cache_mla_latent"][0, sl]); m["c_kr"] = f(inp["cache_mla_krope"][0, sl])
        m["c_fk"] = f(inp["cache_fox_k"][0, sl]).reshape(NS, PAST, 512); m["c_fv"] = f(inp["cache_fox_v"][0, sl]).reshape(NS, PAST, 512)
        m["c_lf"] = f(inp["cache_fox_logf"][0, sl])
        maps.append(m)
    return maps


def assemble(results, ncores, S, NS):
    LP = NMETA + S
    cat = lambda k: np.concatenate([np.asarray(r[k], dtype=np.float32)[None] for r in results], axis=0)
    y_p = cat("y_p")
    y_s = cat("y_s").reshape(ncores * NS, 32, D)
    lat_p = cat("lat_p")[None]; kr_p = cat("kr_p")[None]
    fk_p = cat("fk_p").reshape(1, ncores, LP, 8, 64); fv_p = cat("fv_p").reshape(1, ncores, LP, 8, 64)
    lf_p = cat("lf_p")[None]
    lat_s = cat("lat_s").reshape(1, ncores * NS, 32, 256); kr_s = cat("kr_s").reshape(1, ncores * NS, 32, 32)
    fk_s = cat("fk_s").reshape(1, ncores * NS, 32, 8, 64); fv_s = cat("fv_s").reshape(1, ncores * NS, 32, 8, 64)
    lf_s = cat("lf_s").reshape(1, ncores * NS, 32, 8)
    return (y_p, y_s, lat_p, kr_p, fk_p, fv_p, lf_p, lat_s, kr_s, fk_s, fv_s, lf_s)


def kernel(**inp):
    ncores = 8
    B, S, _ = inp["x_prompt"].shape
    PAST = inp["cache_mla_latent"].shape[2]
    NS = inp["x_sample"].shape[0] // ncores
    assert B == ncores
    nc = build(S, PAST, NS)
    maps = make_in_maps(inp, ncores, S, PAST, NS)
    res = run_bass_kernel_spmd(nc, maps, core_ids=list(range(ncores)))
    return assemble(res.results, ncores, S, NS)
```

```python
from contextlib import ExitStack
import numpy as np
import concourse.bass as bass
import concourse.mybir as mybir
from concourse.bass_utils import run_bass_kernel_spmd

F32 = mybir.dt.float32
BF16 = mybir.dt.bfloat16
AF = mybir.ActivationFunctionType
ALU = mybir.AluOpType

D = 1024
DIN = 2216
DFF = 2816
NMETA = 16
EPS = 1e-6
MLA_SCALE = 96 ** -0.5
FOX_SCALE = 0.125
NEG = -30000.0
COMPUTE = ("pe", "act", "dve", "pool")


class Prog:
    uid = 0

    def __init__(self):
        self.ins = []

    def op(self, eng, fn, r=(), w=()):
        self.ins.append((eng, fn, tuple(r), tuple(w), None))

    def dma(self, eng, fn, r=(), w=(), tag=None):
        assert tag is not None
        self.ins.append((eng, fn, tuple(r), tuple(w), tag))

    def analyze(self):
        ins = self.ins
        n = len(ins)
        last_w, readers = {}, {}
        need = [False] * n
        waits = [None] * n
        for i, (eng, fn, R, W, tag) in enumerate(ins):
            d = {}
            for k in R:
                j = last_w.get(k)
                if j is not None:
                    d[j] = True
            for k in W:
                j = last_w.get(k)
                if j is not None and j not in d:
                    d[j] = d.get(j, False)
                for j in readers.get(k, ()):
                    if j != i and j not in d:
                        d[j] = False
            for k in R:
                readers.setdefault(k, []).append(i)
            for k in W:
                last_w[k] = i
                readers[k] = []
            wl = []
            for j, raw in d.items():
                ej, tj = ins[j][0], ins[j][4]
                if tj is None:
                    if ej == "pe" and eng == "pe":
                        continue
                    need[j] = True
                wl.append(j)
            waits[i] = wl
        cnt, val, semkey = {}, [0] * n, [None] * n
        for i in range(n):
            eng, tag = ins[i][0], ins[i][4]
            if tag is not None:
                key = ("d", tag)
                cnt[key] = cnt.get(key, 0) + 16
            elif need[i]:
                key = ("e", eng)
                cnt[key] = cnt.get(key, 0) + 1
            else:
                continue
            val[i] = cnt[key]
            semkey[i] = key
        self.need, self.waits, self.val, self.semkey = need, waits, val, semkey
        return cnt

    def emit(self, nc, final_tags=None):
        cnt = self.analyze()
        ins = self.ins
        with nc.cleanup_on_exit():
          with ExitStack() as st:
            sems = {}
            for idx, k in enumerate(cnt.keys()):
                sems[k] = nc.alloc_semaphore(name="s%d_%d_%s" % (Prog.uid, idx, str(k[1])[:12]))
            Prog.uid += 1
            block = st.enter_context(nc.Block())
            per_eng = {}
            for i, rec in enumerate(ins):
                per_eng.setdefault(rec[0], []).append(i)
            if final_tags is None:
                final_tags = [k[1] for k in cnt if k[0] == "d"]

            def run(name, e):
                waited = {}
                for i in per_eng.get(name, []):
                    fn, tag = ins[i][1], ins[i][4]
                    req = {}
                    for j in self.waits[i]:
                        k, v = self.semkey[j], self.val[j]
                        if v > req.get(k, 0):
                            req[k] = v
                    for k, v in req.items():
                        if waited.get(k, 0) >= v:
                            continue
                        e.wait_ge(sems[k], v)
                        waited[k] = v
                    bi = fn(e)
                    if tag is not None:
                        bi.then_inc(sems[("d", tag)], 16)
                    elif self.need[i]:
                        bi.then_inc(sems[("e", name)], 1)
                if name == "sp":
                    for t in final_tags:
                        e.wait_ge(sems[("d", t)], cnt[("d", t)])

            block.tensor(lambda e: run("pe", e))
            block.scalar(lambda e: run("act", e))
            block.vector(lambda e: run("dve", e))
            block.gpsimd(lambda e: run("pool", e))
            block.sync(lambda e: run("sp", e))
          nc.all_engine_barrier()


class Ring:
    def __init__(self, alloc, name, shape, dt, n):
        self.t = [alloc("%s%d" % (name, i), shape, dt) for i in range(n)]
        self.k = ["%s%d" % (name, i) for i in range(n)]
        self.n = n
        self.i = -1

    def next(self):
        self.i += 1
        s = self.i % self.n
        return self.t[s], self.k[s]


class Em:
    def __init__(self, P):
        self.P = P

    def mm(self, out, lhsT, rhs, start, stop, r, w):
        self.P.op("pe", lambda e: e.matmul(out, lhsT=lhsT, rhs=rhs, start=start, stop=stop), r, w)

    def tr(self, out, in_, ident, r, w):
        self.P.op("pe", lambda e: e.transpose(out=out, in_=in_, identity=ident), r, w)

    def act(self, out, in_, func, r, w, bias=None, scale=None, accum=None):
        kw = {}
        if bias is not None:
            kw["bias"] = bias
        if scale is not None:
            kw["scale"] = scale
        if accum is not None:
            kw["accum_out"] = accum
        self.P.op("act", lambda e: e.activation(out=out, in_=in_, func=func, **kw), r, w)

    def cp(self, eng, out, in_, r, w):
        if eng == "act":
            self.P.op("act", lambda e: e.copy(out=out, in_=in_), r, w)
        else:
            self.P.op(eng, lambda e: e.tensor_copy(out=out, in_=in_), r, w)

    def tt(self, eng, out, in0, in1, op, r, w):
        self.P.op(eng, lambda e: e.tensor_tensor(out=out, in0=in0, in1=in1, op=op), r, w)

    def ts(self, eng, out, in0, s1, s2, op0, op1, r, w):
        if s2 is None:
            self.P.op(eng, lambda e: e.tensor_scalar(out=out, in0=in0, scalar1=s1, scalar2=None, op0=op0), r, w)
        else:
            self.P.op(eng, lambda e: e.tensor_scalar(out=out, in0=in0, scalar1=s1, scalar2=s2, op0=op0, op1=op1), r, w)

    def stt(self, out, in0, scalar, in1, op0, op1, r, w):
        self.P.op("dve", lambda e: e.scalar_tensor_tensor(out=out, in0=in0, scalar=scalar, in1=in1, op0=op0, op1=op1), r, w)

    def memset(self, eng, ap, v, w):
        self.P.op(eng, lambda e: e.memset(ap, v), (), w)

    def dma(self, eng, out, in_, r, w, tag):
        self.P.dma(eng, lambda e: e.dma_start(out=out, in_=in_), r, w, tag)


class Job:
    pass


def build(S, PAST, NS):
    import os
    nc = bass.Bass("TRN2", target_bir_lowering=False)
    NBF = S // 128
    QW_P = 512 if S % 512 == 0 else 128
    NBC = PAST // 128
    LP = NMETA + S

    def din(name, shape):
        return nc.dram_tensor(name, list(shape), F32, kind="ExternalInput").ap()

    def dout(name, shape):
        return nc.dram_tensor(name, list(shape), F32, kind="ExternalOutput").ap()

    def dscr(name, shape, dt=BF16):
        return nc.dram_tensor(name, list(shape), dt, kind="Internal").ap()

    xp = din("xp", [S, D]); meta = din("meta", [NMETA, D]); xs = din("xs", [NS, 32, D])
    c_lat = din("c_lat", [NS, PAST, 256]); c_kr = din("c_kr", [NS, PAST, 32])
    c_fk = din("c_fk", [NS, PAST, 512]); c_fv = din("c_fv", [NS, PAST, 512]); c_lf = din("c_lf", [NS, PAST, 8])
    w_in = din("w_in", [D, DIN]); w_uq = din("w_uq", [384, 768]); w_ukv = din("w_ukv", [256, 1024])
    w_out = din("w_out", [D, D]); w_gate = din("w_gate", [D, DFF]); w_up = din("w_up", [D, DFF]); w_down = din("w_down", [DFF, D])
    g_mix = din("g_mix", [1, D]); b_forget = din("b_forget", [1, 8]); g_q = din("g_q", [1, 384]); g_kv = din("g_kv", [1, 256])
    g_ffn = din("g_ffn", [1, D]); g_fin = din("g_fin", [1, D])
    c_ident = din("c_ident", [128, 128])
    c_tri = din("c_tri", [3, 128, 128])
    c_ones = din("c_ones", [3, 128, 128])
    c_mask = din("c_mask", [2, 128, 128])
    NPB = NBF + 1
    c_ropep = din("c_ropep", [128, NPB, 64])
    c_ropes = din("c_ropes", [128, 1, 64])

    y_p = dout("y_p", [S, D]); lat_p = dout("lat_p", [LP, 256]); kr_p = dout("kr_p", [LP, 32])
    fk_p = dout("fk_p", [LP, 512]); fv_p = dout("fv_p", [LP, 512]); lf_p = dout("lf_p", [LP, 8])
    y_s = dout("y_s", [NS, 32, D]); lat_s = dout("lat_s", [NS, 32, 256]); kr_s = dout("kr_s", [NS, 32, 32])
    fk_s = dout("fk_s", [NS, 32, 512]); fv_s = dout("fv_s", [NS, 32, 512]); lf_s = dout("lf_s", [NS, 32, 8])

    jobs = []
    jp = Job()
    jp.name = "p"; jp.nbc = 0; jp.cache = None
    jp.blocks = [dict(src=meta, r0=0, nv=NMETA, q=False, tri=1, rope=(c_ropep, 0), orow=0)]
    for b in range(NBF):
        jp.blocks.append(dict(src=xp, r0=b * 128, nv=128, q=True, tri=0, rope=(c_ropep, b + 1), orow=NMETA + b * 128))
    jp.outs = (lat_p, kr_p, fk_p, fv_p, lf_p); jp.y = y_p; jp.xq = xp
    jp.qw = QW_P; jp.lq = S; jp.nbk = NBF + 1; jp.nvq = 128
    jp.mla_mask = 1
    jobs.append(jp)
    for s in range(NS):
        j = Job()
        j.name = "s%d" % s; j.nbc = NBC
        j.cache = (c_lat[s], c_kr[s], c_fk[s], c_fv[s], c_lf[s])
        j.blocks = [dict(src=xs[s], r0=0, nv=32, q=True, tri=2, rope=(c_ropes, 0), orow=0)]
        j.outs = (lat_s[s], kr_s[s], fk_s[s], fv_s[s], lf_s[s]); j.y = y_s[s]; j.xq = xs[s]
        j.qw = 128; j.lq = 128; j.nbk = NBC + 1; j.nvq = 32
        j.mla_mask = None
        jobs.append(j)
    for j in jobs:
        lk = j.nbk * 128
        n = j.name
        j.KTn = dscr("KTn_" + n, [8, 64, lk]); j.KTr = dscr("KTr_" + n, [32, lk]); j.KTf = (nc.dram_tensor("KTf_" + n, [8, 64, lk], BF16, kind="ExternalOutput").ap() if (os.environ.get("K_DBG") and n == "p") else dscr("KTf_" + n, [8, 64, lk]))
        j.Vm = dscr("Vm_" + n, [8, 128, j.nbk, 64]); j.Vf = dscr("Vf_" + n, [8, 128, j.nbk, 64])
        j.QTm = dscr("QTm_" + n, [8, 96, j.lq]); j.QTf = (nc.dram_tensor("QTf_" + n, [8, 65, j.lq], BF16, kind="ExternalOutput").ap() if (os.environ.get("K_DBG") and n == "p") else dscr("QTf_" + n, [8, 65, j.lq]))
        j.MIX = (nc.dram_tensor("MIX_" + n, [D, j.lq], BF16, kind="ExternalOutput").ap() if (os.environ.get("K_DBG") and n == "p") else dscr("MIX_" + n, [D, j.lq])); j.NCUM = dscr("NCUM_" + n, [128, j.nbk, 8], F32)
    LKMAX = max(j.nbk for j in jobs) * 128
    LQMAX = max(j.lq for j in jobs)
    NBKMAX = LKMAX // 128

    import os
    PH = os.environ.get('K_PHASES', 'ABC')
    with ExitStack() as st:
        sb = lambda name, shape, dt: st.enter_context(nc.sbuf_tensor("A_" + name, shape, dt))
        psb = lambda name, shape, dt: st.enter_context(nc.psum_tensor("A_" + name, shape, dt))
        P = Prog(); E = Em(P)
        STQ = os.environ.get('K_STQ', 'sp')
        win = sb("win", [128, 8, DIN], BF16)
        wuq = sb("wuq", [128, 3, 768], BF16)
        wuk = sb("wuk", [128, 2, 8, 64], BF16)
        wuv = sb("wuv", [128, 2, 8, 64], BF16)
        identf = sb("identf", [128, 128], F32); identb = sb("identb", [128, 128], BF16)
        tri = sb("tri", [128, 3, 128], F32); ones3 = sb("ones3", [128, 3, 128], F32)
        gmix = sb("gmix", [128, D], F32); gq = sb("gq", [128, 384], F32); gkv = sb("gkv", [128, 256], F32)
        bfg = sb("bfg", [128, 8], F32)
        ropep = sb("ropep", [128, NPB, 64], F32); ropes = sb("ropes", [128, 1, 64], F32)
        carry = sb("carry", [128, 8], F32)
        ncum = sb("ncum", [128, NBKMAX, 8], F32)
        CQ, CKV, CKR, CLG, CFQ, CFK, CFV = 0, 384, 640, 672, 680, 1192, 1704
        src_off = dict(cq=0, ckv=384, kr=640, fq=672, fk=1184, fv=1696, lg=2208)
        stg_r = Ring(sb, "xt", [128, D], F32, 2)
        _ce = [0]

        def wload(dst_ap, src_ap, wkey, ncols):
            stg, sk = stg_r.next()
            E.dma("sp", stg[:, 0:ncols], src_ap, [], [sk], sk)
            eng = "act" if _ce[0] % 2 == 0 else "dve"
            _ce[0] += 1
            E.cp(eng, dst_ap, stg[:, 0:ncols], [sk], [wkey])
        for c in range(8):
            for (dst, so, wd_) in ((CQ, 0, 384), (CKV, 384, 256), (CKR, 640, 32), (CLG, 2208, 8), (CFQ, 672, 512), (CFK, 1184, 512), (CFV, 1696, 512)):
                wload(win[:, c, dst:dst + wd_], w_in[c * 128:(c + 1) * 128, so:so + wd_], "win", wd_)
        for c in range(3):
            wload(wuq[:, c, :], w_uq[c * 128:(c + 1) * 128, :], "wuq", 768)
        for c in range(2):
            stg, sk = stg_r.next()
            E.dma("sp", stg[:, :], w_ukv[c * 128:(c + 1) * 128, :], [], [sk], sk)
            sv = stg[:, :].rearrange("p (h t d) -> p h t d", h=8, t=2)
            E.cp("act", wuk[:, c], sv[:, :, 0, :], [sk], ["wuk"])
            E.cp("dve", wuv[:, c], sv[:, :, 1, :], [sk], ["wuv"])
        E.dma("sp", identf[:], c_ident[:, :], [], ["identf"], "w_id")
        E.cp("dve", identb[:], identf[:], ["identf"], ["identb"])
        E.dma("sp", tri[:], c_tri.rearrange("a p n -> p a n"), [], ["tri"], "w_tri")
        E.dma("sp", ones3[:], c_ones.rearrange("a p n -> p a n"), [], ["ones3"], "w_ones")
        E.dma("sp", gmix[:], g_mix.partition_broadcast(128), [], ["gmix"], "w_gmix")
        E.dma("sp", gq[:], g_q.partition_broadcast(128), [], ["gq"], "w_gq")
        E.dma("sp", gkv[:], g_kv.partition_broadcast(128), [], ["gkv"], "w_gkv")
        E.dma("sp", bfg[:], b_forget.partition_broadcast(128), [], ["bfg"], "w_bfg")
        E.dma("sp", ropep[:], c_ropep[:, :, :], [], ["ropep"], "w_ropep")
        E.dma("sp", ropes[:], c_ropes[:, :, :], [], ["ropes"], "w_ropes")
        rope_sb = {id(c_ropep): (ropep, "ropep"), id(c_ropes): (ropes, "ropes")}

        xt_r = stg_r
        xpart = sb("xpart", [128, D], F32)
        E.memset("dve", xpart[:], 0.0, ["xpart"])
        junk_r = Ring(sb, "junk", [128, D], BF16, 2)
        st_r = Ring(sb, "stat", [128, 8], F32, 4)
        for _i in range(4):
            E.memset("dve", st_r.t[_i][:], 0.0, [st_r.k[_i] + "z"])
        hb_r = Ring(sb, "hb", [128, D], BF16, 2)
        hT_r = Ring(sb, "hT", [128, 8, 128], BF16, 2)
        lat32_r = Ring(sb, "lat32", [128, 256], F32, 3)
        kr32_r = Ring(sb, "kr32", [128, 32], F32, 3)
        krt_r = Ring(sb, "krt", [128, 64], F32, 2)
        lf32_r = Ring(sb, "lf32", [128, 8], F32, 4)
        lgt_r = Ring(sb, "lgt", [128, 8], F32, 2)
        fk32_r = Ring(sb, "fk32", [128, 512], F32, 3)
        fv32_r = Ring(sb, "fv32", [128, 512], F32, 3)
        lkb_r = Ring(sb, "lkb", [128, 352], BF16, 2)
        for _i in range(2):
            E.memset("dve", lkb_r.t[_i][:, 256:320], 0.0, [lkb_r.k[_i] + "p"])
        fkb_r = Ring(sb, "fkb", [128, 512], BF16, 2)
        vfb_r = Ring(sb, "vfb", [128, 512], BF16, 2)
        vmb_r = Ring(sb, "vmb", [128, 512], BF16, 2)
        cqn_r = Ring(sb, "cqn", [128, 384], BF16, 2)
        fqb_r = Ring(sb, "fqb", [128, 512], BF16, 2)
        cqT_r = Ring(sb, "cqT", [128, 3, 128], BF16, 2)
        qb_r = Ring(sb, "qb", [128, 768], BF16, 2)
        qrt_r = Ring(sb, "qrt", [128, 8, 64], F32, 2)
        latT_r = Ring(sb, "latT", [128, 3, 128], BF16, 2)
        ktn_r = Ring(sb, "ktn", [64, 8, 128], BF16, 2)
        ktf_r = Ring(sb, "ktf", [64, 8, 128], BF16, 2)
        qtm_r = Ring(sb, "qtm", [96, 8, 128], BF16, 2)
        qtf_r = Ring(sb, "qtf", [64, 8, 128], BF16, 2)
        cum32_r = Ring(sb, "cum32", [128, 8], F32, 2)
        cum8_r = Ring(sb, "cum8", [128, 8], BF16, 2)
        cumT_r = Ring(sb, "cumT", [8, 128], BF16, 2)
        pT_r = Ring(psb, "pT", [128, 1024], BF16, 2)
        pG_r = Ring(psb, "pG", [128, 512], F32, 5)

        def rstd_from(ss_ap, n, stt, stk):
            E.cp("act", stt[:, 6:7], stt[:, 7:8], [stk + "a", stk + "z"], [stk + "a2"])
            E.act(stt[:, 1:2], ss_ap, AF.Ln, [stk + "a", stk + "a2"], [stk + "b"], bias=EPS, scale=1.0 / n)
            E.act(stt[:, 2:3], stt[:, 1:2], AF.Exp, [stk + "b"], [stk + "c"], scale=-0.5)
            return stt[:, 2:3], stk + "c"

        def rope32(out32, ok, src, sk, tab, tk, tmp, tmk):
            E.tt("dve", tmp[:, 0:32], src, tab[:, 0:32], ALU.mult, [sk, tk], [tmk + "a"])
            E.tt("dve", tmp[:, 32:48], src[:, 16:32], tab[:, 32:48], ALU.mult, [sk, tk], [tmk + "b"])
            E.tt("dve", tmp[:, 48:64], src[:, 0:16], tab[:, 48:64], ALU.mult, [sk, tk], [tmk + "c"])
            E.tt("dve", out32, tmp[:, 0:32], tmp[:, 32:64], ALU.add, [tmk + "a", tmk + "b", tmk + "c"], [ok])

        for job in jobs:
            lat_o, kr_o, fk_o, fv_o, lf_o = job.outs
            jn = job.name
            E.memset("dve", carry[:], 0.0, ["carry"])
            kb = 0
            qblk = 0
            allblocks = [("c", i) for i in range(min(job.nbc, int(os.environ.get("K_NBC_LIMIT", "9999"))))] + [("n", b) for b in job.blocks]
            def issue_loads(job_, kind_, bi_):
                d = dict(job=job_, kind=kind_, bi=bi_)
                if kind_ == "c":
                    cl, ckr_, cfk_, cfv_, clf_ = job_.cache
                    r0 = bi_ * 128
                    d["lat32"] = lat32_r.next(); d["kr32"] = kr32_r.next()
                    d["fk32"] = fk32_r.next(); d["fv32"] = fv32_r.next(); d["lf32"] = lf32_r.next()
                    E.dma("sp", d["lat32"][0][:], cl[r0:r0 + 128, :], [], [d["lat32"][1]], d["lat32"][1])
                    E.dma("sp", d["kr32"][0][:], ckr_[r0:r0 + 128, :], [], [d["kr32"][1]], d["kr32"][1])
                    E.dma("sp", d["fk32"][0][:], cfk_[r0:r0 + 128, :], [], [d["fk32"][1]], d["fk32"][1])
                    E.dma("sp", d["fv32"][0][:], cfv_[r0:r0 + 128, :], [], [d["fv32"][1]], d["fv32"][1])
                    E.dma("sp", d["lf32"][0][:], clf_[r0:r0 + 128, :], [], [d["lf32"][1]], d["lf32"][1])
                else:
                    blk_ = bi_
                    nv_ = blk_["nv"]
                    if nv_ == 128:
                        xt_, xtk_ = xt_r.next()
                        E.dma("sp", xt_[:], blk_["src"][blk_["r0"]:blk_["r0"] + 128, :], [], [xtk_], xtk_)
                    else:
                        xt_, xtk_ = xpart, "xpart"
                        E.dma("sp", xt_[0:nv_, :], blk_["src"][blk_["r0"]:blk_["r0"] + nv_, :], [], [xtk_], "xpartd")
                    d["xt"] = (xt_, xtk_)
                return d

            if job is jobs[0]:
                flat = [(j_, k_, b_) for j_ in jobs for (k_, b_) in ([("c", i) for i in range(j_.nbc)] + [("n", b) for b in j_.blocks])]
                flat_i = [0]
                pre_ld = [issue_loads(*flat[0])]
            for kind, bi in allblocks:
                LD = pre_ld[0]
                flat_i[0] += 1
                if flat_i[0] < len(flat):
                    pre_ld[0] = issue_loads(*flat[flat_i[0]])
                lkb, lkbk = lkb_r.next(); fkb, fkbk = fkb_r.next(); vfb, vfbk = vfb_r.next()
                hasq = False
                if kind == "c":
                    lat32, lat32k = LD["lat32"]; kr32, kr32k = LD["kr32"]
                    fk32, fk32k = LD["fk32"]; fv32, fv32k = LD["fv32"]; lf32, lf32k = LD["lf32"]
                    E.cp("act", lkb[:, 0:256], lat32[:], [lat32k], [lkbk + "l"])
                    E.cp("dve", lkb[:, 320:352], kr32[:], [kr32k], [lkbk + "r"])
                    E.cp("act", fkb[:], fk32[:], [fk32k], [fkbk])
                    E.cp("dve", vfb[:], fv32[:], [fv32k], [vfbk])
                    tri_i = 0
                else:
                    lf32, lf32k = lf32_r.next()
                    blk = bi
                    nv = blk["nv"]; hasq = blk["q"]; tri_i = blk["tri"]
                    rtab_t, rtab_k = rope_sb[id(blk["rope"][0])]
                    rtab = rtab_t[:, blk["rope"][1], :]
                    orow = blk["orow"]
                    xt, xtk = LD["xt"]
                    junk, jk = junk_r.next(); stt, stk = st_r.next()
                    E.act(junk[:], xt[:], AF.Square, [xtk, stk + "z"], [jk, stk + "a"], accum=stt[:, 0:1])
                    rs, rsk = rstd_from(stt[:, 0:1], D, stt, stk)
                    hb, hbk = hb_r.next()
                    E.stt(hb[:], xt[:], rs, gmix[:], ALU.mult, ALU.mult, [xtk, rsk, "gmix"], [hbk])
                    pT, pTk = pT_r.next(); hT, hTk = hT_r.next()
                    for c in range(8):
                        E.tr(pT[:, c * 128:(c + 1) * 128], hb[:, c * 128:(c + 1) * 128], identb[:], [hbk, "identb"], [pTk])
                    E.cp("act", hT[:].rearrange("p c t -> p (c t)"), pT[:, :], [pTk], [hTk])
                    pg, pgk = pG_r.next()
                    for c in range(8):
                        E.mm(pg[:, 0:296], hT[:, c, :], win[:, c, CKV:CKV + 296], c == 0, c == 7, [hTk, "win"], [pgk])
                    stt2, stk2 = st_r.next(); junk2, jk2 = junk_r.next()
                    E.act(junk2[:, 0:256], pg[:, 0:256], AF.Square, [pgk, stk2 + "z"], [jk2, stk2 + "a"], accum=stt2[:, 0:1])
                    rs2, rs2k = rstd_from(stt2[:, 0:1], 256, stt2, stk2)
                    lat32, lat32k = lat32_r.next()
                    E.stt(lat32[:], pg[:, 0:256], rs2, gkv[:], ALU.mult, ALU.mult, [pgk, rs2k, "gkv"], [lat32k])
                    E.dma(STQ, lat_o[orow:orow + nv, :], lat32[0:nv, :], [lat32k], [], lat32k + "o")
                    E.cp("act", lkb[:, 0:256], lat32[:], [lat32k], [lkbk + "l"])
                    kr32, kr32k = kr32_r.next(); krt, krtk = krt_r.next()
                    rope32(kr32[:], kr32k, pg[:, 256:288], pgk, rtab, rtab_k, krt, krtk)
                    E.dma(STQ, kr_o[orow:orow + nv, :], kr32[0:nv, :], [kr32k], [], kr32k + "o")
                    E.cp("act", lkb[:, 320:352], kr32[:], [kr32k], [lkbk + "r"])
                    lgt, lgtk = lgt_r.next()
                    E.tt("dve", lgt[:], pg[:, 288:296], bfg[:], ALU.add, [pgk, "bfg"], [lgtk])
                    E.act(lgt[:], lgt[:], AF.Exp, [lgtk], [lgtk], scale=-1.0)
                    E.act(lgt[:], lgt[:], AF.Ln, [lgtk], [lgtk], bias=1.0)
                    E.ts("dve", lf32[:], lgt[:], -1.0, None, ALU.mult, None, [lgtk], [lf32k])
                    E.dma(STQ, lf_o[orow:orow + nv, :], lf32[0:nv, :], [lf32k], [], lf32k + "o")
                    pg, pgk = pG_r.next()
                    for c in range(8):
                        E.mm(pg[:, :], hT[:, c, :], win[:, c, CFK:CFK + 512], c == 0, c == 7, [hTk, "win"], [pgk])
                    fk32, fk32k = fk32_r.next()
                    E.cp("act", fk32[:], pg[:, :], [pgk], [fk32k])
                    E.dma(STQ, fk_o[orow:orow + nv, :], fk32[0:nv, :], [fk32k], [], fk32k + "o")
                    E.cp("act", fkb[:], fk32[:], [fk32k], [fkbk])
                    pg, pgk = pG_r.next()
                    for c in range(8):
                        E.mm(pg[:, :], hT[:, c, :], win[:, c, CFV:CFV + 512], c == 0, c == 7, [hTk, "win"], [pgk])
                    fv32, fv32k = fv32_r.next()
                    E.cp("act", fv32[:], pg[:, :], [pgk], [fv32k])
                    E.dma(STQ, fv_o[orow:orow + nv, :], fv32[0:nv, :], [fv32k], [], fv32k + "o")
                    E.cp("act", vfb[:], fv32[:], [fv32k], [vfbk])
                    if hasq:
                        pg, pgk = pG_r.next()
                        for c in range(8):
                            E.mm(pg[:, 0:384], hT[:, c, :], win[:, c, CQ:CQ + 384], c == 0, c == 7, [hTk, "win"], [pgk])
                        stt3, stk3 = st_r.next(); junk3, jk3 = junk_r.next()
                        E.act(junk3[:, 0:384], pg[:, 0:384], AF.Square, [pgk, stk3 + "z"], [jk3, stk3 + "a"], accum=stt3[:, 0:1])
                        rs3, rs3k = rstd_from(stt3[:, 0:1], 384, stt3, stk3)
                        cqn, cqnk = cqn_r.next()
                        E.stt(cqn[:], pg[:, 0:384], rs3, gq[:], ALU.mult, ALU.mult, [pgk, rs3k, "gq"], [cqnk])
                        pg, pgk = pG_r.next()
                        for c in range(8):
                            E.mm(pg[:, :], hT[:, c, :], win[:, c, CFQ:CFQ + 512], c == 0, c == 7, [hTk, "win"], [pgk])
                        fqb, fqbk = fqb_r.next()
                        E.cp("act", fqb[:], pg[:, :], [pgk], [fqbk])
                t0 = kb * 128
                pg, pgk = pG_r.next()
                E.mm(pg[:, 0:8], tri[:, tri_i, :], lf32[:], True, True, ["tri", lf32k], [pgk])
                E.mm(pg[:, 8:16], ones3[:, tri_i, :], lf32[:], True, True, ["ones3", lf32k], [pgk])
                cum32, cum32k = cum32_r.next()
                E.tt("dve", cum32[:], pg[:, 0:8], carry[:], ALU.add, [pgk, "carry"], [cum32k])
                E.tt("dve", carry[:], pg[:, 8:16], carry[:], ALU.add, [pgk, "carry"], ["carry"])
                E.ts("dve", ncum[:, kb, :], cum32[:], -1.0, None, ALU.mult, None, [cum32k], ["ncum"])
                pT, pTk = pT_r.next(); latT, latTk = latT_r.next()
                E.tr(pT[:, 0:128], lkb[:, 0:128], identb[:], [lkbk + "l", "identb"], [pTk])
                E.tr(pT[:, 128:256], lkb[:, 128:256], identb[:], [lkbk + "l", "identb"], [pTk])
                E.tr(pT[:, 256:384], lkb[:, 224:352], identb[:], [lkbk + "l", lkbk + "r", lkbk + "p", "identb"], [pTk])
                E.cp("dve", latT[:, 0:2, :].rearrange("p c t -> p (c t)"), pT[:, 0:256], [pTk], [latTk + "l"])
                E.cp("dve", latT[96:128, 2, :], pT[96:128, 256:384], [pTk], [latTk + "r"])
                E.dma(STQ, job.KTr[:, t0:t0 + 128], latT[96:128, 2, :], [latTk + "r"], ["dram_KTr_" + jn], latTk + "ro")
                ktn, ktnk = ktn_r.next()
                for half in range(2):
                    pg, pgk = pG_r.next()
                    pgv = pg[0:64, :].rearrange("p (h t) -> p h t", h=4)
                    for hh in range(4):
                        h = half * 4 + hh
                        for c in range(2):
                            E.mm(pgv[:, hh, :], wuk[:, c, h, :], latT[:, c, :], c == 0, c == 1, ["wuk", latTk + "l"], [pgk])
                    E.cp("act" if half == 0 else "dve", ktn[:, half * 4:(half + 1) * 4, :], pgv, [pgk], [ktnk + str(half)])
                E.dma(STQ, job.KTn[:, :, t0:t0 + 128].rearrange("h p t -> p h t"), ktn[:], [ktnk + "0", ktnk + "1"], ["dram_KTn_" + jn], ktnk + "o")
                pg, pgk = pG_r.next()
                for c in range(2):
                    E.mm(pg[:, :], latT[:, c, :], wuv[:, c].rearrange("p h d -> p (h d)"), c == 0, c == 1, [latTk + "l", "wuv"], [pgk])
                vmb, vmbk = vmb_r.next()
                E.cp("act", vmb[:], pg[:, :], [pgk], [vmbk])
                E.dma(STQ, job.Vm[:, :, kb, :].rearrange("h p d -> p h d"), vmb[:].rearrange("p (h d) -> p h d", h=8), [vmbk], ["dram_Vm_" + jn], vmbk + "o")
                E.dma(STQ, job.Vf[:, :, kb, :].rearrange("h p d -> p h d"), vfb[:].rearrange("p (h d) -> p h d", h=8), [vfbk], ["dram_Vf_" + jn], vfbk + "o")
                pT, pTk = pT_r.next(); ktf, ktfk = ktf_r.next()
                for h in range(8):
                    E.tr(pT[0:64, h * 128:(h + 1) * 128], fkb[:, h * 64:(h + 1) * 64], identb[:], [fkbk, "identb"], [pTk])
                E.cp("dve", ktf[:].rearrange("p h t -> p (h t)"), pT[0:64, :], [pTk], [ktfk])
                E.dma(STQ, job.KTf[:, :, t0:t0 + 128].rearrange("h p t -> p h t"), ktf[:], [ktfk], ["dram_KTf_" + jn], ktfk + "o")
                if hasq:
                    q0 = qblk * 128
                    pT, pTk = pT_r.next(); cqT, cqTk = cqT_r.next()
                    for c in range(3):
                        E.tr(pT[:, c * 128:(c + 1) * 128], cqn[:, c * 128:(c + 1) * 128], identb[:], [cqnk, "identb"], [pTk])
                    E.cp("act", cqT[:].rearrange("p c t -> p (c t)"), pT[:, 0:384], [pTk], [cqTk])
                    qb, qbk = qb_r.next(); qrt, qrtk = qrt_r.next()
                    for half in range(2):
                        pg, pgk = pG_r.next()
                        for c in range(3):
                            E.mm(pg[:, 0:384], cqT[:, c, :], wuq[:, c, half * 384:(half + 1) * 384], c == 0, c == 2, [cqTk, "wuq"], [pgk])
                        pv = pg[:, 0:384].rearrange("p (h d) -> p h d", h=4)
                        qv = qb[:, half * 384:(half + 1) * 384].rearrange("p (h d) -> p h d", h=4)
                        E.cp("dve", qv[:, :, 0:64], pv[:, :, 0:64], [pgk], [qbk + "n%d" % half])
                        tb = rtab.unsqueeze(1)
                        tmp = qrt[:, half * 4:(half + 1) * 4, :]
                        tk = qrtk + str(half)
                        E.tt("dve", tmp[:, :, 0:32], pv[:, :, 64:96], tb[:, :, 0:32].broadcast_to([128, 4, 32]), ALU.mult, [pgk, rtab_k], [tk + "a"])
                        E.tt("dve", tmp[:, :, 32:48], pv[:, :, 80:96], tb[:, :, 32:48].broadcast_to([128, 4, 16]), ALU.mult, [pgk, rtab_k], [tk + "b"])
                        E.tt("dve", tmp[:, :, 48:64], pv[:, :, 64:80], tb[:, :, 48:64].broadcast_to([128, 4, 16]), ALU.mult, [pgk, rtab_k], [tk + "c"])
                        E.tt("dve", qv[:, :, 64:96], tmp[:, :, 0:32], tmp[:, :, 32:64], ALU.add, [tk + "a", tk + "b", tk + "c"], [qbk + "r%d" % half])
                    qkeys = [qbk + "n0", qbk + "n1", qbk + "r0", qbk + "r1"]
                    pT, pTk = pT_r.next(); qtm, qtmk = qtm_r.next()
                    for h in range(8):
                        E.tr(pT[0:96, h * 128:(h + 1) * 128], qb[:, h * 96:(h + 1) * 96], identb[:], qkeys + ["identb"], [pTk])
                    E.cp("dve", qtm[:].rearrange("p h t -> p (h t)"), pT[0:96, :], [pTk], [qtmk])
                    E.dma(STQ, job.QTm[:, :, q0:q0 + 128].rearrange("h p t -> p h t"), qtm[:], [qtmk], ["dram_QTm_" + jn], qtmk + "o")
                    pT, pTk = pT_r.next(); qtf, qtfk = qtf_r.next()
                    for h in range(8):
                        E.tr(pT[0:64, h * 128:(h + 1) * 128], fqb[:, h * 64:(h + 1) * 64], identb[:], [fqbk, "identb"], [pTk])
                    E.cp("act", qtf[:].rearrange("p h t -> p (h t)"), pT[0:64, :], [pTk], [qtfk])
                    E.dma(STQ, job.QTf[:, 0:64, q0:q0 + 128].rearrange("h p t -> p h t"), qtf[:], [qtfk], ["dram_QTf_" + jn], qtfk + "o")
                    cum8, cum8k = cum8_r.next(); cumT, cumTk = cumT_r.next()
                    E.ts("dve", cum8[:], cum32[:], 8.0, None, ALU.mult, None, [cum32k], [cum8k])
                    pT, pTk = pT_r.next()
                    E.tr(pT[0:8, 0:128], cum8[:, 0:8], identb[:], [cum8k, "identb"], [pTk])
                    E.cp("dve", cumT[:], pT[0:8, 0:128], [pTk], [cumTk])
                    E.dma(STQ, job.QTf[:, 64, q0:q0 + 128], cumT[:], [cumTk], ["dram_QTf_" + jn], cumTk + "o")
                    qblk += 1
                kb += 1
            E.dma(STQ, job.NCUM[:, :, :], ncum[:, 0:job.nbk, :], ["ncum"], [], "ncum_o")
        if 'A' in PH:
            _tr = int(os.environ.get('K_ATRUNC', '0'))
            if _tr:
                print('PHASE A n_ins', len(P.ins)); P.ins = P.ins[:_tr]
            P.emit(nc)

    with ExitStack() as st:
        sb = lambda name, shape, dt: st.enter_context(nc.sbuf_tensor("B_" + name, shape, dt))
        psb = lambda name, shape, dt: st.enter_context(nc.psum_tensor("B_" + name, shape, dt))
        P = Prog(); E = Em(P)
        identf = sb("identf", [128, 128], F32); identb = sb("identb", [128, 128], BF16)
        maskf = sb("maskf", [128, 2, 128], F32); maskb = sb("maskb", [128, 2, 128], BF16)
        onesr = sb("onesr", [65, 64], F32)
        E.dma("sp", identf[:], c_ident[:, :], [], ["identf"], "w_id")
        E.cp("dve", identb[:], identf[:], ["identf"], ["identb"])
        E.dma("sp", maskf[:], c_mask.rearrange("a p n -> p a n"), [], ["maskf"], "w_mask")
        E.cp("dve", maskb[:], maskf[:], ["maskf"], ["maskb"])
        E.memset("dve", onesr[:], 1.0, ["onesr"])
        ktm_r = Ring(sb, "ktm", [96, LKMAX], BF16, 2)
        ktf_r = Ring(sb, "ktf", [96, LKMAX], BF16, 2)
        v_r = Ring(sb, "vv", [128, NBKMAX, 128], BF16, 2)
        qm_r = Ring(sb, "qm", [96, LQMAX], BF16, 2)
        qf_r = Ring(sb, "qf", [96, LQMAX], BF16, 2)
        pt_r = Ring(sb, "pt", [128, 512], BF16, 5)
        osb_r = Ring(sb, "osb", [65, 512], F32, 2)
        rec_r = Ring(sb, "rec", [65, 512], F32, 2)
        mixo_r = Ring(sb, "mixo", [64, 512], BF16, 2)
        pS_r = Ring(psb, "pS", [128, 512], F32, 5)
        pO_r = Ring(psb, "pO", [128, 512], F32, 2)
        for i in range(2):
            E.memset("dve", ktf_r.t[i][64:96, :], 0.0, [ktf_r.k[i] + "1"])
            E.memset("dve", ktf_r.t[i][64:65, :], 1.0, [ktf_r.k[i] + "1"])
            E.memset("dve", qf_r.t[i][64:96, :], 0.0, [qf_r.k[i] + "z", qf_r.k[i]])
            E.memset("dve", v_r.t[i][:, :, 65:128], 0.0, [v_r.k[i] + "1"])

        ncum_r = Ring(sb, "ncumr", [128, NBKMAX, 8], F32, 2)

        def load_head(job, hd):
            nbk = job.nbk; lk = nbk * 128; lq = job.lq
            fox = hd >= 8
            h = hd % 8
            vt, vk = v_r.next()
            vkeys = []
            if fox:
                kt, ktk = ktf_r.next(); qt, qtk = qf_r.next(); KD = 96
                E.dma("sp", kt[0:64, 0:lk], job.KTf[h], [], [ktk], ktk)
                E.dma("sp", qt[0:65, 0:lq], job.QTf[h], [], [qtk], qtk)
                vsrc = job.Vf
                kr_keys = [ktk, ktk + "1", qtk + "z"]
                scale = FOX_SCALE
            else:
                kt, ktk = ktm_r.next(); qt, qtk = qm_r.next(); KD = 96
                E.dma("sp", kt[0:64, 0:lk], job.KTn[h], [], [ktk], ktk)
                E.dma("sp", kt[64:96, 0:lk], job.KTr[:, :], [], [ktk + "r"], ktk + "r")
                E.dma("sp", qt[0:96, 0:lq], job.QTm[h], [], [qtk], qtk)
                vsrc = job.Vm
                kr_keys = [ktk, ktk + "r"]
                scale = MLA_SCALE
            for b0_ in range(0, nbk, 16):
                b1_ = min(nbk, b0_ + 16)
                E.dma("sp", vt[:, b0_:b1_, 0:64], vsrc[h, :, b0_:b1_, :], [], [vk + "c%d" % b0_], vk + "_%d" % b0_)
                vkeys.append(vk + "c%d" % b0_)
            return dict(fox=fox, h=h, vt=vt, vk=vk, vkeys=vkeys, kt=kt, ktk=ktk, qt=qt, qtk=qtk, KD=KD, kr_keys=kr_keys, scale=scale)

        RS = dscr("RS_scr", [4, 512], F32)
        rs_cnt = [0]
        items = [(job, hd) for job in jobs for hd in range(16)]
        nxt_loaded = load_head(*items[0])
        for it_i, (job, hd) in enumerate(items):
            nbk = job.nbk; lk = nbk * 128; lq = job.lq; QW = job.qw
            if hd == 0:
                ncum, ncumk = ncum_r.next()
                E.dma("sp", ncum[:, 0:nbk, :], job.NCUM[:, :, :], [], [ncumk], ncumk)
                for i in range(2):
                    vt, vk = v_r.t[i], v_r.k[i]
                    E.memset("dve", vt[:, :, 64:65], 1.0, [vk + "1"])
                    pb_ = job.nbc if job.name != "p" else 0
                    nvb = job.blocks[0]["nv"]
                    E.memset("dve", vt[:, pb_, 64:65], 0.0, [vk + "1"])
                    E.memset("dve", vt[0:nvb, pb_, 64:65], 1.0, [vk + "1"])
            L_ = nxt_loaded
            if it_i + 1 < len(items):
                nxt_loaded = load_head(*items[it_i + 1])
            fox = L_["fox"]; h = L_["h"]; vt = L_["vt"]; vk = L_["vk"]; kt = L_["kt"]; ktk = L_["ktk"]
            qt = L_["qt"]; qtk = L_["qtk"]; KD = L_["KD"]; kr_keys = L_["kr_keys"]; scale = L_["scale"]; vkeys = L_["vkeys"]
            if True:
                nqt = lq // QW
                for t in range(nqt):
                    q0 = t * QW
                    if job.name == "p":
                        nfull = 1 + t * (QW // 128)
                        kbl = [(j, 0, None) for j in range(nfull)]
                        for jj in range(QW // 128):
                            kbl.append((nfull + jj, jj * 128, 0 if fox else 1))
                    else:
                        kbl = [(j, 0, None) for j in range(job.nbc)]
                        kbl.append((job.nbc, 0, 0 if fox else None))
                    pO, pOk = pO_r.next()
                    pend = []

                    def qk(idx):
                        j, c0, mk = kbl[idx]
                        pS, pSk = pS_r.next()
                        E.mm(pS[:, c0:QW], kt[0:KD, j * 128:(j + 1) * 128], qt[0:KD, q0 + c0:q0 + QW], True, mk is None, kr_keys + [qtk], [pSk])
                        if mk is not None:
                            E.mm(pS[:, c0:c0 + 128], identb[:, :], maskb[:, mk, :], False, True, ["identb", "maskb"], [pSk])
                        return pS, pSk

                    LA = 2
                    qq = [qk(i_) for i_ in range(min(LA, len(kbl)))]
                    for idx in range(len(kbl)):
                        j, c0, mk = kbl[idx]
                        pS, pSk = qq.pop(0)
                        if idx + LA < len(kbl):
                            qq.append(qk(idx + LA))
                        pt, ptk = pt_r.next()
                        if fox:
                            E.act(pt[:, c0:QW], pS[:, c0:QW], AF.Exp, [pSk, ncumk], [ptk], bias=ncum[:, j, h:h + 1], scale=scale)
                        else:
                            E.act(pt[:, c0:QW], pS[:, c0:QW], AF.Exp, [pSk], [ptk], scale=scale)
                        E.mm(pO[:, c0:QW], vt[:, j, :], pt[:, c0:QW], idx == 0, idx == len(kbl) - 1, vkeys + [vk + "1", ptk], [pOk])
                    osb, osbk = osb_r.next(); rec, reck = rec_r.next(); mixo, mixok = mixo_r.next()
                    E.cp("dve", osb[0:65, 0:QW], pO[0:65, 0:QW], [pOk], [osbk])
                    rsl = rs_cnt[0] % 4
                    rs_cnt[0] += 1
                    E.dma("sp", RS[rsl:rsl + 1, 0:QW], osb[64:65, 0:QW], [osbk], ["RS%d" % rsl], "rs_w%d" % rsl)
                    E.dma("sp", rec[0:64, 0:QW], RS[rsl:rsl + 1, 0:QW].partition_broadcast(64), ["RS%d" % rsl], [reck], "rs_r%d" % rsl)
                    E.P.op("dve", (lambda o, i: (lambda e: e.reciprocal(out=o, in_=i)))(rec[0:64, 0:QW], rec[0:64, 0:QW]), [reck], [reck])
                    E.tt("dve", mixo[0:64, 0:QW], osb[0:64, 0:QW], rec[0:64, 0:QW], ALU.mult, [osbk, reck], [mixok])
                    E.dma("sp", job.MIX[hd * 64:(hd + 1) * 64, q0:q0 + QW], mixo[0:64, 0:QW], [mixok], [], mixok + "o")
        if 'B' in PH:
            P.emit(nc)

    with ExitStack() as st:
        sb = lambda name, shape, dt: st.enter_context(nc.sbuf_tensor("C_" + name, shape, dt))
        psb = lambda name, shape, dt: st.enter_context(nc.psum_tensor("C_" + name, shape, dt))
        P = Prog(); E = Em(P)
        NFC = DFF // 128
        wo = sb("wo", [128, 8, D], BF16)
        wg = sb("wg", [128, 8, DFF], BF16)
        wu = sb("wu", [128, 8, DFF], BF16)
        wd = sb("wd", [128, NFC, D], BF16)
        identf = sb("identf", [128, 128], F32); identb = sb("identb", [128, 128], BF16)
        gffn = sb("gffn", [128, D], F32); gfin = sb("gfin", [128, D], F32)
        E.dma("sp", identf[:], c_ident[:, :], [], ["identf"], "w_id")
        E.cp("dve", identb[:], identf[:], ["identf"], ["identb"])
        E.dma("sp", gffn[:], g_ffn.partition_broadcast(128), [], ["gffn"], "w_gffn")
        E.dma("sp", gfin[:], g_fin.partition_broadcast(128), [], ["gfin"], "w_gfin")
        TWMAX = 256
        mix_r = Ring(sb, "mixt", [128, 8, TWMAX], BF16, 1)
        xl_r = Ring(sb, "xl", [128, D], F32, 2)
        _ce = [0]

        def wloadc(dst_ap, src_ap, wkey, ncols):
            stg, sk = xl_r.next()
            E.dma("sp", stg[:, 0:ncols], src_ap, [], [sk], sk)
            eng = "act" if _ce[0] % 2 == 0 else "dve"
            _ce[0] += 1
            E.cp(eng, dst_ap, stg[:, 0:ncols], [sk], [wkey])
        for c in range(8):
            wloadc(wo[:, c, :], w_out[c * 128:(c + 1) * 128, :], "wo", 1024)
        for c in range(8):
            for (c0, cw) in ((0, 1024), (1024, 1024), (2048, 768)):
                wloadc(wg[:, c, c0:c0 + cw], w_gate[c * 128:(c + 1) * 128, c0:c0 + cw], "wg", cw)
                wloadc(wu[:, c, c0:c0 + cw], w_up[c * 128:(c + 1) * 128, c0:c0 + cw], "wu", cw)
        for c in range(NFC):
            wloadc(wd[:, c, :], w_down[c * 128:(c + 1) * 128, :], "wd", 1024)
        x2_r = Ring(sb, "x2", [128, 2, D], F32, 1)
        h2_r = Ring(sb, "h2", [128, D], BF16, 2)
        h2T_r = Ring(sb, "h2T", [128, 8, TWMAX], BF16, 1)
        actT_r = Ring(sb, "actT", [128, NFC, TWMAX], BF16, 1)
        sg_r = Ring(sb, "sg", [128, TWMAX], F32, 2)
        junk_r = Ring(sb, "junkc", [128, D], BF16, 1)
        st_r = Ring(sb, "statc", [128, 8], F32, 4)
        for _i in range(4):
            E.memset("dve", st_r.t[_i][:], 0.0, [st_r.k[_i] + "z"])
        pA_r = Ring(psb, "pA", [128, 512], F32, 4)
        pF_r = Ring(psb, "pF", [128, 512], F32, 3)
        pT_r = Ring(psb, "pTc", [128, 1024], BF16, 1)

        def rstd_c(ss_ap, n, stt, stk):
            E.cp("act", stt[:, 6:7], stt[:, 7:8], [stk + "a", stk + "z"], [stk + "a2"])
            E.act(stt[:, 1:2], ss_ap, AF.Ln, [stk + "a", stk + "a2"], [stk + "b"], bias=EPS, scale=1.0 / n)
            E.act(stt[:, 2:3], stt[:, 1:2], AF.Exp, [stk + "b"], [stk + "c"], scale=-0.5)
            return stt[:, 2:3], stk + "c"

        for job in jobs:
            lq = job.lq
            TW = 256 if lq % 256 == 0 else 128
            nsub = TW // 128
            for t in range(lq // TW):
                q0 = t * TW
                mixt, mixk = mix_r.next()
                E.dma("sp", mixt[:, :, 0:TW], job.MIX[:, q0:q0 + TW].rearrange("(c p) t -> p c t", p=128), [], [mixk], mixk)
                x2, x2k = x2_r.next(); h2T, h2Tk = h2T_r.next()
                for s in range(nsub):
                    r0 = q0 + s * 128
                    if job.nvq == 128:
                        xl, xlk = xl_r.next()
                        E.dma("sp", xl[:], job.xq[r0:r0 + 128, :], [], [xlk], xlk)
                    else:
                        xl, xlk = xl_r.next()
                        E.dma("sp", xl[0:job.nvq, :], job.xq[0:job.nvq, :], [], [xlk], xlk)
                    pa0, pa0k = pA_r.next(); pa1, pa1k = pA_r.next()
                    for c in range(8):
                        E.mm(pa0[:, :], mixt[:, c, s * 128:(s + 1) * 128], wo[:, c, 0:512], c == 0, c == 7, [mixk, "wo"], [pa0k])
                        E.mm(pa1[:, :], mixt[:, c, s * 128:(s + 1) * 128], wo[:, c, 512:1024], c == 0, c == 7, [mixk, "wo"], [pa1k])
                    E.tt("dve", x2[:, s, 0:512], pa0[:, :], xl[:, 0:512], ALU.add, [pa0k, xlk], [x2k + "a%d" % s])
                    E.tt("dve", x2[:, s, 512:1024], pa1[:, :], xl[:, 512:1024], ALU.add, [pa1k, xlk], [x2k + "b%d" % s])
                    xk2 = [x2k + "a%d" % s, x2k + "b%d" % s]
                    junk, jk = junk_r.next(); stt, stk = st_r.next()
                    E.act(junk[:], x2[:, s, :], AF.Square, xk2 + [stk + "z"], [jk, stk + "a"], accum=stt[:, 0:1])
                    rs, rsk = rstd_c(stt[:, 0:1], D, stt, stk)
                    h2, h2k = h2_r.next()
                    E.stt(h2[:], x2[:, s, :], rs, gffn[:], ALU.mult, ALU.mult, xk2 + [rsk, "gffn"], [h2k])
                    pT, pTk = pT_r.next()
                    for c in range(8):
                        E.tr(pT[:, c * 128:(c + 1) * 128], h2[:, c * 128:(c + 1) * 128], identb[:], [h2k, "identb"], [pTk])
                    E.cp("act", h2T[:, :, s * 128:(s + 1) * 128], pT[:, :].rearrange("p (c t) -> p c t", c=8), [pTk], [h2Tk + str(s)])
                h2keys = [h2Tk + str(s) for s in range(nsub)]
                actT, actTk = actT_r.next()
                for f in range(NFC):
                    pgt, pgtk = pF_r.next(); put, putk = pF_r.next()
                    for c in range(8):
                        E.mm(pgt[:, 0:TW], wg[:, c, f * 128:(f + 1) * 128], h2T[:, c, 0:TW], c == 0, c == 7, ["wg"] + h2keys, [pgtk])
                    for c in range(8):
                        E.mm(put[:, 0:TW], wu[:, c, f * 128:(f + 1) * 128], h2T[:, c, 0:TW], c == 0, c == 7, ["wu"] + h2keys, [putk])
                    sg, sgk = sg_r.next()
                    E.act(sg[:, 0:TW], pgt[:, 0:TW], AF.Silu, [pgtk], [sgk])
                    E.tt("dve", actT[:, f, 0:TW], sg[:, 0:TW], put[:, 0:TW], ALU.mult, [sgk, putk], [actTk + "_%d" % f])
                akeys = [actTk + "_%d" % f for f in range(NFC)]
                for s in range(nsub):
                    r0 = q0 + s * 128
                    pa0, pa0k = pA_r.next(); pa1, pa1k = pA_r.next()
                    for f in range(NFC):
                        E.mm(pa0[:, :], actT[:, f, s * 128:(s + 1) * 128], wd[:, f, 0:512], f == 0, f == NFC - 1, akeys + ["wd"], [pa0k])
                        E.mm(pa1[:, :], actT[:, f, s * 128:(s + 1) * 128], wd[:, f, 512:1024], f == 0, f == NFC - 1, akeys + ["wd"], [pa1k])
                    xk2 = [x2k + "a%d" % s, x2k + "b%d" % s]
                    E.tt("dve", x2[:, s, 0:512], pa0[:, :], x2[:, s, 0:512], ALU.add, [pa0k] + xk2, [x2k + "a%d" % s])
                    E.tt("dve", x2[:, s, 512:1024], pa1[:, :], x2[:, s, 512:1024], ALU.add, [pa1k] + xk2, [x2k + "b%d" % s])
                    junk, jk = junk_r.next(); stt, stk = st_r.next()
                    E.act(junk[:], x2[:, s, :], AF.Square, xk2 + [stk + "z"], [jk, stk + "a"], accum=stt[:, 0:1])
                    rs, rsk = rstd_c(stt[:, 0:1], D, stt, stk)
                    E.stt(x2[:, s, :], x2[:, s, :], rs, gfin[:], ALU.mult, ALU.mult, xk2 + [rsk, "gfin"], xk2)
                    nv = job.nvq
                    E.dma("sp", job.y[r0:r0 + nv, :], x2[0:nv, s, :], xk2, [], x2k + "o%d" % s)
        if 'C' in PH:
            P.emit(nc)
    return nc


def make_consts(S, PAST):
    NBF = S // 128
    ident = np.eye(128, dtype=np.float32)
    jj = np.arange(128)[:, None]; tt = np.arange(128)[None, :]
    U = (jj <= tt).astype(np.float32)
    tri = np.stack([U, U * (jj < 16), U * (jj < 32)]).astype(np.float32)
    on = np.ones((128, 128), np.float32)
    ones = np.stack([on, on * (jj < 16), on * (jj < 32)]).astype(np.float32)
    m_fox = np.where(jj <= tt, 0.0, NEG)
    m_mla = np.where((jj // 64) <= (tt // 64), 0.0, NEG)
    mask = np.stack([m_fox, m_mla]).astype(np.float32)
    half = 16
    inv = (10000.0 ** (-np.arange(half, dtype=np.float32) / half)).astype(np.float32)

    def tab(pos):
        ang = pos.astype(np.float32)[..., None] * inv
        c = np.cos(ang).astype(np.float32); s = np.sin(ang).astype(np.float32)
        return np.concatenate([c, c, -s, s], axis=-1).astype(np.float32)
    p = np.arange(128)[:, None]
    b = np.arange(NBF + 1)[None, :]
    posp = np.where(b == 0, p, NMETA + (b - 1) * 128 + p)
    ropep = tab(posp)
    ropes = tab((PAST + np.arange(128))[:, None])
    return dict(c_ident=ident, c_tri=tri, c_ones=ones, c_mask=mask, c_ropep=ropep, c_ropes=ropes)


def make_in_maps(inp, ncores, S, PAST, NS):
    consts = make_consts(S, PAST)
    f = lambda a: np.ascontiguousarray(np.asarray(a, dtype=np.float32))
    shared = dict(
        meta=f(inp["meta_tokens"]), w_in=f(inp["w_in"][0]), w_uq=f(inp["w_mla_uq"][0]), w_ukv=f(inp["w_mla_ukv"][0]),
        w_out=f(inp["w_out"][0]), w_gate=f(inp["w_ffn_gate"][0]), w_up=f(inp["w_ffn_up"][0]), w_down=f(inp["w_ffn_down"][0]),
        g_mix=f(inp["norm_mix"][0]).reshape(1, -1), b_forget=f(inp["b_forget"][0]).reshape(1, -1),
        g_q=f(inp["mla_q_norm"][0]).reshape(1, -1), g_kv=f(inp["mla_kv_norm"][0]).reshape(1, -1),
        g_ffn=f(inp["norm_ffn"][0]).reshape(1, -1), g_fin=f(inp["norm_final"]).reshape(1, -1), **consts)
    maps = []
    for c in range(ncores):
        m = dict(shared)
        m["xp"] = f(inp["x_prompt"][c])
        sl = slice(c * NS, (c + 1) * NS)
        m["xs"] = f(inp["x_sample"][sl])
        m["c_lat"] = f(inp["cache_mla_latent"][0, sl]); m["c_kr"] = f(inp["cache_mla_krope"][0, sl])
        m["c_fk"] = f(inp["cache_fox_k"][0, sl]).reshape(NS, PAST, 512); m["c_fv"] = f(inp["cache_fox_v"][0, sl]).reshape(NS, PAST, 512)
        m["c_lf"] = f(inp["cache_fox_logf"][0, sl])
        maps.append(m)
    return maps


def assemble(results, ncores, S, NS):
    LP = NMETA + S
    cat = lambda k: np.concatenate([np.asarray(r[k], dtype=np.float32)[None] for r in results], axis=0)
    y_p = cat("y_p")
    y_s = cat("y_s").reshape(ncores * NS, 32, D)
    lat_p = cat("lat_p")[None]; kr_p = cat("kr_p")[None]
    fk_p = cat("fk_p").reshape(1, ncores, LP, 8, 64); fv_p = cat("fv_p").reshape(1, ncores, LP, 8, 64)
    lf_p = cat("lf_p")[None]
    lat_s = cat("lat_s").reshape(1, ncores * NS, 32, 256); kr_s = cat("kr_s").reshape(1, ncores * NS, 32, 32)
    fk_s = cat("fk_s").reshape(1, ncores * NS, 32, 8, 64); fv_s = cat("fv_s").reshape(1, ncores * NS, 32, 8, 64)
    lf_s = cat("lf_s").reshape(1, ncores * NS, 32, 8)
    return (y_p, y_s, lat_p, kr_p, fk_p, fv_p, lf_p, lat_s, kr_s, fk_s, fv_s, lf_s)


def kernel(**inp):
    ncores = 8
    B, S, _ = inp["x_prompt"].shape
    PAST = inp["cache_mla_latent"].shape[2]
    NS = inp["x_sample"].shape[0] // ncores
    assert B == ncores
    nc = build(S, PAST, NS)
    maps = make_in_maps(inp, ncores, S, PAST, NS)
    res = run_bass_kernel_spmd(nc, maps, core_ids=list(range(ncores)))
    return assemble(res.results, ncores, S, NS)
```

```python
from contextlib import ExitStack
import numpy as np
import concourse.bass as bass
import concourse.mybir as mybir
from concourse.bass_utils import run_bass_kernel_spmd

F32 = mybir.dt.float32
BF16 = mybir.dt.bfloat16
AF = mybir.ActivationFunctionType
ALU = mybir.AluOpType

D = 1024
DIN = 2216
DFF = 2816
NMETA = 16
EPS = 1e-6
MLA_SCALE = 96 ** -0.5
FOX_SCALE = 0.125
NEG = -30000.0
COMPUTE = ("pe", "act", "dve", "pool")


class Prog:
    uid = 0

    def __init__(self):
        self.ins = []

    def op(self, eng, fn, r=(), w=()):
        self.ins.append((eng, fn, tuple(r), tuple(w), None))

    def dma(self, eng, fn, r=(), w=(), tag=None):
        assert tag is not None
        self.ins.append((eng, fn, tuple(r), tuple(w), tag))

    def analyze(self):
        ins = self.ins
        n = len(ins)
        last_w, readers = {}, {}
        need = [False] * n
        waits = [None] * n
        for i, (eng, fn, R, W, tag) in enumerate(ins):
            d = {}
            for k in R:
                j = last_w.get(k)
                if j is not None:
                    d[j] = True
            for k in W:
                j = last_w.get(k)
                if j is not None and j not in d:
                    d[j] = d.get(j, False)
                for j in readers.get(k, ()):
                    if j != i and j not in d:
                        d[j] = False
            for k in R:
                readers.setdefault(k, []).append(i)
            for k in W:
                last_w[k] = i
                readers[k] = []
            wl = []
            for j, raw in d.items():
                ej, tj = ins[j][0], ins[j][4]
                if tj is None:
                    if ej == "pe" and eng == "pe":
                        continue
                    need[j] = True
                wl.append(j)
            waits[i] = wl
        cnt, val, semkey = {}, [0] * n, [None] * n
        for i in range(n):
            eng, tag = ins[i][0], ins[i][4]
            if tag is not None:
                key = ("d", tag)
                cnt[key] = cnt.get(key, 0) + 16
            elif need[i]:
                key = ("e", eng)
                cnt[key] = cnt.get(key, 0) + 1
            else:
                continue
            val[i] = cnt[key]
            semkey[i] = key
        self.need, self.waits, self.val, self.semkey = need, waits, val, semkey
        return cnt

    def emit(self, nc, final_tags=None):
        cnt = self.analyze()
        ins = self.ins
        with nc.cleanup_on_exit():
          with ExitStack() as st:
            sems = {}
            for idx, k in enumerate(cnt.keys()):
                sems[k] = nc.alloc_semaphore(name="s%d_%d_%s" % (Prog.uid, idx, str(k[1])[:12]))
            Prog.uid += 1
            block = st.enter_context(nc.Block())
            per_eng = {}
            for i, rec in enumerate(ins):
                per_eng.setdefault(rec[0], []).append(i)
            if final_tags is None:
                final_tags = [k[1] for k in cnt if k[0] == "d"]

            def run(name, e):
                waited = {}
                for i in per_eng.get(name, []):
                    fn, tag = ins[i][1], ins[i][4]
                    req = {}
                    for j in self.waits[i]:
                        k, v = self.semkey[j], self.val[j]
                        if v > req.get(k, 0):
                            req[k] = v
                    for k, v in req.items():
                        if waited.get(k, 0) >= v:
                            continue
                        e.wait_ge(sems[k], v)
                        waited[k] = v
                    bi = fn(e)
                    if tag is not None:
                        bi.then_inc(sems[("d", tag)], 16)
                    elif self.need[i]:
                        bi.then_inc(sems[("e", name)], 1)
                if name == "sp":
                    for t in final_tags:
                        e.wait_ge(sems[("d", t)], cnt[("d", t)])

            block.tensor(lambda e: run("pe", e))
            block.scalar(lambda e: run("act", e))
            block.vector(lambda e: run("dve", e))
            block.gpsimd(lambda e: run("pool", e))
            block.sync(lambda e: run("sp", e))
          nc.all_engine_barrier()


class Ring:
    def __init__(self, alloc, name, shape, dt, n):
        self.t = [alloc("%s%d" % (name, i), shape, dt) for i in range(n)]
        self.k = ["%s%d" % (name, i) for i in range(n)]
        self.n = n
        self.i = -1

    def next(self):
        self.i += 1
        s = self.i % self.n
        return self.t[s], self.k[s]


class Em:
    def __init__(self, P):
        self.P = P

    def mm(self, out, lhsT, rhs, start, stop, r, w):
        self.P.op("pe", lambda e: e.matmul(out, lhsT=lhsT, rhs=rhs, start=start, stop=stop), r, w)

    def tr(self, out, in_, ident, r, w):
        self.P.op("pe", lambda e: e.transpose(out=out, in_=in_, identity=ident), r, w)

    def act(self, out, in_, func, r, w, bias=None, scale=None, accum=None):
        kw = {}
        if bias is not None:
            kw["bias"] = bias
        if scale is not None:
            kw["scale"] = scale
        if accum is not None:
            kw["accum_out"] = accum
        self.P.op("act", lambda e: e.activation(out=out, in_=in_, func=func, **kw), r, w)

    def cp(self, eng, out, in_, r, w):
        if eng == "act":
            self.P.op("act", lambda e: e.copy(out=out, in_=in_), r, w)
        else:
            self.P.op(eng, lambda e: e.tensor_copy(out=out, in_=in_), r, w)

    def tt(self, eng, out, in0, in1, op, r, w):
        self.P.op(eng, lambda e: e.tensor_tensor(out=out, in0=in0, in1=in1, op=op), r, w)

    def ts(self, eng, out, in0, s1, s2, op0, op1, r, w):
        if s2 is None:
            self.P.op(eng, lambda e: e.tensor_scalar(out=out, in0=in0, scalar1=s1, scalar2=None, op0=op0), r, w)
        else:
            self.P.op(eng, lambda e: e.tensor_scalar(out=out, in0=in0, scalar1=s1, scalar2=s2, op0=op0, op1=op1), r, w)

    def stt(self, out, in0, scalar, in1, op0, op1, r, w):
        self.P.op("dve", lambda e: e.scalar_tensor_tensor(out=out, in0=in0, scalar=scalar, in1=in1, op0=op0, op1=op1), r, w)

    def memset(self, eng, ap, v, w):
        self.P.op(eng, lambda e: e.memset(ap, v), (), w)

    def dma(self, eng, out, in_, r, w, tag):
        self.P.dma(eng, lambda e: e.dma_start(out=out, in_=in_), r, w, tag)


class Job:
    pass


def build(S, PAST, NS):
    import os
    nc = bass.Bass("TRN2", target_bir_lowering=False)
    NBF = S // 128
    QW_P = 512 if S % 512 == 0 else 128
    NBC = PAST // 128
    LP = NMETA + S

    def din(name, shape):
        return nc.dram_tensor(name, list(shape), F32, kind="ExternalInput").ap()

    def dout(name, shape):
        return nc.dram_tensor(name, list(shape), F32, kind="ExternalOutput").ap()

    def dscr(name, shape, dt=BF16):
        return nc.dram_tensor(name, list(shape), dt, kind="Internal").ap()

    xp = din("xp", [S, D]); meta = din("meta", [NMETA, D]); xs = din("xs", [NS, 32, D])
    c_lat = din("c_lat", [NS, PAST, 256]); c_kr = din("c_kr", [NS, PAST, 32])
    c_fk = din("c_fk", [NS, PAST, 512]); c_fv = din("c_fv", [NS, PAST, 512]); c_lf = din("c_lf", [NS, PAST, 8])
    w_in = din("w_in", [D, DIN]); w_uq = din("w_uq", [384, 768]); w_ukv = din("w_ukv", [256, 1024])
    w_out = din("w_out", [D, D]); w_gate = din("w_gate", [D, DFF]); w_up = din("w_up", [D, DFF]); w_down = din("w_down", [DFF, D])
    g_mix = din("g_mix", [1, D]); b_forget = din("b_forget", [1, 8]); g_q = din("g_q", [1, 384]); g_kv = din("g_kv", [1, 256])
    g_ffn = din("g_ffn", [1, D]); g_fin = din("g_fin", [1, D])
    c_ident = din("c_ident", [128, 128])
    c_tri = din("c_tri", [3, 128, 128])
    c_ones = din("c_ones", [3, 128, 128])
    c_mask = din("c_mask", [2, 128, 128])
    NPB = NBF + 1
    c_ropep = din("c_ropep", [128, NPB, 64])
    c_ropes = din("c_ropes", [128, 1, 64])

    y_p = dout("y_p", [S, D]); lat_p = dout("lat_p", [LP, 256]); kr_p = dout("kr_p", [LP, 32])
    fk_p = dout("fk_p", [LP, 512]); fv_p = dout("fv_p", [LP, 512]); lf_p = dout("lf_p", [LP, 8])
    y_s = dout("y_s", [NS, 32, D]); lat_s = dout("lat_s", [NS, 32, 256]); kr_s = dout("kr_s", [NS, 32, 32])
    fk_s = dout("fk_s", [NS, 32, 512]); fv_s = dout("fv_s", [NS, 32, 512]); lf_s = dout("lf_s", [NS, 32, 8])

    jobs = []
    jp = Job()
    jp.name = "p"; jp.nbc = 0; jp.cache = None
    jp.blocks = [dict(src=meta, r0=0, nv=NMETA, q=False, tri=1, rope=(c_ropep, 0), orow=0)]
    for b in range(NBF):
        jp.blocks.append(dict(src=xp, r0=b * 128, nv=128, q=True, tri=0, rope=(c_ropep, b + 1), orow=NMETA + b * 128))
    jp.outs = (lat_p, kr_p, fk_p, fv_p, lf_p); jp.y = y_p; jp.xq = xp
    jp.qw = QW_P; jp.lq = S; jp.nbk = NBF + 1; jp.nvq = 128
    jp.mla_mask = 1
    jobs.append(jp)
    for s in range(NS):
        j = Job()
        j.name = "s%d" % s; j.nbc = NBC
        j.cache = (c_lat[s], c_kr[s], c_fk[s], c_fv[s], c_lf[s])
        j.blocks = [dict(src=xs[s], r0=0, nv=32, q=True, tri=2, rope=(c_ropes, 0), orow=0)]
        j.outs = (lat_s[s], kr_s[s], fk_s[s], fv_s[s], lf_s[s]); j.y = y_s[s]; j.xq = xs[s]
        j.qw = 128; j.lq = 128; j.nbk = NBC + 1; j.nvq = 32
        j.mla_mask = None
        jobs.append(j)
    for j in jobs:
        lk = j.nbk * 128
        n = j.name
        j.KTn = dscr("KTn_" + n, [8, 64, lk]); j.KTr = dscr("KTr_" + n, [32, lk]); j.KTf = (nc.dram_tensor("KTf_" + n, [8, 64, lk], BF16, kind="ExternalOutput").ap() if (os.environ.get("K_DBG") and n == "p") else dscr("KTf_" + n, [8, 64, lk]))
        j.Vm = dscr("Vm_" + n, [8, 128, j.nbk, 64]); j.Vf = dscr("Vf_" + n, [8, 128, j.nbk, 64])
        j.QTm = dscr("QTm_" + n, [8, 96, j.lq]); j.QTf = (nc.dram_tensor("QTf_" + n, [8, 65, j.lq], BF16, kind="ExternalOutput").ap() if (os.environ.get("K_DBG") and n == "p") else dscr("QTf_" + n, [8, 65, j.lq]))
        j.MIX = (nc.dram_tensor("MIX_" + n, [D, j.lq], BF16, kind="ExternalOutput").ap() if (os.environ.get("K_DBG") and n == "p") else dscr("MIX_" + n, [D, j.lq])); j.NCUM = dscr("NCUM_" + n, [128, j.nbk, 8], F32)
    LKMAX = max(j.nbk for j in jobs) * 128
    LQMAX = max(j.lq for j in jobs)
    NBKMAX = LKMAX // 128

    import os
    PH = os.environ.get('K_PHASES', 'ABC')
    with ExitStack() as st:
        sb = lambda name, shape, dt: st.enter_context(nc.sbuf_tensor("A_" + name, shape, dt))
        psb = lambda name, shape, dt: st.enter_context(nc.psum_tensor("A_" + name, shape, dt))
        P = Prog(); E = Em(P)
        STQ = os.environ.get('K_STQ', 'sp')
        win = sb("win", [128, 8, DIN], BF16)
        wuq = sb("wuq", [128, 3, 768], BF16)
        wuk = sb("wuk", [128, 2, 8, 64], BF16)
        wuv = sb("wuv", [128, 2, 8, 64], BF16)
        identf = sb("identf", [128, 128], F32); identb = sb("identb", [128, 128], BF16)
        tri = sb("tri", [128, 3, 128], F32); ones3 = sb("ones3", [128, 3, 128], F32)
        gmix = sb("gmix", [128, D], F32); gq = sb("gq", [128, 384], F32); gkv = sb("gkv", [128, 256], F32)
        bfg = sb("bfg", [128, 8], F32)
        ropep = sb("ropep", [128, NPB, 64], F32); ropes = sb("ropes", [128, 1, 64], F32)
        carry = sb("carry", [128, 8], F32)
        ncum = sb("ncum", [128, NBKMAX, 8], F32)
        CQ, CKV, CKR, CLG, CFQ, CFK, CFV = 0, 384, 640, 672, 680, 1192, 1704
        src_off = dict(cq=0, ckv=384, kr=640, fq=672, fk=1184, fv=1696, lg=2208)
        stg_r = Ring(sb, "xt", [128, D], F32, 2)
        _ce = [0]

        def wload(dst_ap, src_ap, wkey, ncols):
            stg, sk = stg_r.next()
            E.dma("sp", stg[:, 0:ncols], src_ap, [], [sk], sk)
            eng = "act" if _ce[0] % 2 == 0 else "dve"
            _ce[0] += 1
            E.cp(eng, dst_ap, stg[:, 0:ncols], [sk], [wkey])
        for c in range(8):
            for (dst, so, wd_) in ((CQ, 0, 384), (CKV, 384, 256), (CKR, 640, 32), (CLG, 2208, 8), (CFQ, 672, 512), (CFK, 1184, 512), (CFV, 1696, 512)):
                wload(win[:, c, dst:dst + wd_], w_in[c * 128:(c + 1) * 128, so:so + wd_], "win", wd_)
        for c in range(3):
            wload(wuq[:, c, :], w_uq[c * 128:(c + 1) * 128, :], "wuq", 768)
        for c in range(2):
            stg, sk = stg_r.next()
            E.dma("sp", stg[:, :], w_ukv[c * 128:(c + 1) * 128, :], [], [sk], sk)
            sv = stg[:, :].rearrange("p (h t d) -> p h t d", h=8, t=2)
            E.cp("act", wuk[:, c], sv[:, :, 0, :], [sk], ["wuk"])
            E.cp("dve", wuv[:, c], sv[:, :, 1, :], [sk], ["wuv"])
        E.dma("sp", identf[:], c_ident[:, :], [], ["identf"], "w_id")
        E.cp("dve", identb[:], identf[:], ["identf"], ["identb"])
        E.dma("sp", tri[:], c_tri.rearrange("a p n -> p a n"), [], ["tri"], "w_tri")
        E.dma("sp", ones3[:], c_ones.rearrange("a p n -> p a n"), [], ["ones3"], "w_ones")
        E.dma("sp", gmix[:], g_mix.partition_broadcast(128), [], ["gmix"], "w_gmix")
        E.dma("sp", gq[:], g_q.partition_broadcast(128), [], ["gq"], "w_gq")
        E.dma("sp", gkv[:], g_kv.partition_broadcast(128), [], ["gkv"], "w_gkv")
        E.dma("sp", bfg[:], b_forget.partition_broadcast(128), [], ["bfg"], "w_bfg")
        E.dma("sp", ropep[:], c_ropep[:, :, :], [], ["ropep"], "w_ropep")
        E.dma("sp", ropes[:], c_ropes[:, :, :], [], ["ropes"], "w_ropes")
        rope_sb = {id(c_ropep): (ropep, "ropep"), id(c_ropes): (ropes, "ropes")}

        xt_r = stg_r
        xpart = sb("xpart", [128, D], F32)
        E.memset("dve", xpart[:], 0.0, ["xpart"])
        junk_r = Ring(sb, "junk", [128, D], BF16, 2)
        st_r = Ring(sb, "stat", [128, 8], F32, 4)
        for _i in range(4):
            E.memset("dve", st_r.t[_i][:], 0.0, [st_r.k[_i] + "z"])
        hb_r = Ring(sb, "hb", [128, D], BF16, 2)
        hT_r = Ring(sb, "hT", [128, 8, 128], BF16, 2)
        lat32_r = Ring(sb, "lat32", [128, 256], F32, 3)
        kr32_r = Ring(sb, "kr32", [128, 32], F32, 3)
        krt_r = Ring(sb, "krt", [128, 64], F32, 2)
        lf32_r = Ring(sb, "lf32", [128, 8], F32, 4)
        lgt_r = Ring(sb, "lgt", [128, 8], F32, 2)
        fk32_r = Ring(sb, "fk32", [128, 512], F32, 3)
        fv32_r = Ring(sb, "fv32", [128, 512], F32, 3)
        lkb_r = Ring(sb, "lkb", [128, 352], BF16, 2)
        for _i in range(2):
            E.memset("dve", lkb_r.t[_i][:, 256:320], 0.0, [lkb_r.k[_i] + "p"])
        fkb_r = Ring(sb, "fkb", [128, 512], BF16, 2)
        vfb_r = Ring(sb, "vfb", [128, 512], BF16, 2)
        vmb_r = Ring(sb, "vmb", [128, 512], BF16, 2)
        cqn_r = Ring(sb, "cqn", [128, 384], BF16, 2)
        fqb_r = Ring(sb, "fqb", [128, 512], BF16, 2)
        cqT_r = Ring(sb, "cqT", [128, 3, 128], BF16, 2)
        qb_r = Ring(sb, "qb", [128, 768], BF16, 2)
        qrt_r = Ring(sb, "qrt", [128, 8, 64], F32, 2)
        latT_r = Ring(sb, "latT", [128, 3, 128], BF16, 2)
        ktn_r = Ring(sb, "ktn", [64, 8, 128], BF16, 2)
        ktf_r = Ring(sb, "ktf", [64, 8, 128], BF16, 2)
        qtm_r = Ring(sb, "qtm", [96, 8, 128], BF16, 2)
        qtf_r = Ring(sb, "qtf", [64, 8, 128], BF16, 2)
        cum32_r = Ring(sb, "cum32", [128, 8], F32, 2)
        cum8_r = Ring(sb, "cum8", [128, 8], BF16, 2)
        cumT_r = Ring(sb, "cumT", [8, 128], BF16, 2)
        pT_r = Ring(psb, "pT", [128, 1024], BF16, 2)
        pG_r = Ring(psb, "pG", [128, 512], F32, 5)

        def rstd_from(ss_ap, n, stt, stk):
            E.cp("act", stt[:, 6:7], stt[:, 7:8], [stk + "a", stk + "z"], [stk + "a2"])
            E.act(stt[:, 1:2], ss_ap, AF.Ln, [stk + "a", stk + "a2"], [stk + "b"], bias=EPS, scale=1.0 / n)
            E.act(stt[:, 2:3], stt[:, 1:2], AF.Exp, [stk + "b"], [stk + "c"], scale=-0.5)
            return stt[:, 2:3], stk + "c"

        def rope32(out32, ok, src, sk, tab, tk, tmp, tmk):
            E.tt("dve", tmp[:, 0:32], src, tab[:, 0:32], ALU.mult, [sk, tk], [tmk + "a"])
            E.tt("dve", tmp[:, 32:48], src[:, 16:32], tab[:, 32:48], ALU.mult, [sk, tk], [tmk + "b"])
            E.tt("dve", tmp[:, 48:64], src[:, 0:16], tab[:, 48:64], ALU.mult, [sk, tk], [tmk + "c"])
            E.tt("dve", out32, tmp[:, 0:32], tmp[:, 32:64], ALU.add, [tmk + "a", tmk + "b", tmk + "c"], [ok])

        for job in jobs:
            lat_o, kr_o, fk_o, fv_o, lf_o = job.outs
            jn = job.name
            E.memset("dve", carry[:], 0.0, ["carry"])
            kb = 0
            qblk = 0
            allblocks = [("c", i) for i in range(min(job.nbc, int(os.environ.get("K_NBC_LIMIT", "9999"))))] + [("n", b) for b in job.blocks]
            def issue_loads(job_, kind_, bi_):
                d = dict(job=job_, kind=kind_, bi=bi_)
                if kind_ == "c":
                    cl, ckr_, cfk_, cfv_, clf_ = job_.cache
                    r0 = bi_ * 128
                    d["lat32"] = lat32_r.next(); d["kr32"] = kr32_r.next()
                    d["fk32"] = fk32_r.next(); d["fv32"] = fv32_r.next(); d["lf32"] = lf32_r.next()
                    E.dma("sp", d["lat32"][0][:], cl[r0:r0 + 128, :], [], [d["lat32"][1]], d["lat32"][1])
                    E.dma("sp", d["kr32"][0][:], ckr_[r0:r0 + 128, :], [], [d["kr32"][1]], d["kr32"][1])
                    E.dma("sp", d["fk32"][0][:], cfk_[r0:r0 + 128, :], [], [d["fk32"][1]], d["fk32"][1])
                    E.dma("sp", d["fv32"][0][:], cfv_[r0:r0 + 128, :], [], [d["fv32"][1]], d["fv32"][1])
                    E.dma("sp", d["lf32"][0][:], clf_[r0:r0 + 128, :], [], [d["lf32"][1]], d["lf32"][1])
                else:
                    blk_ = bi_
                    nv_ = blk_["nv"]
                    if nv_ == 128:
                        xt_, xtk_ = xt_r.next()
                        E.dma("sp", xt_[:], blk_["src"][blk_["r0"]:blk_["r0"] + 128, :], [], [xtk_], xtk_)
                    else:
                        xt_, xtk_ = xpart, "xpart"
                        E.dma("sp", xt_[0:nv_, :], blk_["src"][blk_["r0"]:blk_["r0"] + nv_, :], [], [xtk_], "xpartd")
                    d["xt"] = (xt_, xtk_)
                return d

            if job is jobs[0]:
                flat = [(j_, k_, b_) for j_ in jobs for (k_, b_) in ([("c", i) for i in range(j_.nbc)] + [("n", b) for b in j_.blocks])]
                flat_i = [0]
                pre_ld = [issue_loads(*flat[0])]
            for kind, bi in allblocks:
                LD = pre_ld[0]
                flat_i[0] += 1
                if flat_i[0] < len(flat):
                    pre_ld[0] = issue_loads(*flat[flat_i[0]])
                lkb, lkbk = lkb_r.next(); fkb, fkbk = fkb_r.next(); vfb, vfbk = vfb_r.next()
                hasq = False
                if kind == "c":
                    lat32, lat32k = LD["lat32"]; kr32, kr32k = LD["kr32"]
                    fk32, fk32k = LD["fk32"]; fv32, fv32k = LD["fv32"]; lf32, lf32k = LD["lf32"]
                    E.cp("act", lkb[:, 0:256], lat32[:], [lat32k], [lkbk + "l"])
                    E.cp("dve", lkb[:, 320:352], kr32[:], [kr32k], [lkbk + "r"])
                    E.cp("act", fkb[:], fk32[:], [fk32k], [fkbk])
                    E.cp("dve", vfb[:], fv32[:], [fv32k], [vfbk])
                    tri_i = 0
                else:
                    lf32, lf32k = lf32_r.next()
                    blk = bi
                    nv = blk["nv"]; hasq = blk["q"]; tri_i = blk["tri"]
                    rtab_t, rtab_k = rope_sb[id(blk["rope"][0])]
                    rtab = rtab_t[:, blk["rope"][1], :]
                    orow = blk["orow"]
                    xt, xtk = LD["xt"]
                    junk, jk = junk_r.next(); stt, stk = st_r.next()
                    E.act(junk[:], xt[:], AF.Square, [xtk, stk + "z"], [jk, stk + "a"], accum=stt[:, 0:1])
                    rs, rsk = rstd_from(stt[:, 0:1], D, stt, stk)
                    hb, hbk = hb_r.next()
                    E.stt(hb[:], xt[:], rs, gmix[:], ALU.mult, ALU.mult, [xtk, rsk, "gmix"], [hbk])
                    pT, pTk = pT_r.next(); hT, hTk = hT_r.next()
                    for c in range(8):
                        E.tr(pT[:, c * 128:(c + 1) * 128], hb[:, c * 128:(c + 1) * 128], identb[:], [hbk, "identb"], [pTk])
                    E.cp("act", hT[:].rearrange("p c t -> p (c t)"), pT[:, :], [pTk], [hTk])
                    pg, pgk = pG_r.next()
                    for c in range(8):
                        E.mm(pg[:, 0:296], hT[:, c, :], win[:, c, CKV:CKV + 296], c == 0, c == 7, [hTk, "win"], [pgk])
                    stt2, stk2 = st_r.next(); junk2, jk2 = junk_r.next()
                    E.act(junk2[:, 0:256], pg[:, 0:256], AF.Square, [pgk, stk2 + "z"], [jk2, stk2 + "a"], accum=stt2[:, 0:1])
                    rs2, rs2k = rstd_from(stt2[:, 0:1], 256, stt2, stk2)
                    lat32, lat32k = lat32_r.next()
                    E.stt(lat32[:], pg[:, 0:256], rs2, gkv[:], ALU.mult, ALU.mult, [pgk, rs2k, "gkv"], [lat32k])
                    E.dma(STQ, lat_o[orow:orow + nv, :], lat32[0:nv, :], [lat32k], [], lat32k + "o")
                    E.cp("act", lkb[:, 0:256], lat32[:], [lat32k], [lkbk + "l"])
                    kr32, kr32k = kr32_r.next(); krt, krtk = krt_r.next()
                    rope32(kr32[:], kr32k, pg[:, 256:288], pgk, rtab, rtab_k, krt, krtk)
                    E.dma(STQ, kr_o[orow:orow + nv, :], kr32[0:nv, :], [kr32k], [], kr32k + "o")
                    E.cp("act", lkb[:, 320:352], kr32[:], [kr32k], [lkbk + "r"])
                    lgt, lgtk = lgt_r.next()
                    E.tt("dve", lgt[:], pg[:, 288:296], bfg[:], ALU.add, [pgk, "bfg"], [lgtk])
                    E.act(lgt[:], lgt[:], AF.Exp, [lgtk], [lgtk], scale=-1.0)
                    E.act(lgt[:], lgt[:], AF.Ln, [lgtk], [lgtk], bias=1.0)
                    E.ts("dve", lf32[:], lgt[:], -1.0, None, ALU.mult, None, [lgtk], [lf32k])
                    E.dma(STQ, lf_o[orow:orow + nv, :], lf32[0:nv, :], [lf32k], [], lf32k + "o")
                    pg, pgk = pG_r.next()
                    for c in range(8):
                        E.mm(pg[:, :], hT[:, c, :], win[:, c, CFK:CFK + 512], c == 0, c == 7, [hTk, "win"], [pgk])
                    fk32, fk32k = fk32_r.next()
                    E.cp("act", fk32[:], pg[:, :], [pgk], [fk32k])
                    E.dma(STQ, fk_o[orow:orow + nv, :], fk32[0:nv, :], [fk32k], [], fk32k + "o")
                    E.cp("act", fkb[:], fk32[:], [fk32k], [fkbk])
                    pg, pgk = pG_r.next()
                    for c in range(8):
                        E.mm(pg[:, :], hT[:, c, :], win[:, c, CFV:CFV + 512], c == 0, c == 7, [hTk, "win"], [pgk])
                    fv32, fv32k = fv32_r.next()
                    E.cp("act", fv32[:], pg[:, :], [pgk], [fv32k])
                    E.dma(STQ, fv_o[orow:orow + nv, :], fv32[0:nv, :], [fv32k], [], fv32k + "o")
                    E.cp("act", vfb[:], fv32[:], [fv32k], [vfbk])
                    if hasq:
                        pg, pgk = pG_r.next()
                        for c in range(8):
                            E.mm(pg[:, 0:384], hT[:, c, :], win[:, c, CQ:CQ + 384], c == 0, c == 7, [hTk, "win"], [pgk])
                        stt3, stk3 = st_r.next(); junk3, jk3 = junk_r.next()
                        E.act(junk3[:, 0:384], pg[:, 0:384], AF.Square, [pgk, stk3 + "z"], [jk3, stk3 + "a"], accum=stt3[:, 0:1])
                        rs3, rs3k = rstd_from(stt3[:, 0:1], 384, stt3, stk3)
                        cqn, cqnk = cqn_r.next()
                        E.stt(cqn[:], pg[:, 0:384], rs3, gq[:], ALU.mult, ALU.mult, [pgk, rs3k, "gq"], [cqnk])
                        pg, pgk = pG_r.next()
                        for c in range(8):
                            E.mm(pg[:, :], hT[:, c, :], win[:, c, CFQ:CFQ + 512], c == 0, c == 7, [hTk, "win"], [pgk])
                        fqb, fqbk = fqb_r.next()
                        E.cp("act", fqb[:], pg[:, :], [pgk], [fqbk])
                t0 = kb * 128
                pg, pgk = pG_r.next()
                E.mm(pg[:, 0:8], tri[:, tri_i, :], lf32[:], True, True, ["tri", lf32k], [pgk])
                E.mm(pg[:, 8:16], ones3[:, tri_i, :], lf32[:], True, True, ["ones3", lf32k], [pgk])
                cum32, cum32k = cum32_r.next()
                E.tt("dve", cum32[:], pg[:, 0:8], carry[:], ALU.add, [pgk, "carry"], [cum32k])
                E.tt("dve", carry[:], pg[:, 8:16], carry[:], ALU.add, [pgk, "carry"], ["carry"])
                E.ts("dve", ncum[:, kb, :], cum32[:], -1.0, None, ALU.mult, None, [cum32k], ["ncum"])
                pT, pTk = pT_r.next(); latT, latTk = latT_r.next()
                E.tr(pT[:, 0:128], lkb[:, 0:128], identb[:], [lkbk + "l", "identb"], [pTk])
                E.tr(pT[:, 128:256], lkb[:, 128:256], identb[:], [lkbk + "l", "identb"], [pTk])
                E.tr(pT[:, 256:384], lkb[:, 224:352], identb[:], [lkbk + "l", lkbk + "r", lkbk + "p", "identb"], [pTk])
                E.cp("dve", latT[:, 0:2, :].rearrange("p c t -> p (c t)"), pT[:, 0:256], [pTk], [latTk + "l"])
                E.cp("dve", latT[96:128, 2, :], pT[96:128, 256:384], [pTk], [latTk + "r"])
                E.dma(STQ, job.KTr[:, t0:t0 + 128], latT[96:128, 2, :], [latTk + "r"], ["dram_KTr_" + jn], latTk + "ro")
                ktn, ktnk = ktn_r.next()
                for half in range(2):
                    pg, pgk = pG_r.next()
                    pgv = pg[0:64, :].rearrange("p (h t) -> p h t", h=4)
                    for hh in range(4):
                        h = half * 4 + hh
                        for c in range(2):
                            E.mm(pgv[:, hh, :], wuk[:, c, h, :], latT[:, c, :], c == 0, c == 1, ["wuk", latTk + "l"], [pgk])
                    E.cp("act" if half == 0 else "dve", ktn[:, half * 4:(half + 1) * 4, :], pgv, [pgk], [ktnk + str(half)])
                E.dma(STQ, job.KTn[:, :, t0:t0 + 128].rearrange("h p t -> p h t"), ktn[:], [ktnk + "0", ktnk + "1"], ["dram_KTn_" + jn], ktnk + "o")
                pg, pgk = pG_r.next()
                for c in range(2):
                    E.mm(pg[:, :], latT[:, c, :], wuv[:, c].rearrange("p h d -> p (h d)"), c == 0, c == 1, [latTk + "l", "wuv"], [pgk])
                vmb, vmbk = vmb_r.next()
                E.cp("act", vmb[:], pg[:, :], [pgk], [vmbk])
                E.dma(STQ, job.Vm[:, :, kb, :].rearrange("h p d -> p h d"), vmb[:].rearrange("p (h d) -> p h d", h=8), [vmbk], ["dram_Vm_" + jn], vmbk + "o")
                E.dma(STQ, job.Vf[:, :, kb, :].rearrange("h p d -> p h d"), vfb[:].rearrange("p (h d) -> p h d", h=8), [vfbk], ["dram_Vf_" + jn], vfbk + "o")
                pT, pTk = pT_r.next(); ktf, ktfk = ktf_r.next()
                for h in range(8):
                    E.tr(pT[0:64, h * 128:(h + 1) * 128], fkb[:, h * 64:(h + 1) * 64], identb[:], [fkbk, "identb"], [pTk])
                E.cp("dve", ktf[:].rearrange("p h t -> p (h t)"), pT[0:64, :], [pTk], [ktfk])
                E.dma(STQ, job.KTf[:, :, t0:t0 + 128].rearrange("h p t -> p h t"), ktf[:], [ktfk], ["dram_KTf_" + jn], ktfk + "o")
                if hasq:
                    q0 = qblk * 128
                    pT, pTk = pT_r.next(); cqT, cqTk = cqT_r.next()
                    for c in range(3):
                        E.tr(pT[:, c * 128:(c + 1) * 128], cqn[:, c * 128:(c + 1) * 128], identb[:], [cqnk, "identb"], [pTk])
                    E.cp("act", cqT[:].rearrange("p c t -> p (c t)"), pT[:, 0:384], [pTk], [cqTk])
                    qb, qbk = qb_r.next(); qrt, qrtk = qrt_r.next()
                    for half in range(2):
                        pg, pgk = pG_r.next()
                        for c in range(3):
                            E.mm(pg[:, 0:384], cqT[:, c, :], wuq[:, c, half * 384:(half + 1) * 384], c == 0, c == 2, [cqTk, "wuq"], [pgk])
                        pv = pg[:, 0:384].rearrange("p (h d) -> p h d", h=4)
                        qv = qb[:, half * 384:(half + 1) * 384].rearrange("p (h d) -> p h d", h=4)
                        E.cp("dve", qv[:, :, 0:64], pv[:, :, 0:64], [pgk], [qbk + "n%d" % half])
                        tb = rtab.unsqueeze(1)
                        tmp = qrt[:, half * 4:(half + 1) * 4, :]
                        tk = qrtk + str(half)
                        E.tt("dve", tmp[:, :, 0:32], pv[:, :, 64:96], tb[:, :, 0:32].broadcast_to([128, 4, 32]), ALU.mult, [pgk, rtab_k], [tk + "a"])
                        E.tt("dve", tmp[:, :, 32:48], pv[:, :, 80:96], tb[:, :, 32:48].broadcast_to([128, 4, 16]), ALU.mult, [pgk, rtab_k], [tk + "b"])
                        E.tt("dve", tmp[:, :, 48:64], pv[:, :, 64:80], tb[:, :, 48:64].broadcast_to([128, 4, 16]), ALU.mult, [pgk, rtab_k], [tk + "c"])
                        E.tt("dve", qv[:, :, 64:96], tmp[:, :, 0:32], tmp[:, :, 32:64], ALU.add, [tk + "a", tk + "b", tk + "c"], [qbk + "r%d" % half])
                    qkeys = [qbk + "n0", qbk + "n1", qbk + "r0", qbk + "r1"]
                    pT, pTk = pT_r.next(); qtm, qtmk = qtm_r.next()
                    for h in range(8):
                        E.tr(pT[0:96, h * 128:(h + 1) * 128], qb[:, h * 96:(h + 1) * 96], identb[:], qkeys + ["identb"], [pTk])
                    E.cp("dve", qtm[:].rearrange("p h t -> p (h t)"), pT[0:96, :], [pTk], [qtmk])
                    E.dma(STQ, job.QTm[:, :, q0:q0 + 128].rearrange("h p t -> p h t"), qtm[:], [qtmk], ["dram_QTm_" + jn], qtmk + "o")
                    pT, pTk = pT_r.next(); qtf, qtfk = qtf_r.next()
                    for h in range(8):
                        E.tr(pT[0:64, h * 128:(h + 1) * 128], fqb[:, h * 64:(h + 1) * 64], identb[:], [fqbk, "identb"], [pTk])
                    E.cp("act", qtf[:].rearrange("p h t -> p (h t)"), pT[0:64, :], [pTk], [qtfk])
                    E.dma(STQ, job.QTf[:, 0:64, q0:q0 + 128].rearrange("h p t -> p h t"), qtf[:], [qtfk], ["dram_QTf_" + jn], qtfk + "o")
                    cum8, cum8k = cum8_r.next(); cumT, cumTk = cumT_r.next()
                    E.ts("dve", cum8[:], cum32[:], 8.0, None, ALU.mult, None, [cum32k], [cum8k])
                    pT, pTk = pT_r.next()
                    E.tr(pT[0:8, 0:128], cum8[:, 0:8], identb[:], [cum8k, "identb"], [pTk])
                    E.cp("dve", cumT[:], pT[0:8, 0:128], [pTk], [cumTk])
                    E.dma(STQ, job.QTf[:, 64, q0:q0 + 128], cumT[:], [cumTk], ["dram_QTf_" + jn], cumTk + "o")
                    qblk += 1
                kb += 1
            E.dma(STQ, job.NCUM[:, :, :], ncum[:, 0:job.nbk, :], ["ncum"], [], "ncum_o")
        if 'A' in PH:
            _tr = int(os.environ.get('K_ATRUNC', '0'))
            if _tr:
                print('PHASE A n_ins', len(P.ins)); P.ins = P.ins[:_tr]
            P.emit(nc)

    with ExitStack() as st:
        sb = lambda name, shape, dt: st.enter_context(nc.sbuf_tensor("B_" + name, shape, dt))
        psb = lambda name, shape, dt: st.enter_context(nc.psum_tensor("B_" + name, shape, dt))
        P = Prog(); E = Em(P)
        identf = sb("identf", [128, 128], F32); identb = sb("identb", [128, 128], BF16)
        maskf = sb("maskf", [128, 2, 128], F32); maskb = sb("maskb", [128, 2, 128], BF16)
        onesr = sb("onesr", [65, 64], F32)
        E.dma("sp", identf[:], c_ident[:, :], [], ["identf"], "w_id")
        E.cp("dve", identb[:], identf[:], ["identf"], ["identb"])
        E.dma("sp", maskf[:], c_mask.rearrange("a p n -> p a n"), [], ["maskf"], "w_mask")
        E.cp("dve", maskb[:], maskf[:], ["maskf"], ["maskb"])
        E.memset("dve", onesr[:], 1.0, ["onesr"])
        ktm_r = Ring(sb, "ktm", [96, LKMAX], BF16, 2)
        ktf_r = Ring(sb, "ktf", [96, LKMAX], BF16, 2)
        v_r = Ring(sb, "vv", [128, NBKMAX, 128], BF16, 2)
        qm_r = Ring(sb, "qm", [96, LQMAX], BF16, 2)
        qf_r = Ring(sb, "qf", [96, LQMAX], BF16, 2)
        pt_r = Ring(sb, "pt", [128, 512], BF16, 5)
        osb_r = Ring(sb, "osb", [65, 512], F32, 2)
        rec_r = Ring(sb, "rec", [65, 512], F32, 2)
        mixo_r = Ring(sb, "mixo", [64, 512], BF16, 2)
        pS_r = Ring(psb, "pS", [128, 512], F32, 5)
        pO_r = Ring(psb, "pO", [128, 512], F32, 2)
        for i in range(2):
            E.memset("dve", ktf_r.t[i][64:96, :], 0.0, [ktf_r.k[i] + "1"])
            E.memset("dve", ktf_r.t[i][64:65, :], 1.0, [ktf_r.k[i] + "1"])
            E.memset("dve", qf_r.t[i][64:96, :], 0.0, [qf_r.k[i] + "z", qf_r.k[i]])
            E.memset("dve", v_r.t[i][:, :, 65:128], 0.0, [v_r.k[i] + "1"])

        ncum_r = Ring(sb, "ncumr", [128, NBKMAX, 8], F32, 2)

        def load_head(job, hd):
            nbk = job.nbk; lk = nbk * 128; lq = job.lq
            fox = hd >= 8
            h = hd % 8
            vt, vk = v_r.next()
            vkeys = []
            if fox:
                kt, ktk = ktf_r.next(); qt, qtk = qf_r.next(); KD = 96
                E.dma("sp", kt[0:64, 0:lk], job.KTf[h], [], [ktk], ktk)
                E.dma("sp", qt[0:65, 0:lq], job.QTf[h], [], [qtk], qtk)
                vsrc = job.Vf
                kr_keys = [ktk, ktk + "1", qtk + "z"]
                scale = FOX_SCALE
            else:
                kt, ktk = ktm_r.next(); qt, qtk = qm_r.next(); KD = 96
                E.dma("sp", kt[0:64, 0:lk], job.KTn[h], [], [ktk], ktk)
                E.dma("sp", kt[64:96, 0:lk], job.KTr[:, :], [], [ktk + "r"], ktk + "r")
                E.dma("sp", qt[0:96, 0:lq], job.QTm[h], [], [qtk], qtk)
                vsrc = job.Vm
                kr_keys = [ktk, ktk + "r"]
                scale = MLA_SCALE
            for b0_ in range(0, nbk, 16):
                b1_ = min(nbk, b0_ + 16)
                E.dma("sp", vt[:, b0_:b1_, 0:64], vsrc[h, :, b0_:b1_, :], [], [vk + "c%d" % b0_], vk + "_%d" % b0_)
                vkeys.append(vk + "c%d" % b0_)
            return dict(fox=fox, h=h, vt=vt, vk=vk, vkeys=vkeys, kt=kt, ktk=ktk, qt=qt, qtk=qtk, KD=KD, kr_keys=kr_keys, scale=scale)

        RS = dscr("RS_scr", [4, 512], F32)
        rs_cnt = [0]
        items = [(job, hd) for job in jobs for hd in range(16)]
        nxt_loaded = load_head(*items[0])
        for it_i, (job, hd) in enumerate(items):
            nbk = job.nbk; lk = nbk * 128; lq = job.lq; QW = job.qw
            if hd == 0:
                ncum, ncumk = ncum_r.next()
                E.dma("sp", ncum[:, 0:nbk, :], job.NCUM[:, :, :], [], [ncumk], ncumk)
                for i in range(2):
                    vt, vk = v_r.t[i], v_r.k[i]
                    E.memset("dve", vt[:, :, 64:65], 1.0, [vk + "1"])
                    pb_ = job.nbc if job.name != "p" else 0
                    nvb = job.blocks[0]["nv"]
                    E.memset("dve", vt[:, pb_, 64:65], 0.0, [vk + "1"])
                    E.memset("dve", vt[0:nvb, pb_, 64:65], 1.0, [vk + "1"])
            L_ = nxt_loaded
            if it_i + 1 < len(items):
                nxt_loaded = load_head(*items[it_i + 1])
            fox = L_["fox"]; h = L_["h"]; vt = L_["vt"]; vk = L_["vk"]; kt = L_["kt"]; ktk = L_["ktk"]
            qt = L_["qt"]; qtk = L_["qtk"]; KD = L_["KD"]; kr_keys = L_["kr_keys"]; scale = L_["scale"]; vkeys = L_["vkeys"]
            if True:
                nqt = lq // QW
                for t in range(nqt):
                    q0 = t * QW
                    if job.name == "p":
                        nfull = 1 + t * (QW // 128)
                        kbl = [(j, 0, None) for j in range(nfull)]
                        for jj in range(QW // 128):
                            kbl.append((nfull + jj, jj * 128, 0 if fox else 1))
                    else:
                        kbl = [(j, 0, None) for j in range(job.nbc)]
                        kbl.append((job.nbc, 0, 0 if fox else None))
                    pO, pOk = pO_r.next()
                    pend = []

                    def qk(idx):
                        j, c0, mk = kbl[idx]
                        pS, pSk = pS_r.next()
                        E.mm(pS[:, c0:QW], kt[0:KD, j * 128:(j + 1) * 128], qt[0:KD, q0 + c0:q0 + QW], True, mk is None, kr_keys + [qtk], [pSk])
                        if mk is not None:
                            E.mm(pS[:, c0:c0 + 128], identb[:, :], maskb[:, mk, :], False, True, ["identb", "maskb"], [pSk])
                        return pS, pSk

                    LA = 2
                    qq = [qk(i_) for i_ in range(min(LA, len(kbl)))]
                    for idx in range(len(kbl)):
                        j, c0, mk = kbl[idx]
                        pS, pSk = qq.pop(0)
                        if idx + LA < len(kbl):
                            qq.append(qk(idx + LA))
                        pt, ptk = pt_r.next()
                        if fox:
                            E.act(pt[:, c0:QW], pS[:, c0:QW], AF.Exp, [pSk, ncumk], [ptk], bias=ncum[:, j, h:h + 1], scale=scale)
                        else:
                            E.act(pt[:, c0:QW], pS[:, c0:QW], AF.Exp, [pSk], [ptk], scale=scale)
                        E.mm(pO[:, c0:QW], vt[:, j, :], pt[:, c0:QW], idx == 0, idx == len(kbl) - 1, vkeys + [vk + "1", ptk], [pOk])
                    osb, osbk = osb_r.next(); rec, reck = rec_r.next(); mixo, mixok = mixo_r.next()
                    E.cp("dve", osb[0:65, 0:QW], pO[0:65, 0:QW], [pOk], [osbk])
                    rsl = rs_cnt[0] % 4
                    rs_cnt[0] += 1
                    E.dma("sp", RS[rsl:rsl + 1, 0:QW], osb[64:65, 0:QW], [osbk], ["RS%d" % rsl], "rs_w%d" % rsl)
                    E.dma("sp", rec[0:64, 0:QW], RS[rsl:rsl + 1, 0:QW].partition_broadcast(64), ["RS%d" % rsl], [reck], "rs_r%d" % rsl)
                    E.P.op("dve", (lambda o, i: (lambda e: e.reciprocal(out=o, in_=i)))(rec[0:64, 0:QW], rec[0:64, 0:QW]), [reck], [reck])
                    E.tt("dve", mixo[0:64, 0:QW], osb[0:64, 0:QW], rec[0:64, 0:QW], ALU.mult, [osbk, reck], [mixok])
                    E.dma("sp", job.MIX[hd * 64:(hd + 1) * 64, q0:q0 + QW], mixo[0:64, 0:QW], [mixok], [], mixok + "o")
        if 'B' in PH:
            P.emit(nc)

    with ExitStack() as st:
        sb = lambda name, shape, dt: st.enter_context(nc.sbuf_tensor("C_" + name, shape, dt))
        psb = lambda name, shape, dt: st.enter_context(nc.psum_tensor("C_" + name, shape, dt))
        P = Prog(); E = Em(P)
        NFC = DFF // 128
        wo = sb("wo", [128, 8, D], BF16)
        wg = sb("wg", [128, 8, DFF], BF16)
        wu = sb("wu", [128, 8, DFF], BF16)
        wd = sb("wd", [128, NFC, D], BF16)
        identf = sb("identf", [128, 128], F32); identb = sb("identb", [128, 128], BF16)
        gffn = sb("gffn", [128, D], F32); gfin = sb("gfin", [128, D], F32)
        E.dma("sp", identf[:], c_ident[:, :], [], ["identf"], "w_id")
        E.cp("dve", identb[:], identf[:], ["identf"], ["identb"])
        E.dma("sp", gffn[:], g_ffn.partition_broadcast(128), [], ["gffn"], "w_gffn")
        E.dma("sp", gfin[:], g_fin.partition_broadcast(128), [], ["gfin"], "w_gfin")
        TWMAX = 256
        mix_r = Ring(sb, "mixt", [128, 8, TWMAX], BF16, 1)
        xl_r = Ring(sb, "xl", [128, D], F32, 2)
        _ce = [0]

        def wloadc(dst_ap, src_ap, wkey, ncols):
            stg, sk = xl_r.next()
            E.dma("sp", stg[:, 0:ncols], src_ap, [], [sk], sk)
            eng = "act" if _ce[0] % 2 == 0 else "dve"
            _ce[0] += 1
            E.cp(eng, dst_ap, stg[:, 0:ncols], [sk], [wkey])
        for c in range(8):
            wloadc(wo[:, c, :], w_out[c * 128:(c + 1) * 128, :], "wo", 1024)
        for c in range(8):
            for (c0, cw) in ((0, 1024), (1024, 1024), (2048, 768)):
                wloadc(wg[:, c, c0:c0 + cw], w_gate[c * 128:(c + 1) * 128, c0:c0 + cw], "wg", cw)
                wloadc(wu[:, c, c0:c0 + cw], w_up[c * 128:(c + 1) * 128, c0:c0 + cw], "wu", cw)
        for c in range(NFC):
            wloadc(wd[:, c, :], w_down[c * 128:(c + 1) * 128, :], "wd", 1024)
        x2_r = Ring(sb, "x2", [128, 2, D], F32, 2)
        h2_r = Ring(sb, "h2", [128, D], BF16, 2)
        h2T_r = Ring(sb, "h2T", [128, 8, TWMAX], BF16, 1)
        actT_r = Ring(sb, "actT", [128, NFC, TWMAX], BF16, 1)
        sg_r = Ring(sb, "sg", [128, TWMAX], F32, 2)
        st_r = Ring(sb, "statc", [128, 8], F32, 4)
        for _i in range(4):
            E.memset("dve", st_r.t[_i][:], 0.0, [st_r.k[_i] + "z"])
        pA_r = Ring(psb, "pA", [128, 512], F32, 4)
        pF_r = Ring(psb, "pF", [128, 512], F32, 3)
        pT_r = Ring(psb, "pTc", [128, 1024], BF16, 1)

        def rstd_c(ss_ap, n, stt, stk):
            E.cp("act", stt[:, 6:7], stt[:, 7:8], [stk + "a", stk + "z"], [stk + "a2"])
            E.act(stt[:, 1:2], ss_ap, AF.Ln, [stk + "a", stk + "a2"], [stk + "b"], bias=EPS, scale=1.0 / n)
            E.act(stt[:, 2:3], stt[:, 1:2], AF.Exp, [stk + "b"], [stk + "c"], scale=-0.5)
            return stt[:, 2:3], stk + "c"

        for job in jobs:
            lq = job.lq
            TW = 256 if lq % 256 == 0 else 128
            nsub = TW // 128
            def c_loads(job_, t_):
                TW_ = 256 if job_.lq % 256 == 0 else 128
                q0_ = t_ * TW_
                mixt_, mixk_ = mix_r.next()
                E.dma("sp", mixt_[:, :, 0:TW_], job_.MIX[:, q0_:q0_ + TW_].rearrange("(c p) t -> p c t", p=128), [], [mixk_], mixk_)
                xls = []
                for s_ in range(TW_ // 128):
                    r0_ = q0_ + s_ * 128
                    xl_, xlk_ = xl_r.next()
                    if job_.nvq == 128:
                        E.dma("sp", xl_[:], job_.xq[r0_:r0_ + 128, :], [], [xlk_], xlk_)
                    else:
                        E.dma("sp", xl_[0:job_.nvq, :], job_.xq[0:job_.nvq, :], [], [xlk_], xlk_)
                    xls.append((xl_, xlk_))
                return (mixt_, mixk_, xls)

            if job is jobs[0]:
                c_flat = [(j_, t_) for j_ in jobs for t_ in range(j_.lq // (256 if j_.lq % 256 == 0 else 128))]
                c_i = [0]
                c_pre = [c_loads(*c_flat[0])]
            for t in range(lq // TW):
                q0 = t * TW
                mixt, mixk, xls = c_pre[0]
                c_i[0] += 1
                x2, x2k = x2_r.next(); h2T, h2Tk = h2T_r.next()
                for s in range(nsub):
                    r0 = q0 + s * 128
                    xl, xlk = xls[s]
                    pa0, pa0k = pA_r.next(); pa1, pa1k = pA_r.next()
                    for c in range(8):
                        E.mm(pa0[:, :], mixt[:, c, s * 128:(s + 1) * 128], wo[:, c, 0:512], c == 0, c == 7, [mixk, "wo"], [pa0k])
                        E.mm(pa1[:, :], mixt[:, c, s * 128:(s + 1) * 128], wo[:, c, 512:1024], c == 0, c == 7, [mixk, "wo"], [pa1k])
                    E.tt("dve", x2[:, s, 0:512], pa0[:, :], xl[:, 0:512], ALU.add, [pa0k, xlk], [x2k + "a%d" % s])
                    E.tt("dve", x2[:, s, 512:1024], pa1[:, :], xl[:, 512:1024], ALU.add, [pa1k, xlk], [x2k + "b%d" % s])
                    xk2 = [x2k + "a%d" % s, x2k + "b%d" % s]
                    junk, jk = h2_r.next(); stt, stk = st_r.next()
                    E.act(junk[:], x2[:, s, :], AF.Square, xk2 + [stk + "z"], [jk, stk + "a"], accum=stt[:, 0:1])
                    rs, rsk = rstd_c(stt[:, 0:1], D, stt, stk)
                    h2, h2k = h2_r.next()
                    E.stt(h2[:], x2[:, s, :], rs, gffn[:], ALU.mult, ALU.mult, xk2 + [rsk, "gffn"], [h2k])
                    pT, pTk = pT_r.next()
                    for c in range(8):
                        E.tr(pT[:, c * 128:(c + 1) * 128], h2[:, c * 128:(c + 1) * 128], identb[:], [h2k, "identb"], [pTk])
                    E.cp("act", h2T[:, :, s * 128:(s + 1) * 128], pT[:, :].rearrange("p (c t) -> p c t", c=8), [pTk], [h2Tk + str(s)])
                if c_i[0] < len(c_flat):
                    c_pre[0] = c_loads(*c_flat[c_i[0]])
                h2keys = [h2Tk + str(s) for s in range(nsub)]
                actT, actTk = actT_r.next()
                for f in range(NFC):
                    pgt, pgtk = pF_r.next(); put, putk = pF_r.next()
                    for c in range(8):
                        E.mm(pgt[:, 0:TW], wg[:, c, f * 128:(f + 1) * 128], h2T[:, c, 0:TW], c == 0, c == 7, ["wg"] + h2keys, [pgtk])
                    for c in range(8):
                        E.mm(put[:, 0:TW], wu[:, c, f * 128:(f + 1) * 128], h2T[:, c, 0:TW], c == 0, c == 7, ["wu"] + h2keys, [putk])
                    sg, sgk = sg_r.next()
                    E.act(sg[:, 0:TW], pgt[:, 0:TW], AF.Silu, [pgtk], [sgk])
                    E.tt("dve", actT[:, f, 0:TW], sg[:, 0:TW], put[:, 0:TW], ALU.mult, [sgk, putk], [actTk + "_%d" % f])
                akeys = [actTk + "_%d" % f for f in range(NFC)]
                for s in range(nsub):
                    r0 = q0 + s * 128
                    pa0, pa0k = pA_r.next(); pa1, pa1k = pA_r.next()
                    for f in range(NFC):
                        E.mm(pa0[:, :], actT[:, f, s * 128:(s + 1) * 128], wd[:, f, 0:512], f == 0, f == NFC - 1, akeys + ["wd"], [pa0k])
                        E.mm(pa1[:, :], actT[:, f, s * 128:(s + 1) * 128], wd[:, f, 512:1024], f == 0, f == NFC - 1, akeys + ["wd"], [pa1k])
                    xk2 = [x2k + "a%d" % s, x2k + "b%d" % s]
                    E.tt("dve", x2[:, s, 0:512], pa0[:, :], x2[:, s, 0:512], ALU.add, [pa0k] + xk2, [x2k + "a%d" % s])
                    E.tt("dve", x2[:, s, 512:1024], pa1[:, :], x2[:, s, 512:1024], ALU.add, [pa1k] + xk2, [x2k + "b%d" % s])
                    junk, jk = h2_r.next(); stt, stk = st_r.next()
                    E.act(junk[:], x2[:, s, :], AF.Square, xk2 + [stk + "z"], [jk, stk + "a"], accum=stt[:, 0:1])
                    rs, rsk = rstd_c(stt[:, 0:1], D, stt, stk)
                    E.stt(x2[:, s, :], x2[:, s, :], rs, gfin[:], ALU.mult, ALU.mult, xk2 + [rsk, "gfin"], xk2)
                    nv = job.nvq
                    E.dma("sp", job.y[r0:r0 + nv, :], x2[0:nv, s, :], xk2, [], x2k + "o%d" % s)
        if 'C' in PH:
            P.emit(nc)
    return nc


def make_consts(S, PAST):
    NBF = S // 128
    ident = np.eye(128, dtype=np.float32)
    jj = np.arange(128)[:, None]; tt = np.arange(128)[None, :]
    U = (jj <= tt).astype(np.float32)
    tri = np.stack([U, U * (jj < 16), U * (jj < 32)]).astype(np.float32)
    on = np.ones((128, 128), np.float32)
    ones = np.stack([on, on * (jj < 16), on * (jj < 32)]).astype(np.float32)
    m_fox = np.where(jj <= tt, 0.0, NEG)
    m_mla = np.where((jj // 64) <= (tt // 64), 0.0, NEG)
    mask = np.stack([m_fox, m_mla]).astype(np.float32)
    half = 16
    inv = (10000.0 ** (-np.arange(half, dtype=np.float32) / half)).astype(np.float32)

    def tab(pos):
        ang = pos.astype(np.float32)[..., None] * inv
        c = np.cos(ang).astype(np.float32); s = np.sin(ang).astype(np.float32)
        return np.concatenate([c, c, -s, s], axis=-1).astype(np.float32)
    p = np.arange(128)[:, None]
    b = np.arange(NBF + 1)[None, :]
    posp = np.where(b == 0, p, NMETA + (b - 1) * 128 + p)
    ropep = tab(posp)
    ropes = tab((PAST + np.arange(128))[:, None])
    return dict(c_ident=ident, c_tri=tri, c_ones=ones, c_mask=mask, c_ropep=ropep, c_ropes=ropes)


def make_in_maps(inp, ncores, S, PAST, NS):
    consts = make_consts(S, PAST)
    f = lambda a: np.ascontiguousarray(np.asarray(a, dtype=np.float32))
    shared = dict(
        meta=f(inp["meta_tokens"]), w_in=f(inp["w_in"][0]), w_uq=f(inp["w_mla_uq"][0]), w_ukv=f(inp["w_mla_ukv"][0]),
        w_out=f(inp["w_out"][0]), w_gate=f(inp["w_ffn_gate"][0]), w_up=f(inp["w_ffn_up"][0]), w_down=f(inp["w_ffn_down"][0]),
        g_mix=f(inp["norm_mix"][0]).reshape(1, -1), b_forget=f(inp["b_forget"][0]).reshape(1, -1),
        g_q=f(inp["mla_q_norm"][0]).reshape(1, -1), g_kv=f(inp["mla_kv_norm"][0]).reshape(1, -1),
        g_ffn=f(inp["norm_ffn"][0]).reshape(1, -1), g_fin=f(inp["norm_final"]).reshape(1, -1), **consts)
    maps = []
    for c in range(ncores):
        m = dict(shared)
        m["xp"] = f(inp["x_prompt"][c])
        sl = slice(c * NS, (c + 1) * NS)
        m["xs"] = f(inp["x_sample"][sl])
        m["c_lat"] = f(inp["cache_mla_latent"][0, sl]); m["c_kr"] = f(inp["cache_mla_krope"][0, sl])
        m["c_fk"] = f(inp["cache_fox_k"][0, sl]).reshape(NS, PAST, 512); m["c_fv"] = f(inp["cache_fox_v"][0, sl]).reshape(NS, PAST, 512)
        m["c_lf"] = f(inp["cache_fox_logf"][0, sl])
        maps.append(m)
    return maps


def assemble(results, ncores, S, NS):
    LP = NMETA + S
    cat = lambda k: np.concatenate([np.asarray(r[k], dtype=np.float32)[None] for r in results], axis=0)
    y_p = cat("y_p")
    y_s = cat("y_s").reshape(ncores * NS, 32, D)
    lat_p = cat("lat_p")[None]; kr_p = cat("kr_p")[None]
    fk_p = cat("fk_p").reshape(1, ncores, LP, 8, 64); fv_p = cat("fv_p").reshape(1, ncores, LP, 8, 64)
    lf_p = cat("lf_p")[None]
    lat_s = cat("lat_s").reshape(1, ncores * NS, 32, 256); kr_s = cat("kr_s").reshape(1, ncores * NS, 32, 32)
    fk_s = cat("fk_s").reshape(1, ncores * NS, 32, 8, 64); fv_s = cat("fv_s").reshape(1, ncores * NS, 32, 8, 64)
    lf_s = cat("lf_s").reshape(1, ncores * NS, 32, 8)
    return (y_p, y_s, lat_p, kr_p, fk_p, fv_p, lf_p, lat_s, kr_s, fk_s, fv_s, lf_s)


def kernel(**inp):
    ncores = 8
    B, S, _ = inp["x_prompt"].shape
    PAST = inp["cache_mla_latent"].shape[2]
    NS = inp["x_sample"].shape[0] // ncores
    assert B == ncores
    nc = build(S, PAST, NS)
    maps = make_in_maps(inp, ncores, S, PAST, NS)
    res = run_bass_kernel_spmd(nc, maps, core_ids=list(range(ncores)))
    return assemble(res.results, ncores, S, NS)
```

```python
from contextlib import ExitStack
import numpy as np
import concourse.bass as bass
import concourse.mybir as mybir
from concourse.bass_utils import run_bass_kernel_spmd

F32 = mybir.dt.float32
BF16 = mybir.dt.bfloat16
AF = mybir.ActivationFunctionType
ALU = mybir.AluOpType

D = 1024
DIN = 2216
DFF = 2816
NMETA = 16
EPS = 1e-6
MLA_SCALE = 96 ** -0.5
FOX_SCALE = 0.125
NEG = -30000.0
COMPUTE = ("pe", "act", "dve", "pool")


class Prog:
    uid = 0

    def __init__(self):
        self.ins = []

    def op(self, eng, fn, r=(), w=()):
        self.ins.append((eng, fn, tuple(r), tuple(w), None))

    def dma(self, eng, fn, r=(), w=(), tag=None):
        assert tag is not None
        self.ins.append((eng, fn, tuple(r), tuple(w), tag))

    def analyze(self):
        ins = self.ins
        n = len(ins)
        last_w, readers = {}, {}
        need = [False] * n
        waits = [None] * n
        for i, (eng, fn, R, W, tag) in enumerate(ins):
            d = {}
            for k in R:
                j = last_w.get(k)
                if j is not None:
                    d[j] = True
            for k in W:
                j = last_w.get(k)
                if j is not None and j not in d:
                    d[j] = d.get(j, False)
                for j in readers.get(k, ()):
                    if j != i and j not in d:
                        d[j] = False
            for k in R:
                readers.setdefault(k, []).append(i)
            for k in W:
                last_w[k] = i
                readers[k] = []
            wl = []
            for j, raw in d.items():
                ej, tj = ins[j][0], ins[j][4]
                if tj is None:
                    if ej == "pe" and eng == "pe":
                        continue
                    need[j] = True
                wl.append(j)
            waits[i] = wl
        cnt, val, semkey = {}, [0] * n, [None] * n
        for i in range(n):
            eng, tag = ins[i][0], ins[i][4]
            if tag is not None:
                key = ("d", tag)
                cnt[key] = cnt.get(key, 0) + 16
            elif need[i]:
                key = ("e", eng)
                cnt[key] = cnt.get(key, 0) + 1
            else:
                continue
            val[i] = cnt[key]
            semkey[i] = key
        self.need, self.waits, self.val, self.semkey = need, waits, val, semkey
        return cnt

    def emit(self, nc, final_tags=None):
        cnt = self.analyze()
        ins = self.ins
        with nc.cleanup_on_exit():
          with ExitStack() as st:
            sems = {}
            for idx, k in enumerate(cnt.keys()):
                sems[k] = nc.alloc_semaphore(name="s%d_%d_%s" % (Prog.uid, idx, str(k[1])[:12]))
            Prog.uid += 1
            block = st.enter_context(nc.Block())
            per_eng = {}
            for i, rec in enumerate(ins):
                per_eng.setdefault(rec[0], []).append(i)
            if final_tags is None:
                final_tags = [k[1] for k in cnt if k[0] == "d"]

            def run(name, e):
                waited = {}
                for i in per_eng.get(name, []):
                    fn, tag = ins[i][1], ins[i][4]
                    req = {}
                    for j in self.waits[i]:
                        k, v = self.semkey[j], self.val[j]
                        if v > req.get(k, 0):
                            req[k] = v
                    for k, v in req.items():
                        if waited.get(k, 0) >= v:
                            continue
                        e.wait_ge(sems[k], v)
                        waited[k] = v
                    bi = fn(e)
                    if tag is not None:
                        bi.then_inc(sems[("d", tag)], 16)
                    elif self.need[i]:
                        bi.then_inc(sems[("e", name)], 1)
                if name == "sp":
                    for t in final_tags:
                        e.wait_ge(sems[("d", t)], cnt[("d", t)])

            block.tensor(lambda e: run("pe", e))
            block.scalar(lambda e: run("act", e))
            block.vector(lambda e: run("dve", e))
            block.gpsimd(lambda e: run("pool", e))
            block.sync(lambda e: run("sp", e))
          nc.all_engine_barrier()


class Ring:
    def __init__(self, alloc, name, shape, dt, n):
        self.t = [alloc("%s%d" % (name, i), shape, dt) for i in range(n)]
        self.k = ["%s%d" % (name, i) for i in range(n)]
        self.n = n
        self.i = -1

    def next(self):
        self.i += 1
        s = self.i % self.n
        return self.t[s], self.k[s]


class Em:
    def __init__(self, P):
        self.P = P

    def mm(self, out, lhsT, rhs, start, stop, r, w):
        self.P.op("pe", lambda e: e.matmul(out, lhsT=lhsT, rhs=rhs, start=start, stop=stop), r, w)

    def tr(self, out, in_, ident, r, w):
        self.P.op("pe", lambda e: e.transpose(out=out, in_=in_, identity=ident), r, w)

    def act(self, out, in_, func, r, w, bias=None, scale=None, accum=None):
        kw = {}
        if bias is not None:
            kw["bias"] = bias
        if scale is not None:
            kw["scale"] = scale
        if accum is not None:
            kw["accum_out"] = accum
        self.P.op("act", lambda e: e.activation(out=out, in_=in_, func=func, **kw), r, w)

    def cp(self, eng, out, in_, r, w):
        if eng == "act":
            self.P.op("act", lambda e: e.copy(out=out, in_=in_), r, w)
        else:
            self.P.op(eng, lambda e: e.tensor_copy(out=out, in_=in_), r, w)

    def tt(self, eng, out, in0, in1, op, r, w):
        self.P.op(eng, lambda e: e.tensor_tensor(out=out, in0=in0, in1=in1, op=op), r, w)

    def ts(self, eng, out, in0, s1, s2, op0, op1, r, w):
        if s2 is None:
            self.P.op(eng, lambda e: e.tensor_scalar(out=out, in0=in0, scalar1=s1, scalar2=None, op0=op0), r, w)
        else:
            self.P.op(eng, lambda e: e.tensor_scalar(out=out, in0=in0, scalar1=s1, scalar2=s2, op0=op0, op1=op1), r, w)

    def stt(self, out, in0, scalar, in1, op0, op1, r, w):
        self.P.op("dve", lambda e: e.scalar_tensor_tensor(out=out, in0=in0, scalar=scalar, in1=in1, op0=op0, op1=op1), r, w)

    def memset(self, eng, ap, v, w):
        self.P.op(eng, lambda e: e.memset(ap, v), (), w)

    def dma(self, eng, out, in_, r, w, tag):
        self.P.dma(eng, lambda e: e.dma_start(out=out, in_=in_), r, w, tag)


class Job:
    pass


def build(S, PAST, NS):
    import os
    nc = bass.Bass("TRN2", target_bir_lowering=False)
    NBF = S // 128
    QW_P = 512 if S % 512 == 0 else 128
    NBC = PAST // 128
    LP = NMETA + S

    def din(name, shape):
        return nc.dram_tensor(name, list(shape), F32, kind="ExternalInput").ap()

    def dout(name, shape):
        return nc.dram_tensor(name, list(shape), F32, kind="ExternalOutput").ap()

    def dscr(name, shape, dt=BF16):
        return nc.dram_tensor(name, list(shape), dt, kind="Internal").ap()

    xp = din("xp", [S, D]); meta = din("meta", [NMETA, D]); xs = din("xs", [NS, 32, D])
    c_lat = din("c_lat", [NS, PAST, 256]); c_kr = din("c_kr", [NS, PAST, 32])
    c_fk = din("c_fk", [NS, PAST, 512]); c_fv = din("c_fv", [NS, PAST, 512]); c_lf = din("c_lf", [NS, PAST, 8])
    w_in = din("w_in", [D, DIN]); w_uq = din("w_uq", [384, 768]); w_ukv = din("w_ukv", [256, 1024])
    w_out = din("w_out", [D, D]); w_gate = din("w_gate", [D, DFF]); w_up = din("w_up", [D, DFF]); w_down = din("w_down", [DFF, D])
    g_mix = din("g_mix", [1, D]); b_forget = din("b_forget", [1, 8]); g_q = din("g_q", [1, 384]); g_kv = din("g_kv", [1, 256])
    g_ffn = din("g_ffn", [1, D]); g_fin = din("g_fin", [1, D])
    c_ident = din("c_ident", [128, 128])
    c_tri = din("c_tri", [3, 128, 128])
    c_ones = din("c_ones", [3, 128, 128])
    c_mask = din("c_mask", [2, 128, 128])
    NPB = NBF + 1
    c_ropep = din("c_ropep", [128, NPB, 64])
    c_ropes = din("c_ropes", [128, 1, 64])

    y_p = dout("y_p", [S, D]); lat_p = dout("lat_p", [LP, 256]); kr_p = dout("kr_p", [LP, 32])
    fk_p = dout("fk_p", [LP, 512]); fv_p = dout("fv_p", [LP, 512]); lf_p = dout("lf_p", [LP, 8])
    y_s = dout("y_s", [NS, 32, D]); lat_s = dout("lat_s", [NS, 32, 256]); kr_s = dout("kr_s", [NS, 32, 32])
    fk_s = dout("fk_s", [NS, 32, 512]); fv_s = dout("fv_s", [NS, 32, 512]); lf_s = dout("lf_s", [NS, 32, 8])

    jobs = []
    jp = Job()
    jp.name = "p"; jp.nbc = 0; jp.cache = None
    jp.blocks = [dict(src=meta, r0=0, nv=NMETA, q=False, tri=1, rope=(c_ropep, 0), orow=0)]
    for b in range(NBF):
        jp.blocks.append(dict(src=xp, r0=b * 128, nv=128, q=True, tri=0, rope=(c_ropep, b + 1), orow=NMETA + b * 128))
    jp.outs = (lat_p, kr_p, fk_p, fv_p, lf_p); jp.y = y_p; jp.xq = xp
    jp.qw = QW_P; jp.lq = S; jp.nbk = NBF + 1; jp.nvq = 128
    jp.mla_mask = 1
    jobs.append(jp)
    for s in range(NS):
        j = Job()
        j.name = "s%d" % s; j.nbc = NBC
        j.cache = (c_lat[s], c_kr[s], c_fk[s], c_fv[s], c_lf[s])
        j.blocks = [dict(src=xs[s], r0=0, nv=32, q=True, tri=2, rope=(c_ropes, 0), orow=0)]
        j.outs = (lat_s[s], kr_s[s], fk_s[s], fv_s[s], lf_s[s]); j.y = y_s[s]; j.xq = xs[s]
        j.qw = 128; j.lq = 128; j.nbk = NBC + 1; j.nvq = 32
        j.mla_mask = None
        jobs.append(j)
    for j in jobs:
        lk = j.nbk * 128
        n = j.name
        j.KTn = dscr("KTn_" + n, [8, 64, lk]); j.KTr = dscr("KTr_" + n, [32, lk]); j.KTf = (nc.dram_tensor("KTf_" + n, [8, 64, lk], BF16, kind="ExternalOutput").ap() if (os.environ.get("K_DBG") and n == "p") else dscr("KTf_" + n, [8, 64, lk]))
        j.Vm = dscr("Vm_" + n, [8, 128, j.nbk, 64]); j.Vf = dscr("Vf_" + n, [8, 128, j.nbk, 64])
        j.QTm = dscr("QTm_" + n, [8, 96, j.lq]); j.QTf = (nc.dram_tensor("QTf_" + n, [8, 65, j.lq], BF16, kind="ExternalOutput").ap() if (os.environ.get("K_DBG") and n == "p") else dscr("QTf_" + n, [8, 65, j.lq]))
        j.MIX = (nc.dram_tensor("MIX_" + n, [D, j.lq], BF16, kind="ExternalOutput").ap() if (os.environ.get("K_DBG") and n == "p") else dscr("MIX_" + n, [D, j.lq])); j.NCUM = dscr("NCUM_" + n, [128, j.nbk, 8], F32)
    LKMAX = max(j.nbk for j in jobs) * 128
    LQMAX = max(j.lq for j in jobs)
    NBKMAX = LKMAX // 128

    import os
    PH = os.environ.get('K_PHASES', 'ABC')
    with ExitStack() as st:
        sb = lambda name, shape, dt: st.enter_context(nc.sbuf_tensor("A_" + name, shape, dt))
        psb = lambda name, shape, dt: st.enter_context(nc.psum_tensor("A_" + name, shape, dt))
        P = Prog(); E = Em(P)
        STQ = os.environ.get('K_STQ', 'sp')
        win = sb("win", [128, 8, DIN], BF16)
        wuq = sb("wuq", [128, 3, 768], BF16)
        wuk = sb("wuk", [128, 2, 8, 64], BF16)
        wuv = sb("wuv", [128, 2, 8, 64], BF16)
        identf = sb("identf", [128, 128], F32); identb = sb("identb", [128, 128], BF16)
        tri = sb("tri", [128, 3, 128], F32); ones3 = sb("ones3", [128, 3, 128], F32)
        gmix = sb("gmix", [128, D], F32); gq = sb("gq", [128, 384], F32); gkv = sb("gkv", [128, 256], F32)
        bfg = sb("bfg", [128, 8], F32)
        ropep = sb("ropep", [128, NPB, 64], F32); ropes = sb("ropes", [128, 1, 64], F32)
        carry = sb("carry", [128, 8], F32)
        ncum = sb("ncum", [128, NBKMAX, 8], F32)
        CQ, CKV, CKR, CLG, CFQ, CFK, CFV = 0, 384, 640, 672, 680, 1192, 1704
        src_off = dict(cq=0, ckv=384, kr=640, fq=672, fk=1184, fv=1696, lg=2208)
        stg_r = Ring(sb, "xt", [128, D], F32, 2)
        _ce = [0]

        def wload(dst_ap, src_ap, wkey, ncols):
            stg, sk = stg_r.next()
            E.dma("sp", stg[:, 0:ncols], src_ap, [], [sk], sk)
            eng = "act" if _ce[0] % 2 == 0 else "dve"
            _ce[0] += 1
            E.cp(eng, dst_ap, stg[:, 0:ncols], [sk], [wkey])
        for c in range(8):
            for (dst, so, wd_) in ((CQ, 0, 384), (CKV, 384, 256), (CKR, 640, 32), (CLG, 2208, 8), (CFQ, 672, 512), (CFK, 1184, 512), (CFV, 1696, 512)):
                wload(win[:, c, dst:dst + wd_], w_in[c * 128:(c + 1) * 128, so:so + wd_], "win", wd_)
        for c in range(3):
            wload(wuq[:, c, :], w_uq[c * 128:(c + 1) * 128, :], "wuq", 768)
        for c in range(2):
            stg, sk = stg_r.next()
            E.dma("sp", stg[:, :], w_ukv[c * 128:(c + 1) * 128, :], [], [sk], sk)
            sv = stg[:, :].rearrange("p (h t d) -> p h t d", h=8, t=2)
            E.cp("act", wuk[:, c], sv[:, :, 0, :], [sk], ["wuk"])
            E.cp("dve", wuv[:, c], sv[:, :, 1, :], [sk], ["wuv"])
        E.dma("sp", identf[:], c_ident[:, :], [], ["identf"], "w_id")
        E.cp("dve", identb[:], identf[:], ["identf"], ["identb"])
        E.dma("sp", tri[:], c_tri.rearrange("a p n -> p a n"), [], ["tri"], "w_tri")
        E.dma("sp", ones3[:], c_ones.rearrange("a p n -> p a n"), [], ["ones3"], "w_ones")
        E.dma("sp", gmix[:], g_mix.partition_broadcast(128), [], ["gmix"], "w_gmix")
        E.dma("sp", gq[:], g_q.partition_broadcast(128), [], ["gq"], "w_gq")
        E.dma("sp", gkv[:], g_kv.partition_broadcast(128), [], ["gkv"], "w_gkv")
        E.dma("sp", bfg[:], b_forget.partition_broadcast(128), [], ["bfg"], "w_bfg")
        E.dma("sp", ropep[:], c_ropep[:, :, :], [], ["ropep"], "w_ropep")
        E.dma("sp", ropes[:], c_ropes[:, :, :], [], ["ropes"], "w_ropes")
        rope_sb = {id(c_ropep): (ropep, "ropep"), id(c_ropes): (ropes, "ropes")}

        xt_r = stg_r
        xpart = sb("xpart", [128, D], F32)
        E.memset("dve", xpart[:], 0.0, ["xpart"])
        junk_r = Ring(sb, "junk", [128, D], BF16, 2)
        st_r = Ring(sb, "stat", [128, 8], F32, 4)
        for _i in range(4):
            E.memset("dve", st_r.t[_i][:], 0.0, [st_r.k[_i] + "z"])
        hb_r = Ring(sb, "hb", [128, D], BF16, 2)
        hT_r = Ring(sb, "hT", [128, 8, 128], BF16, 2)
        lat32_r = Ring(sb, "lat32", [128, 256], F32, 3)
        kr32_r = Ring(sb, "kr32", [128, 32], F32, 3)
        krt_r = Ring(sb, "krt", [128, 64], F32, 2)
        lf32_r = Ring(sb, "lf32", [128, 8], F32, 4)
        lgt_r = Ring(sb, "lgt", [128, 8], F32, 2)
        fk32_r = Ring(sb, "fk32", [128, 512], F32, 3)
        fv32_r = Ring(sb, "fv32", [128, 512], F32, 3)
        lkb_r = Ring(sb, "lkb", [128, 352], BF16, 2)
        for _i in range(2):
            E.memset("dve", lkb_r.t[_i][:, 256:320], 0.0, [lkb_r.k[_i] + "p"])
        fkb_r = Ring(sb, "fkb", [128, 512], BF16, 2)
        vfb_r = Ring(sb, "vfb", [128, 512], BF16, 2)
        vmb_r = Ring(sb, "vmb", [128, 512], BF16, 2)
        cqn_r = Ring(sb, "cqn", [128, 384], BF16, 2)
        fqb_r = Ring(sb, "fqb", [128, 512], BF16, 2)
        cqT_r = Ring(sb, "cqT", [128, 3, 128], BF16, 2)
        qb_r = Ring(sb, "qb", [128, 768], BF16, 2)
        qrt_r = Ring(sb, "qrt", [128, 8, 64], F32, 2)
        latT_r = Ring(sb, "latT", [128, 3, 128], BF16, 2)
        ktn_r = Ring(sb, "ktn", [64, 8, 128], BF16, 2)
        ktf_r = Ring(sb, "ktf", [64, 8, 128], BF16, 2)
        qtm_r = Ring(sb, "qtm", [96, 8, 128], BF16, 2)
        qtf_r = Ring(sb, "qtf", [64, 8, 128], BF16, 2)
        cum32_r = Ring(sb, "cum32", [128, 8], F32, 2)
        cum8_r = Ring(sb, "cum8", [128, 8], BF16, 2)
        cumT_r = Ring(sb, "cumT", [8, 128], BF16, 2)
        pT_r = Ring(psb, "pT", [128, 1024], BF16, 2)
        pG_r = Ring(psb, "pG", [128, 512], F32, 5)

        def rstd_from(ss_ap, n, stt, stk):
            E.cp("act", stt[:, 6:7], stt[:, 7:8], [stk + "a", stk + "z"], [stk + "a2"])
            E.act(stt[:, 1:2], ss_ap, AF.Ln, [stk + "a", stk + "a2"], [stk + "b"], bias=EPS, scale=1.0 / n)
            E.act(stt[:, 2:3], stt[:, 1:2], AF.Exp, [stk + "b"], [stk + "c"], scale=-0.5)
            return stt[:, 2:3], stk + "c"

        def rope32(out32, ok, src, sk, tab, tk, tmp, tmk):
            E.tt("dve", tmp[:, 0:32], src, tab[:, 0:32], ALU.mult, [sk, tk], [tmk + "a"])
            E.tt("dve", tmp[:, 32:48], src[:, 16:32], tab[:, 32:48], ALU.mult, [sk, tk], [tmk + "b"])
            E.tt("dve", tmp[:, 48:64], src[:, 0:16], tab[:, 48:64], ALU.mult, [sk, tk], [tmk + "c"])
            E.tt("dve", out32, tmp[:, 0:32], tmp[:, 32:64], ALU.add, [tmk + "a", tmk + "b", tmk + "c"], [ok])

        for job in jobs:
            lat_o, kr_o, fk_o, fv_o, lf_o = job.outs
            jn = job.name
            E.memset("dve", carry[:], 0.0, ["carry"])
            kb = 0
            qblk = 0
            allblocks = [("c", i) for i in range(min(job.nbc, int(os.environ.get("K_NBC_LIMIT", "9999"))))] + [("n", b) for b in job.blocks]
            def issue_loads(job_, kind_, bi_):
                d = dict(job=job_, kind=kind_, bi=bi_)
                if kind_ == "c":
                    cl, ckr_, cfk_, cfv_, clf_ = job_.cache
                    r0 = bi_ * 128
                    d["lat32"] = lat32_r.next(); d["kr32"] = kr32_r.next()
                    d["fk32"] = fk32_r.next(); d["fv32"] = fv32_r.next(); d["lf32"] = lf32_r.next()
                    E.dma("sp", d["lat32"][0][:], cl[r0:r0 + 128, :], [], [d["lat32"][1]], d["lat32"][1])
                    E.dma("sp", d["kr32"][0][:], ckr_[r0:r0 + 128, :], [], [d["kr32"][1]], d["kr32"][1])
                    E.dma("sp", d["fk32"][0][:], cfk_[r0:r0 + 128, :], [], [d["fk32"][1]], d["fk32"][1])
                    E.dma("sp", d["fv32"][0][:], cfv_[r0:r0 + 128, :], [], [d["fv32"][1]], d["fv32"][1])
                    E.dma("sp", d["lf32"][0][:], clf_[r0:r0 + 128, :], [], [d["lf32"][1]], d["lf32"][1])
                else:
                    blk_ = bi_
                    nv_ = blk_["nv"]
                    if nv_ == 128:
                        xt_, xtk_ = xt_r.next()
                        E.dma("sp", xt_[:], blk_["src"][blk_["r0"]:blk_["r0"] + 128, :], [], [xtk_], xtk_)
                    else:
                        xt_, xtk_ = xpart, "xpart"
                        E.dma("sp", xt_[0:nv_, :], blk_["src"][blk_["r0"]:blk_["r0"] + nv_, :], [], [xtk_], "xpartd")
                    d["xt"] = (xt_, xtk_)
                return d

            if job is jobs[0]:
                flat = [(j_, k_, b_) for j_ in jobs for (k_, b_) in ([("c", i) for i in range(j_.nbc)] + [("n", b) for b in j_.blocks])]
                flat_i = [0]
                pre_ld = [issue_loads(*flat[0])]
            for kind, bi in allblocks:
                LD = pre_ld[0]
                flat_i[0] += 1
                if flat_i[0] < len(flat):
                    pre_ld[0] = issue_loads(*flat[flat_i[0]])
                lkb, lkbk = lkb_r.next(); fkb, fkbk = fkb_r.next(); vfb, vfbk = vfb_r.next()
                hasq = False
                if kind == "c":
                    lat32, lat32k = LD["lat32"]; kr32, kr32k = LD["kr32"]
                    fk32, fk32k = LD["fk32"]; fv32, fv32k = LD["fv32"]; lf32, lf32k = LD["lf32"]
                    E.cp("act", lkb[:, 0:256], lat32[:], [lat32k], [lkbk + "l"])
                    E.cp("dve", lkb[:, 320:352], kr32[:], [kr32k], [lkbk + "r"])
                    E.cp("act", fkb[:], fk32[:], [fk32k], [fkbk])
                    E.cp("dve", vfb[:], fv32[:], [fv32k], [vfbk])
                    tri_i = 0
                else:
                    lf32, lf32k = lf32_r.next()
                    blk = bi
                    nv = blk["nv"]; hasq = blk["q"]; tri_i = blk["tri"]
                    rtab_t, rtab_k = rope_sb[id(blk["rope"][0])]
                    rtab = rtab_t[:, blk["rope"][1], :]
                    orow = blk["orow"]
                    xt, xtk = LD["xt"]
                    junk, jk = junk_r.next(); stt, stk = st_r.next()
                    E.act(junk[:], xt[:], AF.Square, [xtk, stk + "z"], [jk, stk + "a"], accum=stt[:, 0:1])
                    rs, rsk = rstd_from(stt[:, 0:1], D, stt, stk)
                    hb, hbk = hb_r.next()
                    E.stt(hb[:], xt[:], rs, gmix[:], ALU.mult, ALU.mult, [xtk, rsk, "gmix"], [hbk])
                    pT, pTk = pT_r.next(); hT, hTk = hT_r.next()
                    for c in range(8):
                        E.tr(pT[:, c * 128:(c + 1) * 128], hb[:, c * 128:(c + 1) * 128], identb[:], [hbk, "identb"], [pTk])
                    E.cp("act", hT[:].rearrange("p c t -> p (c t)"), pT[:, :], [pTk], [hTk])
                    pg, pgk = pG_r.next()
                    for c in range(8):
                        E.mm(pg[:, 0:296], hT[:, c, :], win[:, c, CKV:CKV + 296], c == 0, c == 7, [hTk, "win"], [pgk])
                    stt2, stk2 = st_r.next(); junk2, jk2 = junk_r.next()
                    E.act(junk2[:, 0:256], pg[:, 0:256], AF.Square, [pgk, stk2 + "z"], [jk2, stk2 + "a"], accum=stt2[:, 0:1])
                    rs2, rs2k = rstd_from(stt2[:, 0:1], 256, stt2, stk2)
                    lat32, lat32k = lat32_r.next()
                    E.stt(lat32[:], pg[:, 0:256], rs2, gkv[:], ALU.mult, ALU.mult, [pgk, rs2k, "gkv"], [lat32k])
                    E.dma(STQ, lat_o[orow:orow + nv, :], lat32[0:nv, :], [lat32k], [], lat32k + "o")
                    E.cp("act", lkb[:, 0:256], lat32[:], [lat32k], [lkbk + "l"])
                    kr32, kr32k = kr32_r.next(); krt, krtk = krt_r.next()
                    rope32(kr32[:], kr32k, pg[:, 256:288], pgk, rtab, rtab_k, krt, krtk)
                    E.dma(STQ, kr_o[orow:orow + nv, :], kr32[0:nv, :], [kr32k], [], kr32k + "o")
                    E.cp("act", lkb[:, 320:352], kr32[:], [kr32k], [lkbk + "r"])
                    lgt, lgtk = lgt_r.next()
                    E.tt("dve", lgt[:], pg[:, 288:296], bfg[:], ALU.add, [pgk, "bfg"], [lgtk])
                    E.act(lgt[:], lgt[:], AF.Exp, [lgtk], [lgtk], scale=-1.0)
                    E.act(lgt[:], lgt[:], AF.Ln, [lgtk], [lgtk], bias=1.0)
                    E.ts("dve", lf32[:], lgt[:], -1.0, None, ALU.mult, None, [lgtk], [lf32k])
                    E.dma(STQ, lf_o[orow:orow + nv, :], lf32[0:nv, :], [lf32k], [], lf32k + "o")
                    pg, pgk = pG_r.next()
                    for c in range(8):
                        E.mm(pg[:, :], hT[:, c, :], win[:, c, CFK:CFK + 512], c == 0, c == 7, [hTk, "win"], [pgk])
                    fk32, fk32k = fk32_r.next()
                    E.cp("act", fk32[:], pg[:, :], [pgk], [fk32k])
                    E.dma(STQ, fk_o[orow:orow + nv, :], fk32[0:nv, :], [fk32k], [], fk32k + "o")
                    E.cp("act", fkb[:], fk32[:], [fk32k], [fkbk])
                    pg, pgk = pG_r.next()
                    for c in range(8):
                        E.mm(pg[:, :], hT[:, c, :], win[:, c, CFV:CFV + 512], c == 0, c == 7, [hTk, "win"], [pgk])
                    fv32, fv32k = fv32_r.next()
                    E.cp("act", fv32[:], pg[:, :], [pgk], [fv32k])
                    E.dma(STQ, fv_o[orow:orow + nv, :], fv32[0:nv, :], [fv32k], [], fv32k + "o")
                    E.cp("act", vfb[:], fv32[:], [fv32k], [vfbk])
                    if hasq:
                        pg, pgk = pG_r.next()
                        for c in range(8):
                            E.mm(pg[:, 0:384], hT[:, c, :], win[:, c, CQ:CQ + 384], c == 0, c == 7, [hTk, "win"], [pgk])
                        stt3, stk3 = st_r.next(); junk3, jk3 = junk_r.next()
                        E.act(junk3[:, 0:384], pg[:, 0:384], AF.Square, [pgk, stk3 + "z"], [jk3, stk3 + "a"], accum=stt3[:, 0:1])
                        rs3, rs3k = rstd_from(stt3[:, 0:1], 384, stt3, stk3)
                        cqn, cqnk = cqn_r.next()
                        E.stt(cqn[:], pg[:, 0:384], rs3, gq[:], ALU.mult, ALU.mult, [pgk, rs3k, "gq"], [cqnk])
                        pg, pgk = pG_r.next()
                        for c in range(8):
                            E.mm(pg[:, :], hT[:, c, :], win[:, c, CFQ:CFQ + 512], c == 0, c == 7, [hTk, "win"], [pgk])
                        fqb, fqbk = fqb_r.next()
                        E.cp("act", fqb[:], pg[:, :], [pgk], [fqbk])
                t0 = kb * 128
                pg, pgk = pG_r.next()
                E.mm(pg[:, 0:8], tri[:, tri_i, :], lf32[:], True, True, ["tri", lf32k], [pgk])
                E.mm(pg[:, 8:16], ones3[:, tri_i, :], lf32[:], True, True, ["ones3", lf32k], [pgk])
                cum32, cum32k = cum32_r.next()
                E.tt("dve", cum32[:], pg[:, 0:8], carry[:], ALU.add, [pgk, "carry"], [cum32k])
                E.tt("dve", carry[:], pg[:, 8:16], carry[:], ALU.add, [pgk, "carry"], ["carry"])
                E.ts("dve", ncum[:, kb, :], cum32[:], -1.0, None, ALU.mult, None, [cum32k], ["ncum"])
                pT, pTk = pT_r.next(); latT, latTk = latT_r.next()
                E.tr(pT[:, 0:128], lkb[:, 0:128], identb[:], [lkbk + "l", "identb"], [pTk])
                E.tr(pT[:, 128:256], lkb[:, 128:256], identb[:], [lkbk + "l", "identb"], [pTk])
                E.tr(pT[:, 256:384], lkb[:, 224:352], identb[:], [lkbk + "l", lkbk + "r", lkbk + "p", "identb"], [pTk])
                E.cp("dve", latT[:, 0:2, :].rearrange("p c t -> p (c t)"), pT[:, 0:256], [pTk], [latTk + "l"])
                E.cp("dve", latT[96:128, 2, :], pT[96:128, 256:384], [pTk], [latTk + "r"])
                E.dma(STQ, job.KTr[:, t0:t0 + 128], latT[96:128, 2, :], [latTk + "r"], ["dram_KTr_" + jn], latTk + "ro")
                ktn, ktnk = ktn_r.next()
                for half in range(2):
                    pg, pgk = pG_r.next()
                    pgv = pg[0:64, :].rearrange("p (h t) -> p h t", h=4)
                    for hh in range(4):
                        h = half * 4 + hh
                        for c in range(2):
                            E.mm(pgv[:, hh, :], wuk[:, c, h, :], latT[:, c, :], c == 0, c == 1, ["wuk", latTk + "l"], [pgk])
                    E.cp("act" if half == 0 else "dve", ktn[:, half * 4:(half + 1) * 4, :], pgv, [pgk], [ktnk + str(half)])
                E.dma(STQ, job.KTn[:, :, t0:t0 + 128].rearrange("h p t -> p h t"), ktn[:], [ktnk + "0", ktnk + "1"], ["dram_KTn_" + jn], ktnk + "o")
                pg, pgk = pG_r.next()
                for c in range(2):
                    E.mm(pg[:, :], latT[:, c, :], wuv[:, c].rearrange("p h d -> p (h d)"), c == 0, c == 1, [latTk + "l", "wuv"], [pgk])
                vmb, vmbk = vmb_r.next()
                E.cp("act", vmb[:], pg[:, :], [pgk], [vmbk])
                E.dma(STQ, job.Vm[:, :, kb, :].rearrange("h p d -> p h d"), vmb[:].rearrange("p (h d) -> p h d", h=8), [vmbk], ["dram_Vm_" + jn], vmbk + "o")
                E.dma(STQ, job.Vf[:, :, kb, :].rearrange("h p d -> p h d"), vfb[:].rearrange("p (h d) -> p h d", h=8), [vfbk], ["dram_Vf_" + jn], vfbk + "o")
                pT, pTk = pT_r.next(); ktf, ktfk = ktf_r.next()
                for h in range(8):
                    E.tr(pT[0:64, h * 128:(h + 1) * 128], fkb[:, h * 64:(h + 1) * 64], identb[:], [fkbk, "identb"], [pTk])
                E.cp("dve", ktf[:].rearrange("p h t -> p (h t)"), pT[0:64, :], [pTk], [ktfk])
                E.dma(STQ, job.KTf[:, :, t0:t0 + 128].rearrange("h p t -> p h t"), ktf[:], [ktfk], ["dram_KTf_" + jn], ktfk + "o")
                if hasq:
                    q0 = qblk * 128
                    pT, pTk = pT_r.next(); cqT, cqTk = cqT_r.next()
                    for c in range(3):
                        E.tr(pT[:, c * 128:(c + 1) * 128], cqn[:, c * 128:(c + 1) * 128], identb[:], [cqnk, "identb"], [pTk])
                    E.cp("act", cqT[:].rearrange("p c t -> p (c t)"), pT[:, 0:384], [pTk], [cqTk])
                    qb, qbk = qb_r.next(); qrt, qrtk = qrt_r.next()
                    for half in range(2):
                        pg, pgk = pG_r.next()
                        for c in range(3):
                            E.mm(pg[:, 0:384], cqT[:, c, :], wuq[:, c, half * 384:(half + 1) * 384], c == 0, c == 2, [cqTk, "wuq"], [pgk])
                        pv = pg[:, 0:384].rearrange("p (h d) -> p h d", h=4)
                        qv = qb[:, half * 384:(half + 1) * 384].rearrange("p (h d) -> p h d", h=4)
                        E.cp("dve", qv[:, :, 0:64], pv[:, :, 0:64], [pgk], [qbk + "n%d" % half])
                        tb = rtab.unsqueeze(1)
                        tmp = qrt[:, half * 4:(half + 1) * 4, :]
                        tk = qrtk + str(half)
                        E.tt("dve", tmp[:, :, 0:32], pv[:, :, 64:96], tb[:, :, 0:32].broadcast_to([128, 4, 32]), ALU.mult, [pgk, rtab_k], [tk + "a"])
                        E.tt("dve", tmp[:, :, 32:48], pv[:, :, 80:96], tb[:, :, 32:48].broadcast_to([128, 4, 16]), ALU.mult, [pgk, rtab_k], [tk + "b"])
                        E.tt("dve", tmp[:, :, 48:64], pv[:, :, 64:80], tb[:, :, 48:64].broadcast_to([128, 4, 16]), ALU.mult, [pgk, rtab_k], [tk + "c"])
                        E.tt("dve", qv[:, :, 64:96], tmp[:, :, 0:32], tmp[:, :, 32:64], ALU.add, [tk + "a", tk + "b", tk + "c"], [qbk + "r%d" % half])
                    qkeys = [qbk + "n0", qbk + "n1", qbk + "r0", qbk + "r1"]
                    pT, pTk = pT_r.next(); qtm, qtmk = qtm_r.next()
                    for h in range(8):
                        E.tr(pT[0:96, h * 128:(h + 1) * 128], qb[:, h * 96:(h + 1) * 96], identb[:], qkeys + ["identb"], [pTk])
                    E.cp("dve", qtm[:].rearrange("p h t -> p (h t)"), pT[0:96, :], [pTk], [qtmk])
                    E.dma(STQ, job.QTm[:, :, q0:q0 + 128].rearrange("h p t -> p h t"), qtm[:], [qtmk], ["dram_QTm_" + jn], qtmk + "o")
                    pT, pTk = pT_r.next(); qtf, qtfk = qtf_r.next()
                    for h in range(8):
                        E.tr(pT[0:64, h * 128:(h + 1) * 128], fqb[:, h * 64:(h + 1) * 64], identb[:], [fqbk, "identb"], [pTk])
                    E.cp("act", qtf[:].rearrange("p h t -> p (h t)"), pT[0:64, :], [pTk], [qtfk])
                    E.dma(STQ, job.QTf[:, 0:64, q0:q0 + 128].rearrange("h p t -> p h t"), qtf[:], [qtfk], ["dram_QTf_" + jn], qtfk + "o")
                    cum8, cum8k = cum8_r.next(); cumT, cumTk = cumT_r.next()
                    E.ts("dve", cum8[:], cum32[:], 8.0, None, ALU.mult, None, [cum32k], [cum8k])
                    pT, pTk = pT_r.next()
                    E.tr(pT[0:8, 0:128], cum8[:, 0:8], identb[:], [cum8k, "identb"], [pTk])
                    E.cp("dve", cumT[:], pT[0:8, 0:128], [pTk], [cumTk])
                    E.dma(STQ, job.QTf[:, 64, q0:q0 + 128], cumT[:], [cumTk], ["dram_QTf_" + jn], cumTk + "o")
                    qblk += 1
                kb += 1
            E.dma(STQ, job.NCUM[:, :, :], ncum[:, 0:job.nbk, :], ["ncum"], [], "ncum_o")
        if 'A' in PH:
            _tr = int(os.environ.get('K_ATRUNC', '0'))
            if _tr:
                print('PHASE A n_ins', len(P.ins)); P.ins = P.ins[:_tr]
            P.emit(nc)

    with ExitStack() as st:
        sb = lambda name, shape, dt: st.enter_context(nc.sbuf_tensor("B_" + name, shape, dt))
        psb = lambda name, shape, dt: st.enter_context(nc.psum_tensor("B_" + name, shape, dt))
        P = Prog(); E = Em(P)
        identf = sb("identf", [128, 128], F32); identb = sb("identb", [128, 128], BF16)
        maskf = sb("maskf", [128, 2, 128], F32); maskb = sb("maskb", [128, 2, 128], BF16)
        onesr = sb("onesr", [65, 64], F32)
        E.dma("sp", identf[:], c_ident[:, :], [], ["identf"], "w_id")
        E.cp("dve", identb[:], identf[:], ["identf"], ["identb"])
        E.dma("sp", maskf[:], c_mask.rearrange("a p n -> p a n"), [], ["maskf"], "w_mask")
        E.cp("dve", maskb[:], maskf[:], ["maskf"], ["maskb"])
        E.memset("dve", onesr[:], 1.0, ["onesr"])
        ktm_r = Ring(sb, "ktm", [96, LKMAX], BF16, 2)
        ktf_r = Ring(sb, "ktf", [96, LKMAX], BF16, 2)
        v_r = Ring(sb, "vv", [128, NBKMAX, 128], BF16, 2)
        qm_r = Ring(sb, "qm", [96, LQMAX], BF16, 2)
        qf_r = Ring(sb, "qf", [96, LQMAX], BF16, 2)
        pt_r = Ring(sb, "pt", [128, 512], BF16, 5)
        osb_r = Ring(sb, "osb", [65, 512], F32, 2)
        rec_r = Ring(sb, "rec", [65, 512], F32, 2)
        mixo_r = Ring(sb, "mixo", [64, 512], BF16, 2)
        pS_r = Ring(psb, "pS", [128, 512], F32, 5)
        pO_r = Ring(psb, "pO", [128, 512], F32, 2)
        for i in range(2):
            E.memset("dve", ktf_r.t[i][64:96, :], 0.0, [ktf_r.k[i] + "1"])
            E.memset("dve", ktf_r.t[i][64:65, :], 1.0, [ktf_r.k[i] + "1"])
            E.memset("dve", qf_r.t[i][64:96, :], 0.0, [qf_r.k[i] + "z", qf_r.k[i]])
            E.memset("dve", v_r.t[i][:, :, 65:128], 0.0, [v_r.k[i] + "1"])

        ncum_r = Ring(sb, "ncumr", [128, NBKMAX, 8], F32, 2)

        def load_head(job, hd):
            nbk = job.nbk; lk = nbk * 128; lq = job.lq
            fox = hd >= 8
            h = hd % 8
            vt, vk = v_r.next()
            vkeys = []
            if fox:
                kt, ktk = ktf_r.next(); qt, qtk = qf_r.next(); KD = 96
                E.dma("sp", kt[0:64, 0:lk], job.KTf[h], [], [ktk], ktk)
                E.dma("sp", qt[0:65, 0:lq], job.QTf[h], [], [qtk], qtk)
                vsrc = job.Vf
                kr_keys = [ktk, ktk + "1", qtk + "z"]
                scale = FOX_SCALE
            else:
                kt, ktk = ktm_r.next(); qt, qtk = qm_r.next(); KD = 96
                E.dma("sp", kt[0:64, 0:lk], job.KTn[h], [], [ktk], ktk)
                E.dma("sp", kt[64:96, 0:lk], job.KTr[:, :], [], [ktk + "r"], ktk + "r")
                E.dma("sp", qt[0:96, 0:lq], job.QTm[h], [], [qtk], qtk)
                vsrc = job.Vm
                kr_keys = [ktk, ktk + "r"]
                scale = MLA_SCALE
            for b0_ in range(0, nbk, 16):
                b1_ = min(nbk, b0_ + 16)
                E.dma("sp", vt[:, b0_:b1_, 0:64], vsrc[h, :, b0_:b1_, :], [], [vk + "c%d" % b0_], vk + "_%d" % b0_)
                vkeys.append(vk + "c%d" % b0_)
            return dict(fox=fox, h=h, vt=vt, vk=vk, vkeys=vkeys, kt=kt, ktk=ktk, qt=qt, qtk=qtk, KD=KD, kr_keys=kr_keys, scale=scale)

        RS = dscr("RS_scr", [4, 512], F32)
        rs_cnt = [0]
        items = [(job, hd) for job in jobs for hd in range(16)]
        nxt_loaded = load_head(*items[0])
        for it_i, (job, hd) in enumerate(items):
            nbk = job.nbk; lk = nbk * 128; lq = job.lq; QW = job.qw
            if hd == 0:
                ncum, ncumk = ncum_r.next()
                E.dma("sp", ncum[:, 0:nbk, :], job.NCUM[:, :, :], [], [ncumk], ncumk)
                for i in range(2):
                    vt, vk = v_r.t[i], v_r.k[i]
                    E.memset("dve", vt[:, :, 64:65], 1.0, [vk + "1"])
                    pb_ = job.nbc if job.name != "p" else 0
                    nvb = job.blocks[0]["nv"]
                    E.memset("dve", vt[:, pb_, 64:65], 0.0, [vk + "1"])
                    E.memset("dve", vt[0:nvb, pb_, 64:65], 1.0, [vk + "1"])
            L_ = nxt_loaded
            if it_i + 1 < len(items):
                nxt_loaded = load_head(*items[it_i + 1])
            fox = L_["fox"]; h = L_["h"]; vt = L_["vt"]; vk = L_["vk"]; kt = L_["kt"]; ktk = L_["ktk"]
            qt = L_["qt"]; qtk = L_["qtk"]; KD = L_["KD"]; kr_keys = L_["kr_keys"]; scale = L_["scale"]; vkeys = L_["vkeys"]
            if True:
                nqt = lq // QW
                for t in range(nqt):
                    q0 = t * QW
                    if job.name == "p":
                        nfull = 1 + t * (QW // 128)
                        kbl = [(j, 0, None) for j in range(nfull)]
                        for jj in range(QW // 128):
                            kbl.append((nfull + jj, jj * 128, 0 if fox else 1))
                    else:
                        kbl = [(j, 0, None) for j in range(job.nbc)]
                        kbl.append((job.nbc, 0, 0 if fox else None))
                    pO, pOk = pO_r.next()
                    pend = []

                    def qk(idx):
                        j, c0, mk = kbl[idx]
                        pS, pSk = pS_r.next()
                        E.mm(pS[:, c0:QW], kt[0:KD, j * 128:(j + 1) * 128], qt[0:KD, q0 + c0:q0 + QW], True, mk is None, kr_keys + [qtk], [pSk])
                        if mk is not None:
                            E.mm(pS[:, c0:c0 + 128], identb[:, :], maskb[:, mk, :], False, True, ["identb", "maskb"], [pSk])
                        return pS, pSk

                    LA = 3
                    qq = [qk(i_) for i_ in range(min(LA, len(kbl)))]
                    for idx in range(len(kbl)):
                        j, c0, mk = kbl[idx]
                        pS, pSk = qq.pop(0)
                        if idx + LA < len(kbl):
                            qq.append(qk(idx + LA))
                        pt, ptk = pt_r.next()
                        if fox:
                            E.act(pt[:, c0:QW], pS[:, c0:QW], AF.Exp, [pSk, ncumk], [ptk], bias=ncum[:, j, h:h + 1], scale=scale)
                        else:
                            E.act(pt[:, c0:QW], pS[:, c0:QW], AF.Exp, [pSk], [ptk], scale=scale)
                        E.mm(pO[:, c0:QW], vt[:, j, :], pt[:, c0:QW], idx == 0, idx == len(kbl) - 1, vkeys + [vk + "1", ptk], [pOk])
                    osb, osbk = osb_r.next(); rec, reck = rec_r.next(); mixo, mixok = mixo_r.next()
                    E.cp("dve", osb[0:65, 0:QW], pO[0:65, 0:QW], [pOk], [osbk])
                    rsl = rs_cnt[0] % 4
                    rs_cnt[0] += 1
                    E.dma("sp", RS[rsl:rsl + 1, 0:QW], osb[64:65, 0:QW], [osbk], ["RS%d" % rsl], "rs_w%d" % rsl)
                    E.dma("sp", rec[0:64, 0:QW], RS[rsl:rsl + 1, 0:QW].partition_broadcast(64), ["RS%d" % rsl], [reck], "rs_r%d" % rsl)
                    E.P.op("dve", (lambda o, i: (lambda e: e.reciprocal(out=o, in_=i)))(rec[0:64, 0:QW], rec[0:64, 0:QW]), [reck], [reck])
                    E.tt("dve", mixo[0:64, 0:QW], osb[0:64, 0:QW], rec[0:64, 0:QW], ALU.mult, [osbk, reck], [mixok])
                    E.dma("sp", job.MIX[hd * 64:(hd + 1) * 64, q0:q0 + QW], mixo[0:64, 0:QW], [mixok], [], mixok + "o")
        if 'B' in PH:
            P.emit(nc)

    with ExitStack() as st:
        sb = lambda name, shape, dt: st.enter_context(nc.sbuf_tensor("C_" + name, shape, dt))
        psb = lambda name, shape, dt: st.enter_context(nc.psum_tensor("C_" + name, shape, dt))
        P = Prog(); E = Em(P)
        NFC = DFF // 128
        wo = sb("wo", [128, 8, D], BF16)
        wg = sb("wg", [128, 8, DFF], BF16)
        wu = sb("wu", [128, 8, DFF], BF16)
        wd = sb("wd", [128, NFC, D], BF16)
        identf = sb("identf", [128, 128], F32); identb = sb("identb", [128, 128], BF16)
        gffn = sb("gffn", [128, D], F32); gfin = sb("gfin", [128, D], F32)
        E.dma("sp", identf[:], c_ident[:, :], [], ["identf"], "w_id")
        E.cp("dve", identb[:], identf[:], ["identf"], ["identb"])
        E.dma("sp", gffn[:], g_ffn.partition_broadcast(128), [], ["gffn"], "w_gffn")
        E.dma("sp", gfin[:], g_fin.partition_broadcast(128), [], ["gfin"], "w_gfin")
        TWMAX = 256
        mix_r = Ring(sb, "mixt", [128, 8, TWMAX], BF16, 1)
        xl_r = Ring(sb, "xl", [128, D], F32, 2)
        _ce = [0]

        def wloadc(dst_ap, src_ap, wkey, ncols):
            stg, sk = xl_r.next()
            E.dma("sp", stg[:, 0:ncols], src_ap, [], [sk], sk)
            eng = "act" if _ce[0] % 2 == 0 else "dve"
            _ce[0] += 1
            E.cp(eng, dst_ap, stg[:, 0:ncols], [sk], [wkey])
        for c in range(8):
            wloadc(wo[:, c, :], w_out[c * 128:(c + 1) * 128, :], "wo", 1024)
        for c in range(8):
            for (c0, cw) in ((0, 1024), (1024, 1024), (2048, 768)):
                wloadc(wg[:, c, c0:c0 + cw], w_gate[c * 128:(c + 1) * 128, c0:c0 + cw], "wg", cw)
                wloadc(wu[:, c, c0:c0 + cw], w_up[c * 128:(c + 1) * 128, c0:c0 + cw], "wu", cw)
        for c in range(NFC):
            wloadc(wd[:, c, :], w_down[c * 128:(c + 1) * 128, :], "wd", 1024)
        x2_r = Ring(sb, "x2", [128, 2, D], F32, 2)
        h2_r = Ring(sb, "h2", [128, D], BF16, 2)
        h2T_r = Ring(sb, "h2T", [128, 8, TWMAX], BF16, 1)
        actT_r = Ring(sb, "actT", [128, NFC, TWMAX], BF16, 1)
        sg_r = Ring(sb, "sg", [128, TWMAX], F32, 2)
        st_r = Ring(sb, "statc", [128, 8], F32, 4)
        for _i in range(4):
            E.memset("dve", st_r.t[_i][:], 0.0, [st_r.k[_i] + "z"])
        pA_r = Ring(psb, "pA", [128, 512], F32, 4)
        pF_r = Ring(psb, "pF", [128, 512], F32, 3)
        pT_r = Ring(psb, "pTc", [128, 1024], BF16, 1)

        def rstd_c(ss_ap, n, stt, stk):
            E.cp("act", stt[:, 6:7], stt[:, 7:8], [stk + "a", stk + "z"], [stk + "a2"])
            E.act(stt[:, 1:2], ss_ap, AF.Ln, [stk + "a", stk + "a2"], [stk + "b"], bias=EPS, scale=1.0 / n)
            E.act(stt[:, 2:3], stt[:, 1:2], AF.Exp, [stk + "b"], [stk + "c"], scale=-0.5)
            return stt[:, 2:3], stk + "c"

        for job in jobs:
            lq = job.lq
            TW = 256 if lq % 256 == 0 else 128
            nsub = TW // 128
            def c_loads(job_, t_):
                TW_ = 256 if job_.lq % 256 == 0 else 128
                q0_ = t_ * TW_
                mixt_, mixk_ = mix_r.next()
                E.dma("sp", mixt_[:, :, 0:TW_], job_.MIX[:, q0_:q0_ + TW_].rearrange("(c p) t -> p c t", p=128), [], [mixk_], mixk_)
                xls = []
                for s_ in range(TW_ // 128):
                    r0_ = q0_ + s_ * 128
                    xl_, xlk_ = xl_r.next()
                    if job_.nvq == 128:
                        E.dma("sp", xl_[:], job_.xq[r0_:r0_ + 128, :], [], [xlk_], xlk_)
                    else:
                        E.dma("sp", xl_[0:job_.nvq, :], job_.xq[0:job_.nvq, :], [], [xlk_], xlk_)
                    xls.append((xl_, xlk_))
                return (mixt_, mixk_, xls)

            if job is jobs[0]:
                c_flat = [(j_, t_) for j_ in jobs for t_ in range(j_.lq // (256 if j_.lq % 256 == 0 else 128))]
                c_i = [0]
                c_pre = [c_loads(*c_flat[0])]
            for t in range(lq // TW):
                q0 = t * TW
                mixt, mixk, xls = c_pre[0]
                c_i[0] += 1
                x2, x2k = x2_r.next(); h2T, h2Tk = h2T_r.next()
                for s in range(nsub):
                    r0 = q0 + s * 128
                    xl, xlk = xls[s]
                    pa0, pa0k = pA_r.next(); pa1, pa1k = pA_r.next()
                    for c in range(8):
                        E.mm(pa0[:, :], mixt[:, c, s * 128:(s + 1) * 128], wo[:, c, 0:512], c == 0, c == 7, [mixk, "wo"], [pa0k])
                        E.mm(pa1[:, :], mixt[:, c, s * 128:(s + 1) * 128], wo[:, c, 512:1024], c == 0, c == 7, [mixk, "wo"], [pa1k])
                    E.tt("dve", x2[:, s, 0:512], pa0[:, :], xl[:, 0:512], ALU.add, [pa0k, xlk], [x2k + "a%d" % s])
                    E.tt("dve", x2[:, s, 512:1024], pa1[:, :], xl[:, 512:1024], ALU.add, [pa1k, xlk], [x2k + "b%d" % s])
                    xk2 = [x2k + "a%d" % s, x2k + "b%d" % s]
                    junk, jk = h2_r.next(); stt, stk = st_r.next()
                    E.act(junk[:], x2[:, s, :], AF.Square, xk2 + [stk + "z"], [jk, stk + "a"], accum=stt[:, 0:1])
                    rs, rsk = rstd_c(stt[:, 0:1], D, stt, stk)
                    h2, h2k = h2_r.next()
                    E.stt(h2[:], x2[:, s, :], rs, gffn[:], ALU.mult, ALU.mult, xk2 + [rsk, "gffn"], [h2k])
                    pT, pTk = pT_r.next()
                    for c in range(8):
                        E.tr(pT[:, c * 128:(c + 1) * 128], h2[:, c * 128:(c + 1) * 128], identb[:], [h2k, "identb"], [pTk])
                    E.cp("act", h2T[:, :, s * 128:(s + 1) * 128], pT[:, :].rearrange("p (c t) -> p c t", c=8), [pTk], [h2Tk + str(s)])
                if c_i[0] < len(c_flat):
                    c_pre[0] = c_loads(*c_flat[c_i[0]])
                h2keys = [h2Tk + str(s) for s in range(nsub)]
                actT, actTk = actT_r.next()
                for f in range(NFC):
                    pgt, pgtk = pF_r.next(); put, putk = pF_r.next()
                    for c in range(8):
                        E.mm(pgt[:, 0:TW], wg[:, c, f * 128:(f + 1) * 128], h2T[:, c, 0:TW], c == 0, c == 7, ["wg"] + h2keys, [pgtk])
                    for c in range(8):
                        E.mm(put[:, 0:TW], wu[:, c, f * 128:(f + 1) * 128], h2T[:, c, 0:TW], c == 0, c == 7, ["wu"] + h2keys, [putk])
                    sg, sgk = sg_r.next()
                    E.act(sg[:, 0:TW], pgt[:, 0:TW], AF.Silu, [pgtk], [sgk])
                    E.tt("dve", actT[:, f, 0:TW], sg[:, 0:TW], put[:, 0:TW], ALU.mult, [sgk, putk], [actTk + "_%d" % f])
                akeys = [actTk + "_%d" % f for f in range(NFC)]
                for s in range(nsub):
                    r0 = q0 + s * 128
                    pa0, pa0k = pA_r.next(); pa1, pa1k = pA_r.next()
                    for f in range(NFC):
                        E.mm(pa0[:, :], actT[:, f, s * 128:(s + 1) * 128], wd[:, f, 0:512], f == 0, f == NFC - 1, akeys + ["wd"], [pa0k])
                        E.mm(pa1[:, :], actT[:, f, s * 128:(s + 1) * 128], wd[:, f, 512:1024], f == 0, f == NFC - 1, akeys + ["wd"], [pa1k])
                    xk2 = [x2k + "a%d" % s, x2k + "b%d" % s]
                    E.tt("dve", x2[:, s, 0:512], pa0[:, :], x2[:, s, 0:512], ALU.add, [pa0k] + xk2, [x2k + "a%d" % s])
                    E.tt("dve", x2[:, s, 512:1024], pa1[:, :], x2[:, s, 512:1024], ALU.add, [pa1k] + xk2, [x2k + "b%d" % s])
                    junk, jk = h2_r.next(); stt, stk = st_r.next()
                    E.act(junk[:], x2[:, s, :], AF.Square, xk2 + [stk + "z"], [jk, stk + "a"], accum=stt[:, 0:1])
                    rs, rsk = rstd_c(stt[:, 0:1], D, stt, stk)
                    E.stt(x2[:, s, :], x2[:, s, :], rs, gfin[:], ALU.mult, ALU.mult, xk2 + [rsk, "gfin"], xk2)
                    nv = job.nvq
                    E.dma("sp", job.y[r0:r0 + nv, :], x2[0:nv, s, :], xk2, [], x2k + "o%d" % s)
        if 'C' in PH:
            P.emit(nc)
    return nc


def make_consts(S, PAST):
    NBF = S // 128
    ident = np.eye(128, dtype=np.float32)
    jj = np.arange(128)[:, None]; tt = np.arange(128)[None, :]
    U = (jj <= tt).astype(np.float32)
    tri = np.stack([U, U * (jj < 16), U * (jj < 32)]).astype(np.float32)
    on = np.ones((128, 128), np.float32)
    ones = np.stack([on, on * (jj < 16), on * (jj < 32)]).astype(np.float32)
    m_fox = np.where(jj <= tt, 0.0, NEG)
    m_mla = np.where((jj // 64) <= (tt // 64), 0.0, NEG)
    mask = np.stack([m_fox, m_mla]).astype(np.float32)
    half = 16
    inv = (10000.0 ** (-np.arange(half, dtype=np.float32) / half)).astype(np.float32)

    def tab(pos):
        ang = pos.astype(np.float32)[..., None] * inv
        c = np.cos(ang).astype(np.float32); s = np.sin(ang).astype(np.float32)
        return np.concatenate([c, c, -s, s], axis=-1).astype(np.float32)
    p = np.arange(128)[:, None]
    b = np.arange(NBF + 1)[None, :]
    posp = np.where(b == 0, p, NMETA + (b - 1) * 128 + p)
    ropep = tab(posp)
    ropes = tab((PAST + np.arange(128))[:, None])
    return dict(c_ident=ident, c_tri=tri, c_ones=ones, c_mask=mask, c_ropep=ropep, c_ropes=ropes)


def make_in_maps(inp, ncores, S, PAST, NS):
    consts = make_consts(S, PAST)
    f = lambda a: np.ascontiguousarray(np.asarray(a, dtype=np.float32))
    shared = dict(
        meta=f(inp["meta_tokens"]), w_in=f(inp["w_in"][0]), w_uq=f(inp["w_mla_uq"][0]), w_ukv=f(inp["w_mla_ukv"][0]),
        w_out=f(inp["w_out"][0]), w_gate=f(inp["w_ffn_gate"][0]), w_up=f(inp["w_ffn_up"][0]), w_down=f(inp["w_ffn_down"][0]),
        g_mix=f(inp["norm_mix"][0]).reshape(1, -1), b_forget=f(inp["b_forget"][0]).reshape(1, -1),
        g_q=f(inp["mla_q_norm"][0]).reshape(1, -1), g_kv=f(inp["mla_kv_norm"][0]).reshape(1, -1),
        g_ffn=f(inp["norm_ffn"][0]).reshape(1, -1), g_fin=f(inp["norm_final"]).reshape(1, -1), **consts)
    maps = []
    for c in range(ncores):
        m = dict(shared)
        m["xp"] = f(inp["x_prompt"][c])
        sl = slice(c * NS, (c + 1) * NS)
        m["xs"] = f(inp["x_sample"][sl])
        m["c_lat"] = f(inp["cache_mla_latent"][0, sl]); m["c_kr"] = f(inp["cache_mla_krope"][0, sl])
        m["c_fk"] = f(inp["cache_fox_k"][0, sl]).reshape(NS, PAST, 512); m["c_fv"] = f(inp["cache_fox_v"][0, sl]).reshape(NS, PAST, 512)
        m["c_lf"] = f(inp["cache_fox_logf"][0, sl])
        maps.append(m)
    return maps


def assemble(results, ncores, S, NS):
    LP = NMETA + S
    cat = lambda k: np.concatenate([np.asarray(r[k], dtype=np.float32)[None] for r in results], axis=0)
    y_p = cat("y_p")
    y_s = cat("y_s").reshape(ncores * NS, 32, D)
    lat_p = cat("lat_p")[None]; kr_p = cat("kr_p")[None]
    fk_p = cat("fk_p").reshape(1, ncores, LP, 8, 64); fv_p = cat("fv_p").reshape(1, ncores, LP, 8, 64)
    lf_p = cat("lf_p")[None]
    lat_s = cat("lat_s").reshape(1, ncores * NS, 32, 256); kr_s = cat("kr_s").reshape(1, ncores * NS, 32, 32)
    fk_s = cat("fk_s").reshape(1, ncores * NS, 32, 8, 64); fv_s = cat("fv_s").reshape(1, ncores * NS, 32, 8, 64)
    lf_s = cat("lf_s").reshape(1, ncores * NS, 32, 8)
    return (y_p, y_s, lat_p, kr_p, fk_p, fv_p, lf_p, lat_s, kr_s, fk_s, fv_s, lf_s)


def kernel(**inp):
    ncores = 8
    B, S, _ = inp["x_prompt"].shape
    PAST = inp["cache_mla_latent"].shape[2]
    NS = inp["x_sample"].shape[0] // ncores
    assert B == ncores
    nc = build(S, PAST, NS)
    maps = make_in_maps(inp, ncores, S, PAST, NS)
    res = run_bass_kernel_spmd(nc, maps, core_ids=list(range(ncores)))
    return assemble(res.results, ncores, S, NS)
```

```python
from contextlib import ExitStack
import numpy as np
import concourse.bass as bass
import concourse.mybir as mybir
from concourse.bass_utils import run_bass_kernel_spmd

F32 = mybir.dt.float32
BF16 = mybir.dt.bfloat16
AF = mybir.ActivationFunctionType
ALU = mybir.AluOpType

D = 1024
DIN = 2216
DFF = 2816
NMETA = 16
EPS = 1e-6
MLA_SCALE = 96 ** -0.5
FOX_SCALE = 0.125
NEG = -30000.0
COMPUTE = ("pe", "act", "dve", "pool")


class Prog:
    uid = 0

    def __init__(self):
        self.ins = []

    def op(self, eng, fn, r=(), w=()):
        self.ins.append((eng, fn, tuple(r), tuple(w), None))

    def dma(self, eng, fn, r=(), w=(), tag=None):
        assert tag is not None
        self.ins.append((eng, fn, tuple(r), tuple(w), tag))

    def analyze(self):
        ins = self.ins
        n = len(ins)
        last_w, readers = {}, {}
        need = [False] * n
        waits = [None] * n
        for i, (eng, fn, R, W, tag) in enumerate(ins):
            d = {}
            for k in R:
                j = last_w.get(k)
                if j is not None:
                    d[j] = True
            for k in W:
                j = last_w.get(k)
                if j is not None and j not in d:
                    d[j] = d.get(j, False)
                for j in readers.get(k, ()):
                    if j != i and j not in d:
                        d[j] = False
            for k in R:
                readers.setdefault(k, []).append(i)
            for k in W:
                last_w[k] = i
                readers[k] = []
            wl = []
            for j, raw in d.items():
                ej, tj = ins[j][0], ins[j][4]
                if tj is None:
                    if ej == "pe" and eng == "pe":
                        continue
                    need[j] = True
                wl.append(j)
            waits[i] = wl
        cnt, val, semkey = {}, [0] * n, [None] * n
        for i in range(n):
            eng, tag = ins[i][0], ins[i][4]
            if tag is not None:
                key = ("d", tag)
                cnt[key] = cnt.get(key, 0) + 16
            elif need[i]:
                key = ("e", eng)
                cnt[key] = cnt.get(key, 0) + 1
            else:
                continue
            val[i] = cnt[key]
            semkey[i] = key
        self.need, self.waits, self.val, self.semkey = need, waits, val, semkey
        return cnt

    def emit(self, nc, final_tags=None):
        cnt = self.analyze()
        ins = self.ins
        with nc.cleanup_on_exit():
          with ExitStack() as st:
            sems = {}
            for idx, k in enumerate(cnt.keys()):
                sems[k] = nc.alloc_semaphore(name="s%d_%d_%s" % (Prog.uid, idx, str(k[1])[:12]))
            Prog.uid += 1
            block = st.enter_context(nc.Block())
            per_eng = {}
            for i, rec in enumerate(ins):
                per_eng.setdefault(rec[0], []).append(i)
            if final_tags is None:
                final_tags = [k[1] for k in cnt if k[0] == "d"]

            def run(name, e):
                waited = {}
                for i in per_eng.get(name, []):
                    fn, tag = ins[i][1], ins[i][4]
                    req = {}
                    for j in self.waits[i]:
                        k, v = self.semkey[j], self.val[j]
                        if v > req.get(k, 0):
                            req[k] = v
                    for k, v in req.items():
                        if waited.get(k, 0) >= v:
                            continue
                        e.wait_ge(sems[k], v)
                        waited[k] = v
                    bi = fn(e)
                    if tag is not None:
                        bi.then_inc(sems[("d", tag)], 16)
                    elif self.need[i]:
                        bi.then_inc(sems[("e", name)], 1)
                if name == "sp":
                    for t in final_tags:
                        e.wait_ge(sems[("d", t)], cnt[("d", t)])

            block.tensor(lambda e: run("pe", e))
            block.scalar(lambda e: run("act", e))
            block.vector(lambda e: run("dve", e))
            block.gpsimd(lambda e: run("pool", e))
            block.sync(lambda e: run("sp", e))
          nc.all_engine_barrier()


class Ring:
    def __init__(self, alloc, name, shape, dt, n):
        self.t = [alloc("%s%d" % (name, i), shape, dt) for i in range(n)]
        self.k = ["%s%d" % (name, i) for i in range(n)]
        self.n = n
        self.i = -1

    def next(self):
        self.i += 1
        s = self.i % self.n
        return self.t[s], self.k[s]


class Em:
    def __init__(self, P):
        self.P = P

    def mm(self, out, lhsT, rhs, start, stop, r, w):
        self.P.op("pe", lambda e: e.matmul(out, lhsT=lhsT, rhs=rhs, start=start, stop=stop), r, w)

    def tr(self, out, in_, ident, r, w):
        self.P.op("pe", lambda e: e.transpose(out=out, in_=in_, identity=ident), r, w)

    def act(self, out, in_, func, r, w, bias=None, scale=None, accum=None):
        kw = {}
        if bias is not None:
            kw["bias"] = bias
        if scale is not None:
            kw["scale"] = scale
        if accum is not None:
            kw["accum_out"] = accum
        self.P.op("act", lambda e: e.activation(out=out, in_=in_, func=func, **kw), r, w)

    def cp(self, eng, out, in_, r, w):
        if eng == "act":
            self.P.op("act", lambda e: e.copy(out=out, in_=in_), r, w)
        else:
            self.P.op(eng, lambda e: e.tensor_copy(out=out, in_=in_), r, w)

    def tt(self, eng, out, in0, in1, op, r, w):
        self.P.op(eng, lambda e: e.tensor_tensor(out=out, in0=in0, in1=in1, op=op), r, w)

    def ts(self, eng, out, in0, s1, s2, op0, op1, r, w):
        if s2 is None:
            self.P.op(eng, lambda e: e.tensor_scalar(out=out, in0=in0, scalar1=s1, scalar2=None, op0=op0), r, w)
        else:
            self.P.op(eng, lambda e: e.tensor_scalar(out=out, in0=in0, scalar1=s1, scalar2=s2, op0=op0, op1=op1), r, w)

    def stt(self, out, in0, scalar, in1, op0, op1, r, w):
        self.P.op("dve", lambda e: e.scalar_tensor_tensor(out=out, in0=in0, scalar=scalar, in1=in1, op0=op0, op1=op1), r, w)

    def memset(self, eng, ap, v, w):
        self.P.op(eng, lambda e: e.memset(ap, v), (), w)

    def dma(self, eng, out, in_, r, w, tag):
        self.P.dma(eng, lambda e: e.dma_start(out=out, in_=in_), r, w, tag)


class Job:
    pass


def build(S, PAST, NS):
    import os
    nc = bass.Bass("TRN2", target_bir_lowering=False)
    NBF = S // 128
    QW_P = 512 if S % 512 == 0 else 128
    NBC = PAST // 128
    LP = NMETA + S

    def din(name, shape):
        return nc.dram_tensor(name, list(shape), F32, kind="ExternalInput").ap()

    def dout(name, shape):
        return nc.dram_tensor(name, list(shape), F32, kind="ExternalOutput").ap()

    def dscr(name, shape, dt=BF16):
        return nc.dram_tensor(name, list(shape), dt, kind="Internal").ap()

    xp = din("xp", [S, D]); meta = din("meta", [NMETA, D]); xs = din("xs", [NS, 32, D])
    c_lat = din("c_lat", [NS, PAST, 256]); c_kr = din("c_kr", [NS, PAST, 32])
    c_fk = din("c_fk", [NS, PAST, 512]); c_fv = din("c_fv", [NS, PAST, 512]); c_lf = din("c_lf", [NS, PAST, 8])
    w_in = din("w_in", [D, DIN]); w_uq = din("w_uq", [384, 768]); w_ukv = din("w_ukv", [256, 1024])
    w_out = din("w_out", [D, D]); w_gate = din("w_gate", [D, DFF]); w_up = din("w_up", [D, DFF]); w_down = din("w_down", [DFF, D])
    g_mix = din("g_mix", [1, D]); b_forget = din("b_forget", [1, 8]); g_q = din("g_q", [1, 384]); g_kv = din("g_kv", [1, 256])
    g_ffn = din("g_ffn", [1, D]); g_fin = din("g_fin", [1, D])
    c_ident = din("c_ident", [128, 128])
    c_tri = din("c_tri", [3, 128, 128])
    c_ones = din("c_ones", [3, 128, 128])
    c_mask = din("c_mask", [2, 128, 128])
    NPB = NBF + 1
    c_ropep = din("c_ropep", [128, NPB, 64])
    c_ropes = din("c_ropes", [128, 1, 64])

    y_p = dout("y_p", [S, D]); lat_p = dout("lat_p", [LP, 256]); kr_p = dout("kr_p", [LP, 32])
    fk_p = dout("fk_p", [LP, 512]); fv_p = dout("fv_p", [LP, 512]); lf_p = dout("lf_p", [LP, 8])
    y_s = dout("y_s", [NS, 32, D]); lat_s = dout("lat_s", [NS, 32, 256]); kr_s = dout("kr_s", [NS, 32, 32])
    fk_s = dout("fk_s", [NS, 32, 512]); fv_s = dout("fv_s", [NS, 32, 512]); lf_s = dout("lf_s", [NS, 32, 8])

    jobs = []
    jp = Job()
    jp.name = "p"; jp.nbc = 0; jp.cache = None
    jp.blocks = [dict(src=meta, r0=0, nv=NMETA, q=False, tri=1, rope=(c_ropep, 0), orow=0)]
    for b in range(NBF):
        jp.blocks.append(dict(src=xp, r0=b * 128, nv=128, q=True, tri=0, rope=(c_ropep, b + 1), orow=NMETA + b * 128))
    jp.outs = (lat_p, kr_p, fk_p, fv_p, lf_p); jp.y = y_p; jp.xq = xp
    jp.qw = QW_P; jp.lq = S; jp.nbk = NBF + 1; jp.nvq = 128
    jp.mla_mask = 1
    jobs.append(jp)
    for s in range(NS):
        j = Job()
        j.name = "s%d" % s; j.nbc = NBC
        j.cache = (c_lat[s], c_kr[s], c_fk[s], c_fv[s], c_lf[s])
        j.blocks = [dict(src=xs[s], r0=0, nv=32, q=True, tri=2, rope=(c_ropes, 0), orow=0)]
        j.outs = (lat_s[s], kr_s[s], fk_s[s], fv_s[s], lf_s[s]); j.y = y_s[s]; j.xq = xs[s]
        j.qw = 128; j.lq = 128; j.nbk = NBC + 1; j.nvq = 32
        j.mla_mask = None
        jobs.append(j)
    for j in jobs:
        lk = j.nbk * 128
        n = j.name
        j.KTn = dscr("KTn_" + n, [8, 64, lk]); j.KTr = dscr("KTr_" + n, [32, lk]); j.KTf = (nc.dram_tensor("KTf_" + n, [8, 64, lk], BF16, kind="ExternalOutput").ap() if (os.environ.get("K_DBG") and n == "p") else dscr("KTf_" + n, [8, 64, lk]))
        j.Vm = dscr("Vm_" + n, [8, 128, j.nbk, 64]); j.Vf = dscr("Vf_" + n, [8, 128, j.nbk, 64])
        j.QTm = dscr("QTm_" + n, [8, 96, j.lq]); j.QTf = (nc.dram_tensor("QTf_" + n, [8, 65, j.lq], BF16, kind="ExternalOutput").ap() if (os.environ.get("K_DBG") and n == "p") else dscr("QTf_" + n, [8, 65, j.lq]))
        j.MIX = (nc.dram_tensor("MIX_" + n, [D, j.lq], BF16, kind="ExternalOutput").ap() if (os.environ.get("K_DBG") and n == "p") else dscr("MIX_" + n, [D, j.lq])); j.NCUM = dscr("NCUM_" + n, [128, j.nbk, 8], F32)
    LKMAX = max(j.nbk for j in jobs) * 128
    LQMAX = max(j.lq for j in jobs)
    NBKMAX = LKMAX // 128

    import os
    PH = os.environ.get('K_PHASES', 'ABC')
    with ExitStack() as st:
        sb = lambda name, shape, dt: st.enter_context(nc.sbuf_tensor("A_" + name, shape, dt))
        psb = lambda name, shape, dt: st.enter_context(nc.psum_tensor("A_" + name, shape, dt))
        P = Prog(); E = Em(P)
        STQ = os.environ.get('K_STQ', 'sp')
        win = sb("win", [128, 8, DIN], BF16)
        wuq = sb("wuq", [128, 3, 768], BF16)
        wuk = sb("wuk", [128, 2, 8, 64], BF16)
        wuv = sb("wuv", [128, 2, 8, 64], BF16)
        identf = sb("identf", [128, 128], F32); identb = sb("identb", [128, 128], BF16)
        tri = sb("tri", [128, 3, 128], F32); ones3 = sb("ones3", [128, 3, 128], F32)
        gmix = sb("gmix", [128, D], F32); gq = sb("gq", [128, 384], F32); gkv = sb("gkv", [128, 256], F32)
        bfg = sb("bfg", [128, 8], F32)
        ropep = sb("ropep", [128, NPB, 64], F32); ropes = sb("ropes", [128, 1, 64], F32)
        carry = sb("carry", [128, 8], F32)
        ncum = sb("ncum", [128, NBKMAX, 8], F32)
        CQ, CKV, CKR, CLG, CFQ, CFK, CFV = 0, 384, 640, 672, 680, 1192, 1704
        src_off = dict(cq=0, ckv=384, kr=640, fq=672, fk=1184, fv=1696, lg=2208)
        stg_r = Ring(sb, "xt", [128, D], F32, 2)
        _ce = [0]

        def wload(dst_ap, src_ap, wkey, ncols):
            stg, sk = stg_r.next()
            E.dma("sp", stg[:, 0:ncols], src_ap, [], [sk], sk)
            eng = "act" if _ce[0] % 2 == 0 else "dve"
            _ce[0] += 1
            E.cp(eng, dst_ap, stg[:, 0:ncols], [sk], [wkey])
        for c in range(8):
            for (dst, so, wd_) in ((CQ, 0, 384), (CKV, 384, 256), (CKR, 640, 32), (CLG, 2208, 8), (CFQ, 672, 512), (CFK, 1184, 512), (CFV, 1696, 512)):
                wload(win[:, c, dst:dst + wd_], w_in[c * 128:(c + 1) * 128, so:so + wd_], "win", wd_)
        for c in range(3):
            wload(wuq[:, c, :], w_uq[c * 128:(c + 1) * 128, :], "wuq", 768)
        for c in range(2):
            stg, sk = stg_r.next()
            E.dma("sp", stg[:, :], w_ukv[c * 128:(c + 1) * 128, :], [], [sk], sk)
            sv = stg[:, :].rearrange("p (h t d) -> p h t d", h=8, t=2)
            E.cp("act", wuk[:, c], sv[:, :, 0, :], [sk], ["wuk"])
            E.cp("dve", wuv[:, c], sv[:, :, 1, :], [sk], ["wuv"])
        E.dma("sp", identf[:], c_ident[:, :], [], ["identf"], "w_id")
        E.cp("dve", identb[:], identf[:], ["identf"], ["identb"])
        E.dma("sp", tri[:], c_tri.rearrange("a p n -> p a n"), [], ["tri"], "w_tri")
        E.dma("sp", ones3[:], c_ones.rearrange("a p n -> p a n"), [], ["ones3"], "w_ones")
        E.dma("sp", gmix[:], g_mix.partition_broadcast(128), [], ["gmix"], "w_gmix")
        E.dma("sp", gq[:], g_q.partition_broadcast(128), [], ["gq"], "w_gq")
        E.dma("sp", gkv[:], g_kv.partition_broadcast(128), [], ["gkv"], "w_gkv")
        E.dma("sp", bfg[:], b_forget.partition_broadcast(128), [], ["bfg"], "w_bfg")
        E.dma("sp", ropep[:], c_ropep[:, :, :], [], ["ropep"], "w_ropep")
        E.dma("sp", ropes[:], c_ropes[:, :, :], [], ["ropes"], "w_ropes")
        rope_sb = {id(c_ropep): (ropep, "ropep"), id(c_ropes): (ropes, "ropes")}

        xt_r = stg_r
        xpart = sb("xpart", [128, D], F32)
        E.memset("dve", xpart[:], 0.0, ["xpart"])
        junk_r = Ring(sb, "junk", [128, D], BF16, 2)
        st_r = Ring(sb, "stat", [128, 8], F32, 4)
        for _i in range(4):
            E.memset("dve", st_r.t[_i][:], 0.0, [st_r.k[_i] + "z"])
        hb_r = Ring(sb, "hb", [128, D], BF16, 2)
        hT_r = Ring(sb, "hT", [128, 8, 128], BF16, 2)
        lat32_r = Ring(sb, "lat32", [128, 256], F32, 3)
        kr32_r = Ring(sb, "kr32", [128, 32], F32, 3)
        krt_r = Ring(sb, "krt", [128, 64], F32, 2)
        lf32_r = Ring(sb, "lf32", [128, 8], F32, 4)
        lgt_r = Ring(sb, "lgt", [128, 8], F32, 2)
        fk32_r = Ring(sb, "fk32", [128, 512], F32, 3)
        fv32_r = Ring(sb, "fv32", [128, 512], F32, 3)
        lkb_r = Ring(sb, "lkb", [128, 352], BF16, 2)
        for _i in range(2):
            E.memset("dve", lkb_r.t[_i][:, 256:320], 0.0, [lkb_r.k[_i] + "p"])
        fkb_r = Ring(sb, "fkb", [128, 512], BF16, 2)
        vfb_r = Ring(sb, "vfb", [128, 512], BF16, 2)
        vmb_r = Ring(sb, "vmb", [128, 512], BF16, 2)
        cqn_r = Ring(sb, "cqn", [128, 384], BF16, 2)
        fqb_r = Ring(sb, "fqb", [128, 512], BF16, 2)
        cqT_r = Ring(sb, "cqT", [128, 3, 128], BF16, 2)
        qb_r = Ring(sb, "qb", [128, 768], BF16, 2)
        qrt_r = Ring(sb, "qrt", [128, 8, 64], F32, 2)
        latT_r = Ring(sb, "latT", [128, 3, 128], BF16, 2)
        ktn_r = Ring(sb, "ktn", [64, 8, 128], BF16, 2)
        ktf_r = Ring(sb, "ktf", [64, 8, 128], BF16, 2)
        qtm_r = Ring(sb, "qtm", [96, 8, 128], BF16, 2)
        qtf_r = Ring(sb, "qtf", [64, 8, 128], BF16, 2)
        cum32_r = Ring(sb, "cum32", [128, 8], F32, 2)
        cum8_r = Ring(sb, "cum8", [128, 8], BF16, 2)
        cumT_r = Ring(sb, "cumT", [8, 128], BF16, 2)
        pT_r = Ring(psb, "pT", [128, 1024], BF16, 2)
        pG_r = Ring(psb, "pG", [128, 512], F32, 5)

        def rstd_from(ss_ap, n, stt, stk):
            E.cp("act", stt[:, 6:7], stt[:, 7:8], [stk + "a", stk + "z"], [stk + "a2"])
            E.act(stt[:, 1:2], ss_ap, AF.Ln, [stk + "a", stk + "a2"], [stk + "b"], bias=EPS, scale=1.0 / n)
            E.act(stt[:, 2:3], stt[:, 1:2], AF.Exp, [stk + "b"], [stk + "c"], scale=-0.5)
            return stt[:, 2:3], stk + "c"

        def rope32(out32, ok, src, sk, tab, tk, tmp, tmk):
            E.tt("dve", tmp[:, 0:32], src, tab[:, 0:32], ALU.mult, [sk, tk], [tmk + "a"])
            E.tt("dve", tmp[:, 32:48], src[:, 16:32], tab[:, 32:48], ALU.mult, [sk, tk], [tmk + "b"])
            E.tt("dve", tmp[:, 48:64], src[:, 0:16], tab[:, 48:64], ALU.mult, [sk, tk], [tmk + "c"])
            E.tt("dve", out32, tmp[:, 0:32], tmp[:, 32:64], ALU.add, [tmk + "a", tmk + "b", tmk + "c"], [ok])

        for job in jobs:
            lat_o, kr_o, fk_o, fv_o, lf_o = job.outs
            jn = job.name
            E.memset("dve", carry[:], 0.0, ["carry"])
            kb = 0
            qblk = 0
            allblocks = [("c", i) for i in range(min(job.nbc, int(os.environ.get("K_NBC_LIMIT", "9999"))))] + [("n", b) for b in job.blocks]
            def issue_loads(job_, kind_, bi_):
                d = dict(job=job_, kind=kind_, bi=bi_)
                if kind_ == "c":
                    cl, ckr_, cfk_, cfv_, clf_ = job_.cache
                    r0 = bi_ * 128
                    d["lat32"] = lat32_r.next(); d["kr32"] = kr32_r.next()
                    d["fk32"] = fk32_r.next(); d["fv32"] = fv32_r.next(); d["lf32"] = lf32_r.next()
                    E.dma("sp", d["lat32"][0][:], cl[r0:r0 + 128, :], [], [d["lat32"][1]], d["lat32"][1])
                    E.dma("sp", d["kr32"][0][:], ckr_[r0:r0 + 128, :], [], [d["kr32"][1]], d["kr32"][1])
                    E.dma("sp", d["fk32"][0][:], cfk_[r0:r0 + 128, :], [], [d["fk32"][1]], d["fk32"][1])
                    E.dma("sp", d["fv32"][0][:], cfv_[r0:r0 + 128, :], [], [d["fv32"][1]], d["fv32"][1])
                    E.dma("sp", d["lf32"][0][:], clf_[r0:r0 + 128, :], [], [d["lf32"][1]], d["lf32"][1])
                else:
                    blk_ = bi_
                    nv_ = blk_["nv"]
                    if nv_ == 128:
                        xt_, xtk_ = xt_r.next()
                        E.dma("sp", xt_[:], blk_["src"][blk_["r0"]:blk_["r0"] + 128, :], [], [xtk_], xtk_)
                    else:
                        xt_, xtk_ = xpart, "xpart"
                        E.dma("sp", xt_[0:nv_, :], blk_["src"][blk_["r0"]:blk_["r0"] + nv_, :], [], [xtk_], "xpartd")
                    d["xt"] = (xt_, xtk_)
                return d

            if job is jobs[0]:
                flat = [(j_, k_, b_) for j_ in jobs for (k_, b_) in ([("c", i) for i in range(j_.nbc)] + [("n", b) for b in j_.blocks])]
                flat_i = [0]
                pre_ld = [issue_loads(*flat[0])]
            for kind, bi in allblocks:
                LD = pre_ld[0]
                flat_i[0] += 1
                if flat_i[0] < len(flat):
                    pre_ld[0] = issue_loads(*flat[flat_i[0]])
                lkb, lkbk = lkb_r.next(); fkb, fkbk = fkb_r.next(); vfb, vfbk = vfb_r.next()
                hasq = False
                if kind == "c":
                    lat32, lat32k = LD["lat32"]; kr32, kr32k = LD["kr32"]
                    fk32, fk32k = LD["fk32"]; fv32, fv32k = LD["fv32"]; lf32, lf32k = LD["lf32"]
                    E.cp("act", lkb[:, 0:256], lat32[:], [lat32k], [lkbk + "l"])
                    E.cp("dve", lkb[:, 320:352], kr32[:], [kr32k], [lkbk + "r"])
                    E.cp("act", fkb[:], fk32[:], [fk32k], [fkbk])
                    E.cp("dve", vfb[:], fv32[:], [fv32k], [vfbk])
                    tri_i = 0
                else:
                    lf32, lf32k = lf32_r.next()
                    blk = bi
                    nv = blk["nv"]; hasq = blk["q"]; tri_i = blk["tri"]
                    rtab_t, rtab_k = rope_sb[id(blk["rope"][0])]
                    rtab = rtab_t[:, blk["rope"][1], :]
                    orow = blk["orow"]
                    xt, xtk = LD["xt"]
                    junk, jk = junk_r.next(); stt, stk = st_r.next()
                    E.act(junk[:], xt[:], AF.Square, [xtk, stk + "z"], [jk, stk + "a"], accum=stt[:, 0:1])
                    rs, rsk = rstd_from(stt[:, 0:1], D, stt, stk)
                    hb, hbk = hb_r.next()
                    E.stt(hb[:], xt[:], rs, gmix[:], ALU.mult, ALU.mult, [xtk, rsk, "gmix"], [hbk])
                    pT, pTk = pT_r.next(); hT, hTk = hT_r.next()
                    for c in range(8):
                        E.tr(pT[:, c * 128:(c + 1) * 128], hb[:, c * 128:(c + 1) * 128], identb[:], [hbk, "identb"], [pTk])
                    E.cp("act", hT[:].rearrange("p c t -> p (c t)"), pT[:, :], [pTk], [hTk])
                    pg, pgk = pG_r.next()
                    for c in range(8):
                        E.mm(pg[:, 0:296], hT[:, c, :], win[:, c, CKV:CKV + 296], c == 0, c == 7, [hTk, "win"], [pgk])
                    stt2, stk2 = st_r.next(); junk2, jk2 = junk_r.next()
                    E.act(junk2[:, 0:256], pg[:, 0:256], AF.Square, [pgk, stk2 + "z"], [jk2, stk2 + "a"], accum=stt2[:, 0:1])
                    rs2, rs2k = rstd_from(stt2[:, 0:1], 256, stt2, stk2)
                    lat32, lat32k = lat32_r.next()
                    E.stt(lat32[:], pg[:, 0:256], rs2, gkv[:], ALU.mult, ALU.mult, [pgk, rs2k, "gkv"], [lat32k])
                    E.dma(STQ, lat_o[orow:orow + nv, :], lat32[0:nv, :], [lat32k], [], lat32k + "o")
                    E.cp("act", lkb[:, 0:256], lat32[:], [lat32k], [lkbk + "l"])
                    kr32, kr32k = kr32_r.next(); krt, krtk = krt_r.next()
                    rope32(kr32[:], kr32k, pg[:, 256:288], pgk, rtab, rtab_k, krt, krtk)
                    E.dma(STQ, kr_o[orow:orow + nv, :], kr32[0:nv, :], [kr32k], [], kr32k + "o")
                    E.cp("act", lkb[:, 320:352], kr32[:], [kr32k], [lkbk + "r"])
                    lgt, lgtk = lgt_r.next()
                    E.tt("dve", lgt[:], pg[:, 288:296], bfg[:], ALU.add, [pgk, "bfg"], [lgtk])
                    E.act(lgt[:], lgt[:], AF.Exp, [lgtk], [lgtk], scale=-1.0)
                    E.act(lgt[:], lgt[:], AF.Ln, [lgtk], [lgtk], bias=1.0)
                    E.ts("dve", lf32[:], lgt[:], -1.0, None, ALU.mult, None, [lgtk], [lf32k])
                    E.dma(STQ, lf_o[orow:orow + nv, :], lf32[0:nv, :], [lf32k], [], lf32k + "o")
                    pg, pgk = pG_r.next()
                    for c in range(8):
                        E.mm(pg[:, :], hT[:, c, :], win[:, c, CFK:CFK + 512], c == 0, c == 7, [hTk, "win"], [pgk])
                    fk32, fk32k = fk32_r.next()
                    E.cp("act", fk32[:], pg[:, :], [pgk], [fk32k])
                    E.dma(STQ, fk_o[orow:orow + nv, :], fk32[0:nv, :], [fk32k], [], fk32k + "o")
                    E.cp("act", fkb[:], fk32[:], [fk32k], [fkbk])
                    pg, pgk = pG_r.next()
                    for c in range(8):
                        E.mm(pg[:, :], hT[:, c, :], win[:, c, CFV:CFV + 512], c == 0, c == 7, [hTk, "win"], [pgk])
                    fv32, fv32k = fv32_r.next()
                    E.cp("act", fv32[:], pg[:, :], [pgk], [fv32k])
                    E.dma(STQ, fv_o[orow:orow + nv, :], fv32[0:nv, :], [fv32k], [], fv32k + "o")
                    E.cp("act", vfb[:], fv32[:], [fv32k], [vfbk])
                    if hasq:
                        pg, pgk = pG_r.next()
                        for c in range(8):
                            E.mm(pg[:, 0:384], hT[:, c, :], win[:, c, CQ:CQ + 384], c == 0, c == 7, [hTk, "win"], [pgk])
                        stt3, stk3 = st_r.next(); junk3, jk3 = junk_r.next()
                        E.act(junk3[:, 0:384], pg[:, 0:384], AF.Square, [pgk, stk3 + "z"], [jk3, stk3 + "a"], accum=stt3[:, 0:1])
                        rs3, rs3k = rstd_from(stt3[:, 0:1], 384, stt3, stk3)
                        cqn, cqnk = cqn_r.next()
                        E.stt(cqn[:], pg[:, 0:384], rs3, gq[:], ALU.mult, ALU.mult, [pgk, rs3k, "gq"], [cqnk])
                        pg, pgk = pG_r.next()
                        for c in range(8):
                            E.mm(pg[:, :], hT[:, c, :], win[:, c, CFQ:CFQ + 512], c == 0, c == 7, [hTk, "win"], [pgk])
                        fqb, fqbk = fqb_r.next()
                        E.cp("act", fqb[:], pg[:, :], [pgk], [fqbk])
                t0 = kb * 128
                pg, pgk = pG_r.next()
                E.mm(pg[:, 0:8], tri[:, tri_i, :], lf32[:], True, True, ["tri", lf32k], [pgk])
                E.mm(pg[:, 8:16], ones3[:, tri_i, :], lf32[:], True, True, ["ones3", lf32k], [pgk])
                cum32, cum32k = cum32_r.next()
                E.tt("dve", cum32[:], pg[:, 0:8], carry[:], ALU.add, [pgk, "carry"], [cum32k])
                E.tt("dve", carry[:], pg[:, 8:16], carry[:], ALU.add, [pgk, "carry"], ["carry"])
                E.ts("dve", ncum[:, kb, :], cum32[:], -1.0, None, ALU.mult, None, [cum32k], ["ncum"])
                pT, pTk = pT_r.next(); latT, latTk = latT_r.next()
                E.tr(pT[:, 0:128], lkb[:, 0:128], identb[:], [lkbk + "l", "identb"], [pTk])
                E.tr(pT[:, 128:256], lkb[:, 128:256], identb[:], [lkbk + "l", "identb"], [pTk])
                E.tr(pT[:, 256:384], lkb[:, 224:352], identb[:], [lkbk + "l", lkbk + "r", lkbk + "p", "identb"], [pTk])
                E.cp("dve", latT[:, 0:2, :].rearrange("p c t -> p (c t)"), pT[:, 0:256], [pTk], [latTk + "l"])
                E.cp("dve", latT[96:128, 2, :], pT[96:128, 256:384], [pTk], [latTk + "r"])
                E.dma(STQ, job.KTr[:, t0:t0 + 128], latT[96:128, 2, :], [latTk + "r"], ["dram_KTr_" + jn], latTk + "ro")
                ktn, ktnk = ktn_r.next()
                for half in range(2):
                    pg, pgk = pG_r.next()
                    pgv = pg[0:64, :].rearrange("p (h t) -> p h t", h=4)
                    for hh in range(4):
                        h = half * 4 + hh
                        for c in range(2):
                            E.mm(pgv[:, hh, :], wuk[:, c, h, :], latT[:, c, :], c == 0, c == 1, ["wuk", latTk + "l"], [pgk])
                    E.cp("act" if half == 0 else "dve", ktn[:, half * 4:(half + 1) * 4, :], pgv, [pgk], [ktnk + str(half)])
                E.dma(STQ, job.KTn[:, :, t0:t0 + 128].rearrange("h p t -> p h t"), ktn[:], [ktnk + "0", ktnk + "1"], ["dram_KTn_" + jn], ktnk + "o")
                pg, pgk = pG_r.next()
                for c in range(2):
                    E.mm(pg[:, :], latT[:, c, :], wuv[:, c].rearrange("p h d -> p (h d)"), c == 0, c == 1, [latTk + "l", "wuv"], [pgk])
                vmb, vmbk = vmb_r.next()
                E.cp("act", vmb[:], pg[:, :], [pgk], [vmbk])
                E.dma(STQ, job.Vm[:, :, kb, :].rearrange("h p d -> p h d"), vmb[:].rearrange("p (h d) -> p h d", h=8), [vmbk], ["dram_Vm_" + jn], vmbk + "o")
                E.dma(STQ, job.Vf[:, :, kb, :].rearrange("h p d -> p h d"), vfb[:].rearrange("p (h d) -> p h d", h=8), [vfbk], ["dram_Vf_" + jn], vfbk + "o")
                pT, pTk = pT_r.next(); ktf, ktfk = ktf_r.next()
                for h in range(8):
                    E.tr(pT[0:64, h * 128:(h + 1) * 128], fkb[:, h * 64:(h + 1) * 64], identb[:], [fkbk, "identb"], [pTk])
                E.cp("dve", ktf[:].rearrange("p h t -> p (h t)"), pT[0:64, :], [pTk], [ktfk])
                E.dma(STQ, job.KTf[:, :, t0:t0 + 128].rearrange("h p t -> p h t"), ktf[:], [ktfk], ["dram_KTf_" + jn], ktfk + "o")
                if hasq:
                    q0 = qblk * 128
                    pT, pTk = pT_r.next(); cqT, cqTk = cqT_r.next()
                    for c in range(3):
                        E.tr(pT[:, c * 128:(c + 1) * 128], cqn[:, c * 128:(c + 1) * 128], identb[:], [cqnk, "identb"], [pTk])
                    E.cp("act", cqT[:].rearrange("p c t -> p (c t)"), pT[:, 0:384], [pTk], [cqTk])
                    qb, qbk = qb_r.next(); qrt, qrtk = qrt_r.next()
                    for half in range(2):
                        pg, pgk = pG_r.next()
                        for c in range(3):
                            E.mm(pg[:, 0:384], cqT[:, c, :], wuq[:, c, half * 384:(half + 1) * 384], c == 0, c == 2, [cqTk, "wuq"], [pgk])
                        pv = pg[:, 0:384].rearrange("p (h d) -> p h d", h=4)
                        qv = qb[:, half * 384:(half + 1) * 384].rearrange("p (h d) -> p h d", h=4)
                        E.cp("dve", qv[:, :, 0:64], pv[:, :, 0:64], [pgk], [qbk + "n%d" % half])
                        tb = rtab.unsqueeze(1)
                        tmp = qrt[:, half * 4:(half + 1) * 4, :]
                        tk = qrtk + str(half)
                        E.tt("dve", tmp[:, :, 0:32], pv[:, :, 64:96], tb[:, :, 0:32].broadcast_to([128, 4, 32]), ALU.mult, [pgk, rtab_k], [tk + "a"])
                        E.tt("dve", tmp[:, :, 32:48], pv[:, :, 80:96], tb[:, :, 32:48].broadcast_to([128, 4, 16]), ALU.mult, [pgk, rtab_k], [tk + "b"])
                        E.tt("dve", tmp[:, :, 48:64], pv[:, :, 64:80], tb[:, :, 48:64].broadcast_to([128, 4, 16]), ALU.mult, [pgk, rtab_k], [tk + "c"])
                        E.tt("dve", qv[:, :, 64:96], tmp[:, :, 0:32], tmp[:, :, 32:64], ALU.add, [tk + "a", tk + "b", tk + "c"], [qbk + "r%d" % half])
                    qkeys = [qbk + "n0", qbk + "n1", qbk + "r0", qbk + "r1"]
                    pT, pTk = pT_r.next(); qtm, qtmk = qtm_r.next()
                    for h in range(8):
                        E.tr(pT[0:96, h * 128:(h + 1) * 128], qb[:, h * 96:(h + 1) * 96], identb[:], qkeys + ["identb"], [pTk])
                    E.cp("dve", qtm[:].rearrange("p h t -> p (h t)"), pT[0:96, :], [pTk], [qtmk])
                    E.dma(STQ, job.QTm[:, :, q0:q0 + 128].rearrange("h p t -> p h t"), qtm[:], [qtmk], ["dram_QTm_" + jn], qtmk + "o")
                    pT, pTk = pT_r.next(); qtf, qtfk = qtf_r.next()
                    for h in range(8):
                        E.tr(pT[0:64, h * 128:(h + 1) * 128], fqb[:, h * 64:(h + 1) * 64], identb[:], [fqbk, "identb"], [pTk])
                    E.cp("act", qtf[:].rearrange("p h t -> p (h t)"), pT[0:64, :], [pTk], [qtfk])
                    E.dma(STQ, job.QTf[:, 0:64, q0:q0 + 128].rearrange("h p t -> p h t"), qtf[:], [qtfk], ["dram_QTf_" + jn], qtfk + "o")
                    cum8, cum8k = cum8_r.next(); cumT, cumTk = cumT_r.next()
                    E.ts("dve", cum8[:], cum32[:], 8.0, None, ALU.mult, None, [cum32k], [cum8k])
                    pT, pTk = pT_r.next()
                    E.tr(pT[0:8, 0:128], cum8[:, 0:8], identb[:], [cum8k, "identb"], [pTk])
                    E.cp("dve", cumT[:], pT[0:8, 0:128], [pTk], [cumTk])
                    E.dma(STQ, job.QTf[:, 64, q0:q0 + 128], cumT[:], [cumTk], ["dram_QTf_" + jn], cumTk + "o")
                    qblk += 1
                kb += 1
            E.dma(STQ, job.NCUM[:, :, :], ncum[:, 0:job.nbk, :], ["ncum"], [], "ncum_o")
        if 'A' in PH:
            _tr = int(os.environ.get('K_ATRUNC', '0'))
            if _tr:
                print('PHASE A n_ins', len(P.ins)); P.ins = P.ins[:_tr]
            P.emit(nc)

    with ExitStack() as st:
        sb = lambda name, shape, dt: st.enter_context(nc.sbuf_tensor("B_" + name, shape, dt))
        psb = lambda name, shape, dt: st.enter_context(nc.psum_tensor("B_" + name, shape, dt))
        P = Prog(); E = Em(P)
        identf = sb("identf", [128, 128], F32); identb = sb("identb", [128, 128], BF16)
        maskf = sb("maskf", [128, 2, 128], F32); maskb = sb("maskb", [128, 2, 128], BF16)
        onesr = sb("onesr", [65, 64], F32)
        E.dma("sp", identf[:], c_ident[:, :], [], ["identf"], "w_id")
        E.cp("dve", identb[:], identf[:], ["identf"], ["identb"])
        E.dma("sp", maskf[:], c_mask.rearrange("a p n -> p a n"), [], ["maskf"], "w_mask")
        E.cp("dve", maskb[:], maskf[:], ["maskf"], ["maskb"])
        E.memset("dve", onesr[:], 1.0, ["onesr"])
        ktm_r = Ring(sb, "ktm", [96, LKMAX], BF16, 2)
        ktf_r = Ring(sb, "ktf", [96, LKMAX], BF16, 2)
        v_r = Ring(sb, "vv", [128, NBKMAX, 128], BF16, 2)
        qm_r = Ring(sb, "qm", [96, LQMAX], BF16, 2)
        qf_r = Ring(sb, "qf", [96, LQMAX], BF16, 2)
        pt_r = Ring(sb, "pt", [128, 512], BF16, 5)
        osb_r = Ring(sb, "osb", [65, 512], F32, 2)
        rec_r = Ring(sb, "rec", [65, 512], F32, 2)
        mixo_r = Ring(sb, "mixo", [64, 512], BF16, 2)
        pS_r = Ring(psb, "pS", [128, 512], F32, 5)
        pO_r = Ring(psb, "pO", [128, 512], F32, 2)
        for i in range(2):
            E.memset("dve", ktf_r.t[i][64:96, :], 0.0, [ktf_r.k[i] + "1"])
            E.memset("dve", ktf_r.t[i][64:65, :], 1.0, [ktf_r.k[i] + "1"])
            E.memset("dve", qf_r.t[i][64:96, :], 0.0, [qf_r.k[i] + "z", qf_r.k[i]])
            E.memset("dve", v_r.t[i][:, :, 65:128], 0.0, [v_r.k[i] + "1"])

        ncum_r = Ring(sb, "ncumr", [128, NBKMAX, 8], F32, 2)

        def load_head(job, hd):
            nbk = job.nbk; lk = nbk * 128; lq = job.lq
            fox = hd >= 8
            h = hd % 8
            vt, vk = v_r.next()
            vkeys = []
            if fox:
                kt, ktk = ktf_r.next(); qt, qtk = qf_r.next(); KD = 96
                E.dma("sp", kt[0:64, 0:lk], job.KTf[h], [], [ktk], ktk)
                E.dma("sp", qt[0:65, 0:lq], job.QTf[h], [], [qtk], qtk)
                vsrc = job.Vf
                kr_keys = [ktk, ktk + "1", qtk + "z"]
                scale = FOX_SCALE
            else:
                kt, ktk = ktm_r.next(); qt, qtk = qm_r.next(); KD = 96
                E.dma("sp", kt[0:64, 0:lk], job.KTn[h], [], [ktk], ktk)
                E.dma("sp", kt[64:96, 0:lk], job.KTr[:, :], [], [ktk + "r"], ktk + "r")
                E.dma("sp", qt[0:96, 0:lq], job.QTm[h], [], [qtk], qtk)
                vsrc = job.Vm
                kr_keys = [ktk, ktk + "r"]
                scale = MLA_SCALE
            for b0_ in range(0, nbk, 16):
                b1_ = min(nbk, b0_ + 16)
                E.dma("sp", vt[:, b0_:b1_, 0:64], vsrc[h, :, b0_:b1_, :], [], [vk + "c%d" % b0_], vk + "_%d" % b0_)
                vkeys.append(vk + "c%d" % b0_)
            return dict(fox=fox, h=h, vt=vt, vk=vk, vkeys=vkeys, kt=kt, ktk=ktk, qt=qt, qtk=qtk, KD=KD, kr_keys=kr_keys, scale=scale)

        RS = dscr("RS_scr", [4, 512], F32)
        rs_cnt = [0]
        items = [(job, hd) for job in jobs for hd in range(16)]
        nxt_loaded = load_head(*items[0])
        for it_i, (job, hd) in enumerate(items):
            nbk = job.nbk; lk = nbk * 128; lq = job.lq; QW = job.qw
            if hd == 0:
                ncum, ncumk = ncum_r.next()
                E.dma("sp", ncum[:, 0:nbk, :], job.NCUM[:, :, :], [], [ncumk], ncumk)
                for i in range(2):
                    vt, vk = v_r.t[i], v_r.k[i]
                    E.memset("dve", vt[:, :, 64:65], 1.0, [vk + "1"])
                    pb_ = job.nbc if job.name != "p" else 0
                    nvb = job.blocks[0]["nv"]
                    E.memset("dve", vt[:, pb_, 64:65], 0.0, [vk + "1"])
                    E.memset("dve", vt[0:nvb, pb_, 64:65], 1.0, [vk + "1"])
            L_ = nxt_loaded
            if it_i + 1 < len(items):
                nxt_loaded = load_head(*items[it_i + 1])
            fox = L_["fox"]; h = L_["h"]; vt = L_["vt"]; vk = L_["vk"]; kt = L_["kt"]; ktk = L_["ktk"]
            qt = L_["qt"]; qtk = L_["qtk"]; KD = L_["KD"]; kr_keys = L_["kr_keys"]; scale = L_["scale"]; vkeys = L_["vkeys"]
            if True:
                nqt = lq // QW
                for t in range(nqt):
                    q0 = t * QW
                    if job.name == "p":
                        nfull = 1 + t * (QW // 128)
                        kbl = [(j, 0, None) for j in range(nfull)]
                        for jj in range(QW // 128):
                            kbl.append((nfull + jj, jj * 128, 0 if fox else 1))
                    else:
                        kbl = [(j, 0, None) for j in range(job.nbc)]
                        kbl.append((job.nbc, 0, 0 if fox else None))
                    pO, pOk = pO_r.next()
                    pend = []

                    def qk(idx):
                        j, c0, mk = kbl[idx]
                        pS, pSk = pS_r.next()
                        E.mm(pS[:, c0:QW], kt[0:KD, j * 128:(j + 1) * 128], qt[0:KD, q0 + c0:q0 + QW], True, mk is None, kr_keys + [qtk], [pSk])
                        if mk is not None:
                            E.mm(pS[:, c0:c0 + 128], identb[:, :], maskb[:, mk, :], False, True, ["identb", "maskb"], [pSk])
                        return pS, pSk

                    LA = 3
                    qq = [qk(i_) for i_ in range(min(LA, len(kbl)))]
                    for idx in range(len(kbl)):
                        j, c0, mk = kbl[idx]
                        pS, pSk = qq.pop(0)
                        if idx + LA < len(kbl):
                            qq.append(qk(idx + LA))
                        pt, ptk = pt_r.next()
                        if fox:
                            E.act(pt[:, c0:QW], pS[:, c0:QW], AF.Exp, [pSk, ncumk], [ptk], bias=ncum[:, j, h:h + 1], scale=scale)
                        else:
                            E.act(pt[:, c0:QW], pS[:, c0:QW], AF.Exp, [pSk], [ptk], scale=scale)
                        E.mm(pO[:, c0:QW], vt[:, j, :], pt[:, c0:QW], idx == 0, idx == len(kbl) - 1, vkeys + [vk + "1", ptk], [pOk])
                    osb, osbk = osb_r.next(); rec, reck = rec_r.next(); mixo, mixok = mixo_r.next()
                    E.cp("dve", osb[0:65, 0:QW], pO[0:65, 0:QW], [pOk], [osbk])
                    rsl = rs_cnt[0] % 4
                    rs_cnt[0] += 1
                    E.dma("sp", RS[rsl:rsl + 1, 0:QW], osb[64:65, 0:QW], [osbk], ["RS%d" % rsl], "rs_w%d" % rsl)
                    E.dma("sp", rec[0:64, 0:QW], RS[rsl:rsl + 1, 0:QW].partition_broadcast(64), ["RS%d" % rsl], [reck], "rs_r%d" % rsl)
                    E.P.op("dve", (lambda o, i: (lambda e: e.reciprocal(out=o, in_=i)))(rec[0:64, 0:QW], rec[0:64, 0:QW]), [reck], [reck])
                    E.tt("dve", mixo[0:64, 0:QW], osb[0:64, 0:QW], rec[0:64, 0:QW], ALU.mult, [osbk, reck], [mixok])
                    E.dma("sp", job.MIX[hd * 64:(hd + 1) * 64, q0:q0 + QW], mixo[0:64, 0:QW], [mixok], [], mixok + "o")
        if 'B' in PH:
            P.emit(nc)

    with ExitStack() as st:
        sb = lambda name, shape, dt: st.enter_context(nc.sbuf_tensor("C_" + name, shape, dt))
        psb = lambda name, shape, dt: st.enter_context(nc.psum_tensor("C_" + name, shape, dt))
        P = Prog(); E = Em(P)
        NFC = DFF // 128
        wo = sb("wo", [128, 8, D], BF16)
        wg = sb("wg", [128, 8, DFF], BF16)
        wu = sb("wu", [128, 8, DFF], BF16)
        wd = sb("wd", [128, NFC, D], BF16)
        identf = sb("identf", [128, 128], F32); identb = sb("identb", [128, 128], BF16)
        gffn = sb("gffn", [128, D], F32); gfin = sb("gfin", [128, D], F32)
        E.dma("sp", identf[:], c_ident[:, :], [], ["identf"], "w_id")
        E.cp("dve", identb[:], identf[:], ["identf"], ["identb"])
        E.dma("sp", gffn[:], g_ffn.partition_broadcast(128), [], ["gffn"], "w_gffn")
        E.dma("sp", gfin[:], g_fin.partition_broadcast(128), [], ["gfin"], "w_gfin")
        TWMAX = 256
        mix_r = Ring(sb, "mixt", [128, 8, TWMAX], BF16, 1)
        xl_r = Ring(sb, "xl", [128, D], F32, 2)
        x2_r = Ring(sb, "x2", [128, 2, D], F32, 2)
        _ce = [0]
        stg_slots = [(xl_r.t[i][:, :], [xl_r.k[i]], xl_r.k[i]) for i in range(2)]
        for i in range(2):
            for s_ in range(2):
                stg_slots.append((x2_r.t[i][:, s_, :], [x2_r.k[i] + "a%d" % s_, x2_r.k[i] + "b%d" % s_], "stgx2_%d_%d" % (i, s_)))

        def wloadc(dst_ap, src_ap, wkey, ncols):
            stg, sks, stag = stg_slots[_ce[0] % len(stg_slots)]
            E.dma("sp", stg[:, 0:ncols], src_ap, [], sks, stag)
            eng = "act" if _ce[0] % 2 == 0 else "dve"
            _ce[0] += 1
            E.cp(eng, dst_ap, stg[:, 0:ncols], sks, [wkey])
        for c in range(8):
            wloadc(wo[:, c, :], w_out[c * 128:(c + 1) * 128, :], "wo", 1024)
        for c in range(8):
            for (c0, cw) in ((0, 1024), (1024, 1024), (2048, 768)):
                wloadc(wg[:, c, c0:c0 + cw], w_gate[c * 128:(c + 1) * 128, c0:c0 + cw], "wg", cw)
                wloadc(wu[:, c, c0:c0 + cw], w_up[c * 128:(c + 1) * 128, c0:c0 + cw], "wu", cw)
        for c in range(NFC):
            wloadc(wd[:, c, :], w_down[c * 128:(c + 1) * 128, :], "wd", 1024)
        h2_r = Ring(sb, "h2", [128, D], BF16, 2)
        h2T_r = Ring(sb, "h2T", [128, 8, TWMAX], BF16, 1)
        actT_r = Ring(sb, "actT", [128, NFC, TWMAX], BF16, 1)
        sg_r = Ring(sb, "sg", [128, TWMAX], F32, 2)
        st_r = Ring(sb, "statc", [128, 8], F32, 4)
        for _i in range(4):
            E.memset("dve", st_r.t[_i][:], 0.0, [st_r.k[_i] + "z"])
        pA_r = Ring(psb, "pA", [128, 512], F32, 4)
        pF_r = Ring(psb, "pF", [128, 512], F32, 3)
        pT_r = Ring(psb, "pTc", [128, 1024], BF16, 1)

        def rstd_c(ss_ap, n, stt, stk):
            E.cp("act", stt[:, 6:7], stt[:, 7:8], [stk + "a", stk + "z"], [stk + "a2"])
            E.act(stt[:, 1:2], ss_ap, AF.Ln, [stk + "a", stk + "a2"], [stk + "b"], bias=EPS, scale=1.0 / n)
            E.act(stt[:, 2:3], stt[:, 1:2], AF.Exp, [stk + "b"], [stk + "c"], scale=-0.5)
            return stt[:, 2:3], stk + "c"

        for job in jobs:
            lq = job.lq
            TW = 256 if lq % 256 == 0 else 128
            nsub = TW // 128
            def c_loads(job_, t_):
                TW_ = 256 if job_.lq % 256 == 0 else 128
                q0_ = t_ * TW_
                mixt_, mixk_ = mix_r.next()
                E.dma("sp", mixt_[:, :, 0:TW_], job_.MIX[:, q0_:q0_ + TW_].rearrange("(c p) t -> p c t", p=128), [], [mixk_], mixk_)
                xls = []
                for s_ in range(TW_ // 128):
                    r0_ = q0_ + s_ * 128
                    xl_, xlk_ = xl_r.next()
                    if job_.nvq == 128:
                        E.dma("sp", xl_[:], job_.xq[r0_:r0_ + 128, :], [], [xlk_], xlk_)
                    else:
                        E.dma("sp", xl_[0:job_.nvq, :], job_.xq[0:job_.nvq, :], [], [xlk_], xlk_)
                    xls.append((xl_, xlk_))
                return (mixt_, mixk_, xls)

            if job is jobs[0]:
                c_flat = [(j_, t_) for j_ in jobs for t_ in range(j_.lq // (256 if j_.lq % 256 == 0 else 128))]
                c_i = [0]
                c_pre = [c_loads(*c_flat[0])]
            for t in range(lq // TW):
                q0 = t * TW
                mixt, mixk, xls = c_pre[0]
                c_i[0] += 1
                x2, x2k = x2_r.next(); h2T, h2Tk = h2T_r.next()
                for s in range(nsub):
                    r0 = q0 + s * 128
                    xl, xlk = xls[s]
                    pa0, pa0k = pA_r.next(); pa1, pa1k = pA_r.next()
                    for c in range(8):
                        E.mm(pa0[:, :], mixt[:, c, s * 128:(s + 1) * 128], wo[:, c, 0:512], c == 0, c == 7, [mixk, "wo"], [pa0k])
                        E.mm(pa1[:, :], mixt[:, c, s * 128:(s + 1) * 128], wo[:, c, 512:1024], c == 0, c == 7, [mixk, "wo"], [pa1k])
                    E.tt("dve", x2[:, s, 0:512], pa0[:, :], xl[:, 0:512], ALU.add, [pa0k, xlk], [x2k + "a%d" % s])
                    E.tt("dve", x2[:, s, 512:1024], pa1[:, :], xl[:, 512:1024], ALU.add, [pa1k, xlk], [x2k + "b%d" % s])
                    xk2 = [x2k + "a%d" % s, x2k + "b%d" % s]
                    junk, jk = h2_r.next(); stt, stk = st_r.next()
                    E.act(junk[:], x2[:, s, :], AF.Square, xk2 + [stk + "z"], [jk, stk + "a"], accum=stt[:, 0:1])
                    rs, rsk = rstd_c(stt[:, 0:1], D, stt, stk)
                    h2, h2k = h2_r.next()
                    E.stt(h2[:], x2[:, s, :], rs, gffn[:], ALU.mult, ALU.mult, xk2 + [rsk, "gffn"], [h2k])
                    pT, pTk = pT_r.next()
                    for c in range(8):
                        E.tr(pT[:, c * 128:(c + 1) * 128], h2[:, c * 128:(c + 1) * 128], identb[:], [h2k, "identb"], [pTk])
                    E.cp("act", h2T[:, :, s * 128:(s + 1) * 128], pT[:, :].rearrange("p (c t) -> p c t", c=8), [pTk], [h2Tk + str(s)])
                if c_i[0] < len(c_flat):
                    c_pre[0] = c_loads(*c_flat[c_i[0]])
                h2keys = [h2Tk + str(s) for s in range(nsub)]
                actT, actTk = actT_r.next()
                for f in range(NFC):
                    pgt, pgtk = pF_r.next(); put, putk = pF_r.next()
                    for c in range(8):
                        E.mm(pgt[:, 0:TW], wg[:, c, f * 128:(f + 1) * 128], h2T[:, c, 0:TW], c == 0, c == 7, ["wg"] + h2keys, [pgtk])
                    for c in range(8):
                        E.mm(put[:, 0:TW], wu[:, c, f * 128:(f + 1) * 128], h2T[:, c, 0:TW], c == 0, c == 7, ["wu"] + h2keys, [putk])
                    sg, sgk = sg_r.next()
                    E.act(sg[:, 0:TW], pgt[:, 0:TW], AF.Silu, [pgtk], [sgk])
                    E.tt("dve", actT[:, f, 0:TW], sg[:, 0:TW], put[:, 0:TW], ALU.mult, [sgk, putk], [actTk + "_%d" % f])
                akeys = [actTk + "_%d" % f for f in range(NFC)]
                for s in range(nsub):
                    r0 = q0 + s * 128
                    pa0, pa0k = pA_r.next(); pa1, pa1k = pA_r.next()
                    for f in range(NFC):
                        E.mm(pa0[:, :], actT[:, f, s * 128:(s + 1) * 128], wd[:, f, 0:512], f == 0, f == NFC - 1, akeys + ["wd"], [pa0k])
                        E.mm(pa1[:, :], actT[:, f, s * 128:(s + 1) * 128], wd[:, f, 512:1024], f == 0, f == NFC - 1, akeys + ["wd"], [pa1k])
                    xk2 = [x2k + "a%d" % s, x2k + "b%d" % s]
                    E.tt("dve", x2[:, s, 0:512], pa0[:, :], x2[:, s, 0:512], ALU.add, [pa0k] + xk2, [x2k + "a%d" % s])
                    E.tt("dve", x2[:, s, 512:1024], pa1[:, :], x2[:, s, 512:1024], ALU.add, [pa1k] + xk2, [x2k + "b%d" % s])
                    junk, jk = h2_r.next(); stt, stk = st_r.next()
                    E.act(junk[:], x2[:, s, :], AF.Square, xk2 + [stk + "z"], [jk, stk + "a"], accum=stt[:, 0:1])
                    rs, rsk = rstd_c(stt[:, 0:1], D, stt, stk)
                    E.stt(x2[:, s, :], x2[:, s, :], rs, gfin[:], ALU.mult, ALU.mult, xk2 + [rsk, "gfin"], xk2)
                    nv = job.nvq
                    E.dma("sp", job.y[r0:r0 + nv, :], x2[0:nv, s, :], xk2, [], x2k + "o%d" % s)
        if 'C' in PH:
            P.emit(nc)
    return nc


def make_consts(S, PAST):
    NBF = S // 128
    ident = np.eye(128, dtype=np.float32)
    jj = np.arange(128)[:, None]; tt = np.arange(128)[None, :]
    U = (jj <= tt).astype(np.float32)
    tri = np.stack([U, U * (jj < 16), U * (jj < 32)]).astype(np.float32)
    on = np.ones((128, 128), np.float32)
    ones = np.stack([on, on * (jj < 16), on * (jj < 32)]).astype(np.float32)
    m_fox = np.where(jj <= tt, 0.0, NEG)
    m_mla = np.where((jj // 64) <= (tt // 64), 0.0, NEG)
    mask = np.stack([m_fox, m_mla]).astype(np.float32)
    half = 16
    inv = (10000.0 ** (-np.arange(half, dtype=np.float32) / half)).astype(np.float32)

    def tab(pos):
        ang = pos.astype(np.float32)[..., None] * inv
        c = np.cos(ang).astype(np.float32); s = np.sin(ang).astype(np.float32)
        return np.concatenate([c, c, -s, s], axis=-1).astype(np.float32)
    p = np.arange(128)[:, None]
    b = np.arange(NBF + 1)[None, :]
    posp = np.where(b == 0, p, NMETA + (b - 1) * 128 + p)
    ropep = tab(posp)
    ropes = tab((PAST + np.arange(128))[:, None])
    return dict(c_ident=ident, c_tri=tri, c_ones=ones, c_mask=mask, c_ropep=ropep, c_ropes=ropes)


def make_in_maps(inp, ncores, S, PAST, NS):
    consts = make_consts(S, PAST)
    f = lambda a: np.ascontiguousarray(np.asarray(a, dtype=np.float32))
    shared = dict(
        meta=f(inp["meta_tokens"]), w_in=f(inp["w_in"][0]), w_uq=f(inp["w_mla_uq"][0]), w_ukv=f(inp["w_mla_ukv"][0]),
        w_out=f(inp["w_out"][0]), w_gate=f(inp["w_ffn_gate"][0]), w_up=f(inp["w_ffn_up"][0]), w_down=f(inp["w_ffn_down"][0]),
        g_mix=f(inp["norm_mix"][0]).reshape(1, -1), b_forget=f(inp["b_forget"][0]).reshape(1, -1),
        g_q=f(inp["mla_q_norm"][0]).reshape(1, -1), g_kv=f(inp["mla_kv_norm"][0]).reshape(1, -1),
        g_ffn=f(inp["norm_ffn"][0]).reshape(1, -1), g_fin=f(inp["norm_final"]).reshape(1, -1), **consts)
    maps = []
    for c in range(ncores):
        m = dict(shared)
        m["xp"] = f(inp["x_prompt"][c])
        sl = slice(c * NS, (c + 1) * NS)
        m["xs"] = f(inp["x_sample"][sl])
        m["c_lat"] = f(inp["cache_mla_latent"][0, sl]); m["c_kr"] = f(inp["cache_mla_krope"][0, sl])
        m["c_fk"] = f(inp["cache_fox_k"][0, sl]).reshape(NS, PAST, 512); m["c_fv"] = f(inp["cache_fox_v"][0, sl]).reshape(NS, PAST, 512)
        m["c_lf"] = f(inp["cache_fox_logf"][0, sl])
        maps.append(m)
    return maps


def assemble(results, ncores, S, NS):
    LP = NMETA + S
    cat = lambda k: np.concatenate([np.asarray(r[k], dtype=np.float32)[None] for r in results], axis=0)
    y_p = cat("y_p")
    y_s = cat("y_s").reshape(ncores * NS, 32, D)
    lat_p = cat("lat_p")[None]; kr_p = cat("kr_p")[None]
    fk_p = cat("fk_p").reshape(1, ncores, LP, 8, 64); fv_p = cat("fv_p").reshape(1, ncores, LP, 8, 64)
    lf_p = cat("lf_p")[None]
    lat_s = cat("lat_s").reshape(1, ncores * NS, 32, 256); kr_s = cat("kr_s").reshape(1, ncores * NS, 32, 32)
    fk_s = cat("fk_s").reshape(1, ncores * NS, 32, 8, 64); fv_s = cat("fv_s").reshape(1, ncores * NS, 32, 8, 64)
    lf_s = cat("lf_s").reshape(1, ncores * NS, 32, 8)
    return (y_p, y_s, lat_p, kr_p, fk_p, fv_p, lf_p, lat_s, kr_s, fk_s, fv_s, lf_s)


def kernel(**inp):
    ncores = 8
    B, S, _ = inp["x_prompt"].shape
    PAST = inp["cache_mla_latent"].shape[2]
    NS = inp["x_sample"].shape[0] // ncores
    assert B == ncores
    nc = build(S, PAST, NS)
    maps = make_in_maps(inp, ncores, S, PAST, NS)
    res = run_bass_kernel_spmd(nc, maps, core_ids=list(range(ncores)))
    return assemble(res.results, ncores, S, NS)
```
